# Optimizing a Trainium2 kernel written in Bass

```python
import math
import jax, jax.numpy as jnp
from jax import lax
import numpy as np

D_MODEL = 1024
BATCH = 4
SEQ = 4096
DEPTH = 4

MIX_WIDTH = D_MODEL
POOL_WIDTH = D_MODEL // 2
POOL_WINDOWS = (2, 4, 8, 16)
POOL_GROUPS = len(POOL_WINDOWS)
POOL_GC = POOL_WIDTH // POOL_GROUPS
HEAD_DIM = 64
NSA_HEADS = (MIX_WIDTH - POOL_WIDTH) // HEAD_DIM
KV_GROUPS = 2
HEADS_PER_GROUP = NSA_HEADS // KV_GROUPS
N_BRANCH = 3
CMP_LEN = 32
CMP_STRIDE = 16
SEL_LEN = 64
SEL_TOPN = 16
WINDOW = 512
Q_BLOCK = 128
D_FF = 4 * D_MODEL
ROPE_THETA = 10000.0
EPS = 1e-6
NEG = -1e30
FORCED_SCORE = 1e4

Q_COLS = NSA_HEADS * HEAD_DIM
KV_COLS = KV_GROUPS * HEAD_DIM
GATE_COLS = NSA_HEADS * N_BRANCH
IN_COLS = POOL_WIDTH + Q_COLS + 2 * N_BRANCH * KV_COLS + GATE_COLS

kernel_name = "hybrid_pool_nsa_adaln_trunk"


def rms_norm(x, g):
    xf = x.astype(jnp.float32)
    y = xf * lax.rsqrt(jnp.mean(xf * xf, axis=-1, keepdims=True) + EPS)
    return (y * g.astype(jnp.float32)).astype(x.dtype)


def rope(x, pos):
    dk = x.shape[-1]
    half = dk // 2
    inv = ROPE_THETA ** (-jnp.arange(half, dtype=jnp.float32) * 2.0 / dk)
    ang = pos.astype(jnp.float32)[:, None] * inv[None, :]
    shape = (1, pos.shape[0]) + (1,) * (x.ndim - 3) + (half,)
    cos = jnp.cos(ang).reshape(shape)
    sin = jnp.sin(ang).reshape(shape)
    xf = x.astype(jnp.float32)
    x1, x2 = xf[..., :half], xf[..., half:]
    return jnp.concatenate([x1 * cos - x2 * sin, x2 * cos + x1 * sin], axis=-1).astype(x.dtype)


def pool_mixer(u, pool_w, pool_scale):
    B, S, _ = u.shape
    ug = u.reshape(B, S, POOL_GROUPS, POOL_GC).astype(jnp.float32)
    cs = jnp.concatenate([jnp.zeros((B, 1, POOL_GROUPS, POOL_GC), jnp.float32),
                          jnp.cumsum(ug, axis=1)], axis=1)
    t = jnp.arange(S)
    win = jnp.array(POOL_WINDOWS, dtype=jnp.int32)
    lower = jnp.maximum(t[:, None] + 1 - win[None, :], 0)
    count = jnp.minimum(t[:, None] + 1, win[None, :]).astype(jnp.float32)
    gidx = jnp.arange(POOL_GROUPS)[None, :]
    sums = cs[:, 1:] - cs[:, lower, gidx]
    pooled = (sums / count[None, :, :, None] - ug).astype(u.dtype)
    y = jnp.einsum('bsgc,gcd->bsgd', pooled, pool_w)
    return y.reshape(B, S, POOL_WIDTH) * pool_scale


def compress(t, pe, w1, w2, blk):
    B = t.shape[0]
    n_cmp = blk.shape[0]
    tb = t[:, blk] + pe[None, None, :, None, :]
    flat = tb.transpose(0, 1, 3, 2, 4).reshape(B, n_cmp, KV_GROUPS, CMP_LEN * HEAD_DIM)
    return jax.nn.silu(flat @ w1) @ w2


def nsa_mixer(q, k_all, v_all, gates, q_norm, k_norm, cmp_pe, cmp_w1, cmp_w2):
    B, S = q.shape[0], q.shape[1]
    G, Hg, dk = KV_GROUPS, HEADS_PER_GROUP, HEAD_DIM
    scale = 1.0 / math.sqrt(dk)
    pos = jnp.arange(S)

    q = rope(rms_norm(q, q_norm), pos).reshape(B, S, G, Hg, dk)
    k_s = rope(rms_norm(k_all[:, :, 1], k_norm[1]), pos)
    k_w = rope(rms_norm(k_all[:, :, 2], k_norm[2]), pos)
    v_s = v_all[:, :, 1]
    v_w = v_all[:, :, 2]

    n_cmp = (S - CMP_LEN) // CMP_STRIDE + 1
    blk = np.arange(n_cmp)[:, None] * CMP_STRIDE + np.arange(CMP_LEN)[None, :]
    cmp_end = jnp.arange(n_cmp) * CMP_STRIDE + (CMP_LEN - 1)
    k_cmp = compress(k_all[:, :, 0], cmp_pe[0], cmp_w1[0], cmp_w2[0], blk)
    v_cmp = compress(v_all[:, :, 0], cmp_pe[1], cmp_w1[1], cmp_w2[1], blk)
    k_cmp = rope(rms_norm(k_cmp, k_norm[0]), cmp_end)
    s_c = jnp.einsum('bsghd,bngd->bsghn', q, k_cmp).astype(jnp.float32) * scale
    mask_c = (cmp_end[None, :] <= pos[:, None])[None, :, None, None, :]
    p_c = jax.nn.softmax(jnp.where(mask_c, s_c, NEG), axis=-1)
    p_c = jnp.where(mask_c, p_c, 0.0)
    o_cmp = jnp.einsum('bsghn,bngd->bsghd', p_c.astype(v_cmp.dtype), v_cmp)

    n_sel = S // SEL_LEN
    cs0 = np.arange(n_cmp) * CMP_STRIDE
    ss0 = np.arange(n_sel) * SEL_LEN
    ov = np.minimum(cs0[:, None] + CMP_LEN, ss0[None, :] + SEL_LEN) - np.maximum(cs0[:, None], ss0[None, :])
    ov_mat = jnp.asarray(np.clip(ov, 0, None).astype(np.float32) / CMP_LEN)
    imp = jnp.einsum('bsghn,nj->bsgj', p_c, ov_mat)
    blk_id = jnp.arange(n_sel)[None, :]
    cur = (pos // SEL_LEN)[:, None]
    causal = blk_id <= cur
    forced = (blk_id == 0) | (blk_id == cur) | (blk_id == cur - 1)
    imp = jnp.where(causal[None, :, None, :],
                    jnp.where(forced[None, :, None, :], FORCED_SCORE, imp), -1.0)
    topn = min(SEL_TOPN, n_sel)
    vals, sel_idx = lax.top_k(imp, topn)
    sel_valid = vals >= 0.0

    k_sb = k_s.reshape(B, n_sel, SEL_LEN, G, dk).transpose(0, 3, 1, 2, 4)
    v_sb = v_s.reshape(B, n_sel, SEL_LEN, G, dk).transpose(0, 3, 1, 2, 4)
    pad = ((0, 0), (WINDOW, 0), (0, 0), (0, 0))
    k_wp = jnp.pad(k_w, pad)
    v_wp = jnp.pad(v_w, pad)
    bi = jnp.arange(B)[:, None, None, None]
    gi = jnp.arange(G)[None, None, :, None]
    C = Q_BLOCK

    def block_fn(qb):
        start = qb * C
        qc = lax.dynamic_slice_in_dim(q, start, C, axis=1)
        tc = start + jnp.arange(C)
        idx = lax.dynamic_slice_in_dim(sel_idx, start, C, axis=1)
        val = lax.dynamic_slice_in_dim(sel_valid, start, C, axis=1)
        kg = k_sb[bi, gi, idx]
        vg = v_sb[bi, gi, idx]
        s = jnp.einsum('bcghd,bcgnld->bcghnl', qc, kg).astype(jnp.float32) * scale
        kpos = idx[..., None] * SEL_LEN + jnp.arange(SEL_LEN)
        m = (kpos <= tc[None, :, None, None, None]) & val[..., None]
        s = jnp.where(m[:, :, :, None], s, NEG)
        sh = s.shape
        p = jax.nn.softmax(s.reshape(sh[:4] + (sh[4] * sh[5],)), axis=-1).reshape(sh)
        o_s = jnp.einsum('bcghnl,bcgnld->bcghd', p.astype(vg.dtype), vg)
        kw = lax.dynamic_slice_in_dim(k_wp, start, C + WINDOW, axis=1)
        vw = lax.dynamic_slice_in_dim(v_wp, start, C + WINDOW, axis=1)
        kpos_w = start - WINDOW + jnp.arange(C + WINDOW)
        diff = tc[:, None] - kpos_w[None, :]
        mw = ((diff >= 0) & (diff < WINDOW) & (kpos_w[None, :] >= 0))[None, :, None, None, :]
        sw = jnp.einsum('bcghd,bkgd->bcghk', qc, kw).astype(jnp.float32) * scale
        pw = jax.nn.softmax(jnp.where(mw, sw, NEG), axis=-1)
        o_w = jnp.einsum('bcghk,bkgd->bcghd', pw.astype(vw.dtype), vw)
        return o_s, o_w

    o_s, o_w = lax.map(block_fn, jnp.arange(S // C))
    o_s = o_s.transpose(1, 0, 2, 3, 4, 5).reshape(B, S, G, Hg, dk)
    o_w = o_w.transpose(1, 0, 2, 3, 4, 5).reshape(B, S, G, Hg, dk)

    g = jax.nn.sigmoid(gates.astype(jnp.float32)).reshape(B, S, G, Hg, N_BRANCH).astype(q.dtype)
    o = g[..., 0:1] * o_cmp + g[..., 1:2] * o_s + g[..., 2:3] * o_w
    return o.reshape(B, S, NSA_HEADS * dk)


def hybrid_layer(x, mod, norm1, norm2, w_in, pool_w, pool_scale, q_norm, k_norm,
                 cmp_pe, cmp_w1, cmp_w2, w_out, w_ff1, w_ff2):
    B, S, _ = x.shape
    shift1, scale1, gate1, shift2, scale2, gate2 = jnp.split(mod[:, None, :], 6, axis=-1)
    h = rms_norm(x, norm1) * (1.0 + scale1) + shift1
    proj = h @ w_in
    o0 = POOL_WIDTH
    o1 = o0 + Q_COLS
    o2 = o1 + 2 * N_BRANCH * KV_COLS
    u = proj[..., :o0]
    q = proj[..., o0:o1].reshape(B, S, NSA_HEADS, HEAD_DIM)
    kv = proj[..., o1:o2].reshape(B, S, 2, N_BRANCH, KV_GROUPS, HEAD_DIM)
    gates = proj[..., o2:]
    y_pool = pool_mixer(u, pool_w, pool_scale)
    y_nsa = nsa_mixer(q, kv[:, :, 0], kv[:, :, 1], gates, q_norm, k_norm, cmp_pe, cmp_w1, cmp_w2)
    mix = jnp.concatenate([y_pool, y_nsa], axis=-1) @ w_out
    x = x + gate1 * mix
    h2 = rms_norm(x, norm2) * (1.0 + scale2) + shift2
    ff = jnp.square(jax.nn.relu(h2 @ w_ff1)) @ w_ff2
    return x + gate2 * ff


def setup_inputs(seed: int = 0) -> dict:
    key = jax.random.key(seed)
    ks = jax.random.split(key, 20)
    f32 = jnp.float32
    L, D = DEPTH, D_MODEL
    nrm = lambda k, shape, s: jax.random.normal(k, shape, f32) * s
    return {
        "x": nrm(ks[0], (BATCH, SEQ, D), 1.0),
        "c": nrm(ks[1], (BATCH, D), 1.0),
        "w_mod": nrm(ks[2], (L, D, 6 * D), 0.5 * D ** -0.5),
        "b_mod": nrm(ks[3], (L, 6 * D), 0.01),
        "norm1": 1.0 + nrm(ks[4], (L, D), 0.05),
        "norm2": 1.0 + nrm(ks[5], (L, D), 0.05),
        "w_in": nrm(ks[6], (L, D, IN_COLS), D ** -0.5),
        "pool_w": nrm(ks[7], (L, POOL_GROUPS, POOL_GC, POOL_GC), POOL_GC ** -0.5),
        "pool_scale": 0.5 + nrm(ks[8], (L, POOL_WIDTH), 0.1),
        "q_norm": 1.0 + nrm(ks[9], (L, HEAD_DIM), 0.05),
        "k_norm": 1.0 + nrm(ks[10], (L, N_BRANCH, HEAD_DIM), 0.05),
        "cmp_pe": nrm(ks[11], (L, 2, CMP_LEN, HEAD_DIM), 0.5),
        "cmp_w1": nrm(ks[12], (L, 2, CMP_LEN * HEAD_DIM, HEAD_DIM), (CMP_LEN * HEAD_DIM) ** -0.5),
        "cmp_w2": nrm(ks[13], (L, 2, HEAD_DIM, HEAD_DIM), HEAD_DIM ** -0.5),
        "w_out": nrm(ks[14], (L, MIX_WIDTH, D), MIX_WIDTH ** -0.5),
        "w_ff1": nrm(ks[15], (L, D, D_FF), D ** -0.5),
        "w_ff2": nrm(ks[16], (L, D_FF, D), D_FF ** -0.5),
    }


def reference(x, c, w_mod, b_mod, norm1, norm2, w_in, pool_w, pool_scale, q_norm, k_norm,
              cmp_pe, cmp_w1, cmp_w2, w_out, w_ff1, w_ff2):
    c_act = jax.nn.silu(c)
    for l in range(DEPTH):
        mod = c_act @ w_mod[l] + b_mod[l]
        x = hybrid_layer(x, mod, norm1[l], norm2[l], w_in[l], pool_w[l], pool_scale[l],
                         q_norm[l], k_norm[l], cmp_pe[l], cmp_w1[l], cmp_w2[l],
                         w_out[l], w_ff1[l], w_ff2[l])
    return x
```

```python
import numpy as np
import ml_dtypes
import concourse.bass as bass
import concourse.mybir as mybir
from concourse.bass_utils import run_bass_kernel_spmd

F32 = mybir.dt.float32
BF16 = mybir.dt.bfloat16
ALU = mybir.AluOpType
ACTF = mybir.ActivationFunctionType
AX = mybir.AxisListType

NCORES = 4
B, S, D, L = 4, 4096, 1024, 4
NT = S // 128
DFF = 4096
EPS = 1e-6
GW = 652
INC = 512 + 2 * GW
POOL_WINDOWS = (2, 4, 8, 16)
ENGS = ("pe", "act", "dve", "pool", "sp")


class _Rec:
    def __getattr__(self, name):
        def f(*a, **k):
            self.call = (name, a, k)
            return self
        return f


class Prog:
    def __init__(self, nc, n_dma_sems=32):
        self.nc = nc
        self.q = {e: [] for e in ENGS}
        self.cnt = {e: 0 for e in ENGS}
        self.sems = {}
        self._ctx = []
        for e in ENGS:
            cm = nc.semaphore("s_" + e)
            self.sems[e] = cm.__enter__()
            self._ctx.append(cm)
        self.n_dma = n_dma_sems
        self.dma_uses = [0] * n_dma_sems
        self.dma_rr = 0
        for i in range(n_dma_sems):
            cm = nc.semaphore("s_dma%d" % i)
            self.sems["dma%d" % i] = cm.__enter__()
            self._ctx.append(cm)
        self.waited = {e: {} for e in ENGS}
        self.last_w = {}
        self.readers = {}
        self.ninst = 0

    def close(self):
        for cm in reversed(self._ctx):
            cm.__exit__(None, None, None)

    def _deps(self, eng, reads, writes):
        need = {}

        def add(tok):
            s, v = tok
            if eng == "pe" and s == "pe":
                return
            if need.get(s, 0) < v:
                need[s] = v
        for k in reads:
            if k in self.last_w:
                add(self.last_w[k])
        for k in writes:
            if k in self.last_w:
                add(self.last_w[k])
            for tok in self.readers.get(k, ()):
                add(tok)
        out = []
        w = self.waited[eng]
        for s, v in need.items():
            if w.get(s, 0) < v:
                w[s] = v
                out.append((s, v))
        return out

    def _commit(self, tok, reads, writes):
        for k in writes:
            self.last_w[k] = tok
            self.readers[k] = []
        for k in reads:
            if k in writes:
                continue
            self.readers.setdefault(k, []).append(tok)

    def op(self, eng, fn, reads=(), writes=()):
        waits = self._deps(eng, reads, writes)
        self.cnt[eng] += 1
        tok = (eng, self.cnt[eng])
        sems = self.sems
        rec = _Rec()
        fn(rec)
        name, a, k = rec.call

        def run(E, waits=waits, s=sems[eng], name=name, a=a, k=k):
            for (ws, wv) in waits:
                E.wait_ge(sems[ws], wv)
            getattr(E, name)(*a, **k).then_inc(s, 1)
        self.q[eng].append(run)
        self._commit(tok, reads, writes)
        self.ninst += 1 + len(waits)
        return tok

    def dma(self, eng, out, in_, reads=(), writes=(), **kw):
        i = self.dma_rr
        self.dma_rr = (self.dma_rr + 1) % self.n_dma
        sname = "dma%d" % i
        waits = self._deps(eng, reads, writes)
        prev = 16 * self.dma_uses[i]
        if prev and self.waited[eng].get(sname, 0) < prev:
            self.waited[eng][sname] = prev
            waits.append((sname, prev))
        self.dma_uses[i] += 1
        tok = (sname, 16 * self.dma_uses[i])
        sems = self.sems

        def run(E, waits=waits, s=sems[sname]):
            for (ws, wv) in waits:
                E.wait_ge(sems[ws], wv)
            E.dma_start(out=out, in_=in_, **kw).then_inc(s, 16)
        self.q[eng].append(run)
        self._commit(tok, reads, writes)
        self.ninst += 1 + len(waits)
        return tok

    def barrier(self):
        waits = []
        for e in ENGS:
            if self.cnt[e]:
                waits.append((e, self.cnt[e]))
        for i in range(self.n_dma):
            if self.dma_uses[i]:
                waits.append(("dma%d" % i, 16 * self.dma_uses[i]))
        sems = self.sems
        for e in ENGS:
            mine = [(s, v) for (s, v) in waits if self.waited[e].get(s, 0) < v and not (s == e and e == "pe")]
            for (s, v) in mine:
                self.waited[e][s] = v

            def run(E, mine=mine):
                for (ws, wv) in mine:
                    E.wait_ge(sems[ws], wv)
            self.q[e].append(run)
        self.last_w = {}
        self.readers = {}

    def emit(self):
        nc = self.nc
        q = self.q
        with nc.Block() as block:
            @block.tensor
            def _(E):
                for f in q["pe"]:
                    f(E)

            @block.scalar
            def _(E):
                for f in q["act"]:
                    f(E)

            @block.vector
            def _(E):
                for f in q["dve"]:
                    f(E)

            @block.gpsimd
            def _(E):
                for f in q["pool"]:
                    f(E)

            @block.sync
            def _(E):
                for f in q["sp"]:
                    f(E)


class Arena:
    def __init__(self, big, nbytes):
        self.big = big
        self.nbytes = nbytes
        self.off = 0
        self.marks = []

    def alloc(self, shape, dtype, parts=128):
        n = int(np.prod(shape))
        esz = 4 if dtype == F32 else 2
        nb = (n * esz + 31) // 32 * 32
        assert self.off + nb <= self.nbytes, ("SBUF arena overflow", self.off, nb, self.nbytes)
        w0 = self.off // 4
        v = self.big[0:parts, w0:w0 + nb // 4]
        if dtype != F32:
            v = v.bitcast(dtype)
        v = v[:, 0:n]
        self.off += nb
        if len(shape) == 2:
            return v.rearrange("p (a b) -> p a b", b=shape[1])
        if len(shape) == 3:
            return v.rearrange("p (a b c) -> p a b c", b=shape[1], c=shape[2])
        return v

    def mark(self):
        self.marks.append(self.off)

    def release(self):
        self.off = self.marks.pop()


def _consts():
    bf = ml_dtypes.bfloat16
    c = {}
    c["ident"] = np.eye(128, dtype=np.float32)
    inv = (10000.0 ** (-np.arange(32, dtype=np.float32) * 2.0 / 64.0)).astype(np.float32)
    pos = (np.arange(NT)[None, :] * 128 + np.arange(128)[:, None]).astype(np.float32)
    ang = pos[:, :, None] * inv[None, None, :]
    c["cos"] = np.cos(ang).astype(np.float32)
    c["sin"] = np.sin(ang).astype(np.float32)
    m = (np.arange(NT)[None, :] * 8 + np.arange(8)[:, None])
    cpos = (16 * m + 15).astype(np.float32)
    cang = cpos[:, :, None] * inv[None, None, :]
    c["ccos"] = np.cos(cang).astype(np.float32)
    c["csin"] = np.sin(cang).astype(np.float32)
    k = np.arange(128)[:, None]
    q = np.arange(128)[None, :]
    c["tri"] = (k <= q).astype(np.float32)
    c["triinv"] = (k > q).astype(np.float32)
    cm = np.zeros((128, 16, 128), np.float32)
    for tt in range(16):
        i = np.arange(128)[:, None] - 8 * tt
        r = np.arange(128)[None, :]
        vis = (i < 0) | ((i >= 0) & (i <= 7) & (r >= 16 * i + 15))
        cm[:, tt, :] = vis
    c["cmask"] = cm
    ex = np.zeros((64, S), np.float32)
    ex[np.arange(S) // 64, np.arange(S)] = 1.0
    c["expand"] = ex
    r = np.arange(128)[:, None]
    jj = np.arange(128)[None, :]
    cur = 64 + (r >= 64)
    keep = (jj < cur - 1).astype(np.float32)
    add = np.where((jj == cur) | (jj == cur - 1), 1e4, np.where(jj > cur, -1.0, 0.0)).astype(np.float32)
    c["keepB"] = keep
    c["addB"] = add
    n_cmp = (S - 32) // 16 + 1
    cs0 = np.arange(n_cmp) * 16
    ss0 = np.arange(64) * 64
    ov = np.minimum(cs0[:, None] + 32, ss0[None, :] + 64) - np.maximum(cs0[:, None], ss0[None, :])
    ov = np.clip(ov, 0, None).astype(np.float32) / 32.0
    ovs = np.zeros((256, 64), np.float32)
    ovs[1:1 + n_cmp] = ov
    c["ov"] = ovs.reshape(2, 128, 64).transpose(1, 0, 2).copy()
    bands = np.zeros((128, 3, 4, 128), np.float32)
    s = np.arange(128)[:, None]
    t = np.arange(128)[None, :]
    for gi, w in enumerate(POOL_WINDOWS):
        main = ((s <= t) & (s > t - w)).astype(np.float32) / w - (s == t)
        corner = (s >= 129 + t - w).astype(np.float32) / w
        cnt = np.minimum(t + 1, w).astype(np.float32)
        first = ((s <= t) & (s > t - w)).astype(np.float32) / cnt - (s == t)
        bands[:, 0, gi, :] = main
        bands[:, 1, gi, :] = corner
        bands[:, 2, gi, :] = first
    c["bands"] = bands
    c["ones"] = np.ones((128, 128), np.float32)
    return c


_CONST_SPECS = [
    ("ident", [128, 128], BF16, 128), ("cos", [128, NT, 32], F32, 128), ("sin", [128, NT, 32], F32, 128),
    ("ccos", [8, NT, 32], F32, 8), ("csin", [8, NT, 32], F32, 8),
    ("tri", [128, 128], BF16, 128), ("triinv", [128, 128], BF16, 128),
    ("cmask", [128, 16, 128], BF16, 128), ("expand", [64, S], BF16, 64),
    ("keepB", [128, 128], F32, 128), ("addB", [128, 128], F32, 128),
    ("ov", [128, 2, 64], BF16, 128), ("bands", [128, 3, 4, 128], F32, 128), ("ones", [128, 128], F32, 128),
]


def build_program(n_layers, first_layer_norm=True, debug=False, nt=NT, stop=None, mstage=99):
    nc = bass.Bass("TRN2", target_bir_lowering=False)
    LW = n_layers

    def din(name, shape, dt=F32):
        return nc.dram_tensor(name, shape, dt, kind="ExternalInput").ap()

    x_in = din("x", [S, D])
    c_col = din("c_col", [128, 8])
    w_mod = din("w_mod", [LW, D, 6 * D])
    b_mod = din("b_mod", [LW, 1, 6 * D])
    n1c = din("n1c", [LW, 128, 8])
    n2c = din("n2c", [LW, 128, 8])
    w_in = din("w_in", [LW, D, INC])
    pool_w = din("pool_w", [LW, 128, 4, 128])
    pscale = din("pscale", [LW, 128, 4])
    gains = din("gains", [LW, 128, 7, 64])
    w1 = din("w1", [LW, 64, 2 * 32 * 64])
    w2 = din("w2", [LW, 64, 2 * 64])
    peT = din("peT", [LW, 64, 2 * 32])
    w_out = din("w_out", [LW, D, D])
    w_ff1 = din("w_ff1", [LW, D, DFF])
    w_ff2 = din("w_ff2", [LW, DFF, D])
    cd = {name: din("k_" + name, shape) for (name, shape, _, _) in _CONST_SPECS}
    y_out = nc.dram_tensor("y", [S, D], F32, kind="ExternalOutput").ap()
    okind = dict(kind="ExternalOutput") if debug else {}
    hT_d = nc.dram_tensor("hT_d", [128, 8, S], BF16, **okind).ap()
    mixT_d = nc.dram_tensor("mixT_d", [128, 8, S], BF16, **okind).ap()
    xd = nc.dram_tensor("xd", [S, D], F32).ap()

    P = Prog(nc)
    ARENA_BYTES = 196 * 1024
    big_cm = nc.sbuf_tensor("arena", [128, ARENA_BYTES // 4], F32)
    big = big_cm.__enter__()
    A = Arena(big, ARENA_BYTES)
    ps_cms = [nc.psum_tensor("ps%d" % i, [128, 512], F32) for i in range(8)]
    ps = [cm.__enter__() for cm in ps_cms]

    def psv(i, shape, dtype=F32, parts=128):
        v = ps[i][0:parts, :]
        if dtype != F32:
            v = v.bitcast(dtype)
        n = int(np.prod(shape))
        v = v[:, 0:n]
        if len(shape) == 2:
            return v.rearrange("p (a b) -> p a b", b=shape[1])
        if len(shape) == 3:
            return v.rearrange("p (a b c) -> p a b c", b=shape[1], c=shape[2])
        return v

    K = {}
    for (name, shape, dt, parts) in _CONST_SPECS:
        if name in ("expand", "cmask", "cos", "sin", "ccos", "csin", "bands", "keepB", "addB", "ov", "tri", "triinv"):
            continue
        K[name] = A.alloc(shape[1:], dt, parts)
    ident = K["ident"]
    ones = K["ones"]
    cact = A.alloc([8], F32)
    modcols = A.alloc([4, 8], F32)
    s1c = A.alloc([8], F32)
    s2c = A.alloc([8], F32)
    n1t = A.alloc([8], F32)
    n2t = A.alloc([8], F32)
    gate1 = A.alloc([D], F32)
    gate2 = A.alloc([D], F32)
    small = A.alloc([64], F32)
    junk = A.alloc([D], BF16)

    def load_const(name, eng="pool"):
        for (nm, shape, dt, parts) in _CONST_SPECS:
            if nm == name:
                src = cd[name]
                dst = K[name]
                P.dma(eng if dt != F32 else "sp", dst, src, writes=["K_" + name])

    for nm in K:
        load_const(nm)
    P.dma("sp", cact, c_col, writes=["cact"])
    P.op("act", lambda E: E.activation(out=small[:, 0:8], in_=cact, func=ACTF.Exp, scale=-1.0), reads=["cact"], writes=["small"])
    P.op("dve", lambda E: E.tensor_scalar_add(out=small[:, 0:8], in0=small[:, 0:8], scalar1=1.0), reads=["small"], writes=["small"])
    P.op("dve", lambda E: E.reciprocal(out=small[:, 0:8], in_=small[:, 0:8]), reads=["small"], writes=["small"])
    P.op("dve", lambda E: E.tensor_tensor(out=cact, in0=cact, in1=small[:, 0:8], op=ALU.mult), reads=["small", "cact"], writes=["cact"])

    def rstd_from_ss(ss_ap, n, key, scale):
        P.op("act", lambda E: E.activation(out=ss_ap, in_=ss_ap, func=ACTF.Ln, scale=scale, bias=EPS), reads=[key], writes=[key])
        P.op("act", lambda E: E.activation(out=ss_ap, in_=ss_ap, func=ACTF.Exp, scale=-0.5), reads=[key], writes=[key])

    def norm_to_hT(xt, xkey, hT, hkey, scol, bcol, colkeys, xh, tag):
        ss = small[:, 32:33]
        P.op("act", lambda E: E.activation(out=junk, in_=xt, func=ACTF.Square, accum_out=ss), reads=[xkey], writes=["junk", "ss"])
        rstd_from_ss(ss, 1, "ss", 1.0 / D)
        P.op("dve", lambda E: E.tensor_scalar(out=xh, in0=xt, scalar1=ss, scalar2=None, op0=ALU.mult), reads=[xkey, "ss"], writes=["xh" + tag])
        pT = psv(2, [8, 128], BF16)
        for kc in range(8):
            P.op("pe", lambda E, kc=kc: E.transpose(out=pT[:, kc, :], in_=xh[:, kc * 128:(kc + 1) * 128], identity=ident),
                 reads=["xh" + tag, "K_ident"], writes=["ps2"])
        for kc in range(8):
            P.op("act", lambda E, kc=kc: E.activation(out=hT[:, kc, :], in_=pT[:, kc, :], func=ACTF.Identity,
                                                     scale=scol[:, kc:kc + 1], bias=bcol[:, kc:kc + 1]),
                 reads=["ps2"] + colkeys, writes=[hkey])

    for l in range(n_layers):
        x_src = x_in if l == 0 else xd
        x_dst = y_out if l == n_layers - 1 else xd

        P.barrier()
        A.mark()
        modrow = A.alloc([6 * D], F32, parts=1)
        bmrow = A.alloc([6 * D], F32, parts=1)
        wm = [A.alloc([8, 512], F32) for _ in range(2)]
        P.dma("sp", bmrow, b_mod[l], writes=["bmrow"])
        P.dma("sp", n1t, n1c[l], writes=["n1t"])
        P.dma("sp", n2t, n2c[l], writes=["n2t"])
        for ch in range(12):
            buf = wm[ch % 2]
            P.dma("sp", buf, w_mod[l][:, ch * 512:(ch + 1) * 512].rearrange("(k p) n -> p k n", p=128), writes=["wm%d" % (ch % 2)])
            pr = ps[ch % 2][0:1, :]
            for kc in range(8):
                P.op("pe", lambda E, kc=kc, buf=buf, pr=pr: E.matmul(pr, lhsT=cact[:, kc:kc + 1], rhs=buf[:, kc, :], start=(kc == 0), stop=(kc == 7)),
                     reads=["cact", "wm%d" % (ch % 2)], writes=["ps%d" % (ch % 2)])
            P.op("dve", lambda E, ch=ch, pr=pr: E.tensor_tensor(out=modrow[:, ch * 512:(ch + 1) * 512], in0=pr, in1=bmrow[:, ch * 512:(ch + 1) * 512], op=ALU.add),
                 reads=["ps%d" % (ch % 2), "bmrow"], writes=["modrow"])
        pc = ps[2][:, 0:32]
        for vi, off in enumerate((0, 1024, 3072, 4096)):
            for kc in range(8):
                j = vi * 8 + kc
                P.op("pe", lambda E, j=j, off=off, kc=kc: E.matmul(pc[:, j:j + 1], lhsT=modrow[0:1, off + kc * 128: off + (kc + 1) * 128],
                                                                  rhs=ones[0:1, 0:1], start=True, stop=True),
                     reads=["modrow", "K_ones"], writes=["ps2"])
        P.op("dve", lambda E: E.tensor_copy(out=modcols.rearrange("p a b -> p (a b)"), in_=pc), reads=["ps2"], writes=["modcols"])
        P.op("dve", lambda E: E.scalar_tensor_tensor(out=s1c, in0=modcols[:, 1, :], scalar=1.0, in1=n1t, op0=ALU.add, op1=ALU.mult),
             reads=["modcols", "n1t"], writes=["s1c"])
        P.op("dve", lambda E: E.scalar_tensor_tensor(out=s2c, in0=modcols[:, 3, :], scalar=1.0, in1=n2t, op0=ALU.add, op1=ALU.mult),
             reads=["modcols", "n2t"], writes=["s2c"])
        for gi, (gt, off) in enumerate(((gate1, 2048), (gate2, 5120))):
            for h in range(2):
                pb = ps[3 + h]
                P.op("pe", lambda E, off=off, h=h, pb=pb: E.matmul(pb[:, :], lhsT=ones[0:1, :], rhs=modrow[0:1, off + h * 512: off + (h + 1) * 512], start=True, stop=True),
                     reads=["modrow", "K_ones"], writes=["ps%d" % (3 + h)])
                P.op("dve", lambda E, gt=gt, h=h, pb=pb: E.tensor_copy(out=gt[:, h * 512:(h + 1) * 512], in_=pb[:, :]),
                     reads=["ps%d" % (3 + h)], writes=["gate%d" % gi])
        b1c = modcols[:, 0, :]
        b2c = modcols[:, 2, :]
        P.barrier()
        A.release()
        if stop == "mod":
            break

        if True:
            A.mark()
            xts = [A.alloc([D], F32) for _ in range(2)]
            xhs = [A.alloc([D], BF16) for _ in range(2)]
            hTs = [A.alloc([8, 128], BF16) for _ in range(2)]
            for t in range(nt):
                b = t % 2
                P.dma("sp", xts[b], x_src[t * 128:(t + 1) * 128, :], reads=["x_d%d" % t], writes=["xt%d" % b])
                norm_to_hT(xts[b], "xt%d" % b, hTs[b], "hT%d" % b, s1c, b1c, ["s1c", "modcols"], xhs[b], "n%d" % b)
                P.dma("sp", hT_d[:, :, t * 128:(t + 1) * 128], hTs[b], reads=["hT%d" % b], writes=["hT_d%d" % t])
            P.barrier()
            A.release()
        if stop == "norm":
            break

        A.mark()
        mixer_phase(nc, P, A, ps, psv, l, nt, K, cd, dict(
            w_in=w_in, pool_w=pool_w, pscale=pscale, gains=gains, w1=w1, w2=w2, peT=peT,
            hT_d=hT_d, mixT_d=mixT_d, ident=ident, ones=ones, small=small, mstage=mstage))
        P.barrier()
        A.release()
        if stop == "mixer":
            break

        A.mark()
        wo = A.alloc([8, D], BF16)
        f1 = A.alloc([8, DFF], BF16)
        f2 = A.alloc([32, D], BF16)
        for kc in range(8):
            P.dma("pool", wo[:, kc, :], w_out[l][kc * 128:(kc + 1) * 128, :], writes=["wo"])
        for kc in range(8):
            for hh in range(2):
                P.dma("pool", f1[:, kc, hh * 2048:(hh + 1) * 2048], w_ff1[l][kc * 128:(kc + 1) * 128, hh * 2048:(hh + 1) * 2048], writes=["f1"])
        for c4 in range(8):
            P.dma("pool", f2[:, c4 * 4:(c4 + 1) * 4, :], w_ff2[l][c4 * 512:(c4 + 1) * 512, :].rearrange("(c p) n -> p c n", p=128), writes=["f2"])
        mts = [A.alloc([8, 128], BF16) for _ in range(2)]
        xts = [A.alloc([D], F32) for _ in range(2)]
        xh = A.alloc([D], BF16)
        h2T = A.alloc([8, 128], BF16)
        aT = A.alloc([32, 128], BF16)
        rl = [A.alloc([512], F32) for _ in range(2)]
        tmp = A.alloc([512], F32)
        for t in range(nt):
            b = t % 2
            xt = xts[b]
            xk = "xt%d" % b
            P.dma("sp", mts[b], mixT_d[:, :, t * 128:(t + 1) * 128], reads=["mixT_d%d" % t], writes=["mt%d" % b])
            P.dma("sp", xt, x_src[t * 128:(t + 1) * 128, :], reads=["x_d%d" % t], writes=[xk])
            for h in range(2):
                for kc in range(8):
                    P.op("pe", lambda E, h=h, kc=kc, b=b: E.matmul(ps[h][:, :], lhsT=mts[b][:, kc, :], rhs=wo[:, kc, h * 512:(h + 1) * 512], start=(kc == 0), stop=(kc == 7)),
                         reads=["mt%d" % b, "wo"], writes=["ps%d" % h])
                P.op("dve", lambda E, h=h: E.tensor_tensor(out=tmp, in0=ps[h][:, :], in1=gate1[:, h * 512:(h + 1) * 512], op=ALU.mult),
                     reads=["ps%d" % h, "gate0"], writes=["tmp"])
                P.op("dve", lambda E, h=h, xt=xt: E.tensor_tensor(out=xt[:, h * 512:(h + 1) * 512], in0=xt[:, h * 512:(h + 1) * 512], in1=tmp, op=ALU.add),
                     reads=["tmp", xk], writes=[xk])
            norm_to_hT(xt, xk, h2T, "h2T", s2c, b2c, ["s2c", "modcols"], xh, "f")
            for c4 in range(8):
                pf = ps[3 + (c4 % 2)]
                for cc in range(4):
                    c = c4 * 4 + cc
                    for kc in range(8):
                        P.op("pe", lambda E, c=c, cc=cc, kc=kc, pf=pf: E.matmul(pf[:, cc * 128:(cc + 1) * 128], lhsT=f1[:, kc, c * 128:(c + 1) * 128], rhs=h2T[:, kc, :],
                                                                               start=(kc == 0 and cc == 0), stop=(kc == 7 and cc == 3), skip_group_check=True),
                             reads=["h2T", "f1"], writes=["ps%d" % (3 + c4 % 2)])
                r = rl[c4 % 2]
                P.op("act", lambda E, pf=pf, r=r: E.activation(out=r, in_=pf[:, :], func=ACTF.Relu), reads=["ps%d" % (3 + c4 % 2)], writes=["rl%d" % (c4 % 2)])
                P.op("pool", lambda E, r=r, c4=c4: E.tensor_tensor(out=aT[:, c4 * 4:(c4 + 1) * 4, :].rearrange("p a b -> p (a b)"), in0=r, in1=r, op=ALU.mult),
                     reads=["rl%d" % (c4 % 2)], writes=["aT%d" % c4])
            for h in range(2):
                for c in range(32):
                    P.op("pe", lambda E, h=h, c=c: E.matmul(ps[h][:, :], lhsT=aT[:, c, :], rhs=f2[:, c, h * 512:(h + 1) * 512], start=(c == 0), stop=(c == 31)),
                         reads=["aT%d" % (c // 4), "f2"], writes=["ps%d" % h])
                P.op("dve", lambda E, h=h: E.tensor_tensor(out=tmp, in0=ps[h][:, :], in1=gate2[:, h * 512:(h + 1) * 512], op=ALU.mult),
                     reads=["ps%d" % h, "gate1"], writes=["tmp"])
                P.op("dve", lambda E, h=h, xt=xt: E.tensor_tensor(out=xt[:, h * 512:(h + 1) * 512], in0=xt[:, h * 512:(h + 1) * 512], in1=tmp, op=ALU.add),
                     reads=["tmp", xk], writes=[xk])
            P.dma("sp", x_dst[t * 128:(t + 1) * 128, :], xt, reads=[xk], writes=["x_d%d" % t])
            if l < n_layers - 1:
                pass
        P.barrier()
        A.release()
        if l < n_layers - 1:
            pass

    P.barrier()
    P.emit()
    P.close()
    for cm in reversed(ps_cms):
        cm.__exit__(None, None, None)
    big_cm.__exit__(None, None, None)
    return nc


def mixer_phase(nc, P, A, ps, psv, l, nt, K, cd, W):
    ident = W["ident"]
    hT_d, mixT_d = W["hT_d"], W["mixT_d"]
    MS = W.get("mstage", 99)

    def cp(eng, out, in_, reads, writes):
        if eng == "act":
            P.op("act", lambda E: E.activation(out=out, in_=in_, func=ACTF.Identity), reads, writes)
        else:
            P.op(eng, lambda E: E.tensor_copy(out=out, in_=in_), reads, writes)

    def tt(eng, out, a, b, op, reads, writes):
        P.op(eng, lambda E: E.tensor_tensor(out=out, in0=a, in1=b, op=op), reads, writes)

    tri = A.alloc([128], BF16)
    triinv = A.alloc([128], BF16)
    cmask = A.alloc([16, 128], BF16)
    expand = A.alloc([S], BF16, parts=64)
    keepB = A.alloc([128], F32)
    addB = A.alloc([128], F32)
    ov = A.alloc([2, 64], BF16)
    bands = A.alloc([3, 4, 128], F32)
    P.dma("pool", tri, cd["tri"], writes=["tri"])
    P.dma("pool", triinv, cd["triinv"], writes=["triinv"])
    P.dma("pool", cmask, cd["cmask"], writes=["cmask"])
    for hh in range(2):
        P.dma("pool", expand[:, hh * 2048:(hh + 1) * 2048], cd["expand"][:, hh * 2048:(hh + 1) * 2048], writes=["expand"])
    P.dma("sp", keepB, cd["keepB"], writes=["keepB"])
    P.dma("sp", addB, cd["addB"], writes=["addB"])
    P.dma("pool", ov, cd["ov"], writes=["ov"])
    P.dma("sp", bands, cd["bands"], writes=["bands"])
    win = A.alloc([8, INC], BF16)
    for kc in range(8):
        P.dma("pool", win[:, kc, :], W["w_in"][l][kc * 128:(kc + 1) * 128, :], writes=["win"])
    pw = A.alloc([4, 128], BF16)
    P.dma("pool", pw, W["pool_w"][l], writes=["pw"])
    psc = A.alloc([4], F32)
    P.dma("sp", psc, W["pscale"][l], writes=["psc"])
    gn = A.alloc([7, 64], F32)
    P.dma("sp", gn, W["gains"][l], writes=["gn"])
    W1 = A.alloc([2, 32, 64], BF16, parts=64)
    for kv in range(2):
        P.dma("pool", W1[:, kv, :, :].rearrange("p a b -> p (a b)"), W["w1"][l][:, kv * 2048:(kv + 1) * 2048], writes=["W1"])
    W2 = A.alloc([2, 64], BF16, parts=64)
    P.dma("pool", W2.rearrange("p a b -> p (a b)"), W["w2"][l], writes=["W2"])
    peT = A.alloc([2, 32], BF16, parts=64)
    P.dma("pool", peT.rearrange("p a b -> p (a b)"), W["peT"][l], writes=["peT"])
    cbias = A.alloc([2], F32, parts=64)
    pb = ps[0][0:64, 0:2]
    for kv in range(2):
        for p in range(32):
            P.op("pe", lambda E, kv=kv, p=p: E.matmul(pb[:, kv:kv + 1], lhsT=W1[:, kv, p, :], rhs=peT[:, kv, p:p + 1], start=(p == 0), stop=(p == 31)),
                 reads=["W1", "peT"], writes=["ps0"])
    cp("dve", cbias, pb, ["ps0"], ["cbias"])
    if MS < 2:
        return
    kT = [A.alloc([2, S], BF16, parts=64) for _ in range(2)]
    craw = [A.alloc([2, 144], BF16, parts=64) for _ in range(2)]
    V = [A.alloc([nt * 2, 66], BF16).rearrange("p (t k) c -> p t k c", k=2) for _ in range(2)]
    kTc = [A.alloc([256], BF16, parts=64) for _ in range(2)]
    Vc = [A.alloc([2, 66], BF16) for _ in range(2)]
    for g in range(2):
        P.op("pool", lambda E, g=g: E.memset(V[g], 1.0), writes=["V%d" % g])
        P.op("pool", lambda E, g=g: E.memset(Vc[g], 0.0), writes=["Vc%d" % g])
        P.op("pool", lambda E, g=g: E.memset(Vc[g][:, :, 64:65], 1.0), writes=["Vc%d" % g])
        P.op("pool", lambda E, g=g: E.memset(Vc[g][0:1, 0, 64:65], 0.0), writes=["Vc%d" % g])
        P.op("pool", lambda E, g=g: E.memset(kTc[g], 0.0), writes=["kTc%d" % g])
        P.op("pool", lambda E, g=g: E.memset(craw[g], 0.0), writes=["craw%d" % g])
    hTt = [A.alloc([8, 128], BF16) for _ in range(2)]
    cs = [A.alloc([2, 32], F32) for _ in range(2)]
    ccs = [A.alloc([2, 32], F32, parts=8) for _ in range(2)]
    u = [A.alloc([512], F32) for _ in range(2)]
    pj = A.alloc([2 * GW], F32)
    sq = A.alloc([6, 64], F32)
    st = A.alloc([8], F32)
    xn = A.alloc([6, 64], F32)
    ra = A.alloc([6, 32], F32)
    rb = A.alloc([6, 32], F32)
    rc = A.alloc([6, 32], F32)
    rd = A.alloc([6, 32], F32)
    tb = [A.alloc([9, 64], BF16) for _ in range(2)]
    qT = [A.alloc([512], BF16, parts=64) for _ in range(2)]
    sg = [A.alloc([12], F32) for _ in range(2)]
    Eb = [A.alloc([512], BF16) for _ in range(3)]
    msk = A.alloc([128], BF16)
    oall = A.alloc([3, 264], F32)
    rden = A.alloc([12], F32)
    coef = A.alloc([12], F32)
    imp = A.alloc([64], F32)
    imp2 = A.alloc([64], F32)
    wk = A.alloc([64], F32)
    m8 = A.alloc([16], F32)
    thr = A.alloc([1], F32)
    selb = A.alloc([128], BF16)
    selT = A.alloc([128], BF16, parts=64)
    y32 = A.alloc([256], F32)
    ybf = A.alloc([256], BF16)
    mixt = A.alloc([8, 128], BF16)
    pooledT = A.alloc([512], BF16)
    zc = A.alloc([16], F32)
    ec = A.alloc([16], F32)
    P.op("pool", lambda E: E.memset(zc, 0.0), writes=["zc"])
    sTc = A.alloc([16], BF16, parts=64)
    k8 = A.alloc([64], F32, parts=8)
    k8q = A.alloc([64], F32, parts=8)
    k8s = A.alloc([4], F32)
    P.op("pool", lambda E: E.memset(k8s, 1.0), writes=["k8s"])
    k8r = [A.alloc([32], F32, parts=8) for _ in range(4)]
    k8b = A.alloc([128], BF16)
    v8 = A.alloc([64], BF16, parts=8)

    pT128 = psv(2, [8, 128], BF16)
    pT = pT128[0:64]
    P.op("pool", lambda E: E.memset(selb, 0.0), writes=["selb"])
    P.op("pool", lambda E: E.memset(k8b, 0.0), writes=["k8b"])
    for g_ in range(2):
        P.op("pool", lambda E, g_=g_: E.memset(tb[g_], 0.0), writes=["tb%d" % g_])
    pT2 = psv(2, [2, 128], BF16)
    psO = ps[6]
    psI = ps[7]

    def bc_h(ap128):
        return ap128.unsqueeze(1).to_broadcast([128, 4, 128])

    def e4(buf):
        return buf.rearrange("p (h q) -> p h q", h=4)

    def attention_branch(g, t, br, kts, kTsrc, vsrc, vkey, masks, use_sel):
        nk = len(kts)

        def scores(i):
            kt = kts[i]
            bank = 3 + i % 2
            P.op("pe", lambda E: E.matmul(ps[bank][:, :], lhsT=kTsrc(kt), rhs=qT[g], start=True, stop=True),
                 reads=["kcache%d" % g, "kTc%d" % g, "qT%d" % g], writes=["ps%d" % bank])
            if use_sel:
                slot = ps[5][:, (i % 2) * 128:(i % 2 + 1) * 128]
                P.op("pe", lambda E: E.matmul(slot, lhsT=expand[:, kt * 128:(kt + 1) * 128], rhs=selT, start=True, stop=True),
                     reads=["expand", "selT"], writes=["ps5"])
        scores(0)
        for i, kt in enumerate(kts):
            if i + 1 < nk:
                scores(i + 1)
            bank = 3 + i % 2
            E_ = Eb[i % 3]
            ek = "Eb%d" % (i % 3)
            P.op("act", lambda E, bank=bank, E_=E_: E.activation(out=E_, in_=ps[bank][:, :], func=ACTF.Exp, scale=0.125),
                 reads=["ps%d" % bank], writes=[ek])
            m = masks(kt)
            if use_sel:
                slot = ps[5][:, (i % 2) * 128:(i % 2 + 1) * 128]
                sk = "ps5"
                if m is not None:
                    tt("dve", msk, slot, m[0], ALU.mult, [sk, m[1]], ["msk"])
                    tt("pool", e4(E_), e4(E_), bc_h(msk), ALU.mult, [ek, "msk"], [ek])
                else:
                    tt("dve", e4(E_), e4(E_), bc_h(slot), ALU.mult, [ek, sk], [ek])
            elif m is not None:
                tt("pool", e4(E_), e4(E_), bc_h(m[0]), ALU.mult, [ek, m[1]], [ek])
            for h in range(4):
                first = (i == 0 and h == 0)
                last = (i == nk - 1 and h == 3)
                P.op("pe", lambda E, h=h, kt=kt, E_=E_, first=first, last=last: E.matmul(
                    psO[:, h * 66:(h + 1) * 66], lhsT=E_[:, h * 128:(h + 1) * 128], rhs=vsrc(kt), start=first, stop=last, skip_group_check=True),
                    reads=[ek, vkey], writes=["ps6"])
                if br == 0:
                    P.op("pe", lambda E, h=h, kt=kt, E_=E_, first=first, last=last: E.matmul(
                        psI[:, h * 64:(h + 1) * 64], lhsT=E_[:, h * 128:(h + 1) * 128], rhs=ov[:, kt, :], start=first, stop=last, skip_group_check=True),
                        reads=[ek, "ov"], writes=["ps7"])
        cp("act", oall[:, br, :], psO[:, 0:264], ["ps6"], ["oall%d" % br])
        P.op("dve", lambda E: E.tensor_scalar_max(out=rden[:, br * 4:(br + 1) * 4], in0=oall[:, br, :].rearrange("p (h c) -> p h c", c=66)[:, :, 64], scalar1=1e-30),
             reads=["oall%d" % br], writes=["rden%d" % br])
        P.op("dve", lambda E: E.reciprocal(out=rden[:, br * 4:(br + 1) * 4], in_=rden[:, br * 4:(br + 1) * 4]),
             reads=["rden%d" % br], writes=["rden%d" % br])

    for t in range(nt):
        b = t % 2
        ts = slice(t * 128, (t + 1) * 128)
        P.dma("sp", hTt[b], hT_d[:, :, ts], reads=["hT_d%d" % t], writes=["hTt%d" % b])
        P.dma("sp", cs[b][:, 0, :], cd["cos"][:, t, :], writes=["cs%d" % b])
        P.dma("sp", cs[b][:, 1, :], cd["sin"][:, t, :], writes=["cs%d" % b])
        P.dma("sp", ccs[b][:, 0, :], cd["ccos"][:, t, :], writes=["ccs%d" % b])
        P.dma("sp", ccs[b][:, 1, :], cd["csin"][:, t, :], writes=["ccs%d" % b])
        chunks = [(0, 512, u[b], "u%d" % b), (512, 448, pj[:, 0:448], "pj0"), (960, 204, pj[:, 448:652], "pj0"),
                  (1164, 448, pj[:, 652:1100], "pj1"), (1612, 204, pj[:, 1100:1304], "pj1")]
        for ci, (c0, wd, dst, dk) in enumerate(chunks):
            bank = ci % 2
            for kc in range(8):
                P.op("pe", lambda E, kc=kc, c0=c0, wd=wd, bank=bank: E.matmul(ps[bank][:, 0:wd], lhsT=hTt[b][:, kc, :], rhs=win[:, kc, c0:c0 + wd],
                                                                               start=(kc == 0), stop=(kc == 7)),
                     reads=["hTt%d" % b, "win"], writes=["ps%d" % bank])
            cp("act" if ci % 2 == 0 else "dve", dst, ps[bank][:, 0:wd], ["ps%d" % bank], [dk])
        if MS < 4:
            continue
        for g in range(2):
            base = g * GW
            pk = "pj%d" % g
            qk = pj[:, base:base + 384].rearrange("p (s d) -> p s d", d=64)
            tt("dve", sq, qk, qk, ALU.mult, [pk], ["sq"])
            P.op("dve", lambda E: E.tensor_reduce(out=st[:, 0:6], in_=sq, axis=AX.X, op=ALU.add), reads=["sq"], writes=["st"])
            P.op("act", lambda E: E.activation(out=st[:, 0:6], in_=st[:, 0:6], func=ACTF.Ln, scale=1.0 / 64, bias=EPS), reads=["st"], writes=["st"])
            P.op("act", lambda E: E.activation(out=st[:, 0:6], in_=st[:, 0:6], func=ACTF.Exp, scale=-0.5), reads=["st"], writes=["st"])
            tt("dve", xn, qk, st[:, 0:6].unsqueeze(2).to_broadcast([128, 6, 64]), ALU.mult, [pk, "st"], ["xn"])
            tt("pool", xn, xn, gn[:, 0:6, :], ALU.mult, ["xn", "gn"], ["xn"])
            cosb = cs[b][:, 0, :].unsqueeze(1).to_broadcast([128, 6, 32])
            sinb = cs[b][:, 1, :].unsqueeze(1).to_broadcast([128, 6, 32])
            x1 = xn[:, :, 0:32]
            x2 = xn[:, :, 32:64]
            ck = "cs%d" % b
            tbg = tb[g]
            tk = "tb%d" % g
            tt("dve", ra, x1, cosb, ALU.mult, ["xn", ck], ["ra"])
            tt("pool", rb, x2, sinb, ALU.mult, ["xn", ck], ["rb"])
            tt("dve", tbg[:, 0:6, 0:32], ra, rb, ALU.subtract, ["ra", "rb"], [tk])
            tt("pool", rc, x2, cosb, ALU.mult, ["xn", ck], ["rc"])
            tt("dve", rd, x1, sinb, ALU.mult, ["xn", ck], ["rd"])
            tt("pool", tbg[:, 0:6, 32:64], rc, rd, ALU.add, ["rc", "rd"], [tk])
            cp("act", tbg[:, 6:8, :], pj[:, base + 384:base + 512].rearrange("p (s d) -> p s d", d=64), [pk], [tk])
            cp("dve", V[g][:, t, :, 0:64], pj[:, base + 512:base + 640].rearrange("p (s d) -> p s d", d=64), [pk], ["V%d" % g])
            sgg = sg[g]
            P.op("act", lambda E, sgg=sgg, base=base: E.activation(out=sgg, in_=pj[:, base + 640:base + 652], func=ACTF.Exp, scale=-1.0),
                 reads=[pk], writes=["sg%d" % g])
            P.op("dve", lambda E, sgg=sgg: E.tensor_scalar_add(out=sgg, in0=sgg, scalar1=1.0), reads=["sg%d" % g], writes=["sg%d" % g])
            P.op("dve", lambda E, sgg=sgg: E.reciprocal(out=sgg, in_=sgg), reads=["sg%d" % g], writes=["sg%d" % g])
        if MS < 5:
            continue
        psP = ps[0]
        for gp in range(4):
            kind = 2 if t == 0 else 0
            P.op("pe", lambda E, gp=gp, kind=kind: E.matmul(psP[:, gp * 128:(gp + 1) * 128], lhsT=u[b][:, gp * 128:(gp + 1) * 128], rhs=bands[:, kind, gp, :],
                                                            start=True, stop=(t == 0), skip_group_check=True),
                 reads=["u%d" % b, "bands"], writes=["ps0"])
            if t > 0:
                P.op("pe", lambda E, gp=gp: E.matmul(psP[:, gp * 128:(gp + 1) * 128], lhsT=u[1 - b][:, gp * 128:(gp + 1) * 128], rhs=bands[:, 1, gp, :],
                                                     start=False, stop=True, skip_group_check=True),
                     reads=["u%d" % (1 - b), "bands"], writes=["ps0"])
        if MS < 5.3:
            continue
        cp("act", pooledT, psP[:, :], ["ps0"], ["pooledT"])
        if MS < 5.6:
            continue
        psY = ps[1]
        for gp in range(4):
            P.op("pe", lambda E, gp=gp: E.matmul(psY[:, gp * 128:(gp + 1) * 128], lhsT=pw[:, gp, :], rhs=pooledT[:, gp * 128:(gp + 1) * 128], start=True, stop=True),
                 reads=["pw", "pooledT"], writes=["ps1"])
        if MS < 5.8:
            continue
        for gp in range(4):
            P.op("act", lambda E, gp=gp: E.activation(out=mixt[:, gp, :], in_=psY[:, gp * 128:(gp + 1) * 128], func=ACTF.Identity, scale=psc[:, gp:gp + 1]),
                 reads=["ps1", "psc"], writes=["mixt"])
        if MS < 6:
            continue
        for g in range(2):
            tbg = tb[g]
            tk = "tb%d" % g
            for s_ in range(8):
                P.op("pe", lambda E, s_=s_, tbg=tbg: E.transpose(out=pT128[:, s_, :], in_=tbg[:, s_:s_ + 2, :].rearrange("p a b -> p (a b)"), identity=ident), reads=[tk, "K_ident"], writes=["ps2"])
            if MS < 6.2:
                continue
            cp("dve", qT[g].rearrange("p (a b) -> p a b", a=4), pT[:, 0:4, :], ["ps2"], ["qT%d" % g])
            if MS < 6.4:
                continue
            cp("dve", kT[g][:, :, ts], pT[:, 4:6, :], ["ps2"], ["kcache%d" % g])
            if MS < 6.6:
                continue
            if t > 0:
                cp("dve", craw[g][:, :, 0:16], craw[g][:, :, 128:144], ["craw%d" % g], ["craw%d" % g])
            cp("dve", craw[g][:, :, 16:144], pT[:, 6:8, :], ["ps2"], ["craw%d" % g])
            if MS < 7:
                continue
            preT = ps[0][0:64, 0:16]
            for kv in range(2):
                for p in range(32):
                    P.op("pe", lambda E, kv=kv, p=p: E.matmul(preT[:, kv * 8:(kv + 1) * 8], lhsT=W1[:, kv, p, :], rhs=craw[g][:, kv, p:p + 113:16],
                                                              start=(p == 0), stop=(p == 31), skip_group_check=True),
                         reads=["W1", "craw%d" % g], writes=["ps0"])
            if MS < 7.2:
                continue
            for kv in range(2):
                P.op("dve", lambda E, kv=kv: E.tensor_scalar(out=zc[0:64, kv * 8:(kv + 1) * 8], in0=preT[:, kv * 8:(kv + 1) * 8], scalar1=cbias[:, kv:kv + 1],
                                                             scalar2=None, op0=ALU.add), reads=["ps0", "cbias"], writes=["zc"])
            P.op("act", lambda E: E.activation(out=ec, in_=zc, func=ACTF.Exp, scale=-1.0), reads=["zc"], writes=["ec"])
            P.op("dve", lambda E: E.tensor_scalar_add(out=ec[0:64], in0=ec[0:64], scalar1=1.0), reads=["ec"], writes=["ec"])
            P.op("dve", lambda E: E.reciprocal(out=ec[0:64], in_=ec[0:64]), reads=["ec"], writes=["ec"])
            tt("dve", sTc, zc[0:64], ec[0:64], ALU.mult, ["zc", "ec"], ["sTc"])
            if MS < 7.4:
                continue
            k8p = ps[1][0:8, 0:128]
            for kv in range(2):
                P.op("pe", lambda E, kv=kv: E.matmul(k8p[:, kv * 64:(kv + 1) * 64], lhsT=sTc[:, kv * 8:(kv + 1) * 8], rhs=W2[:, kv, :], start=True, stop=True),
                     reads=["sTc", "W2"], writes=["ps1"])
            if MS < 7.5:
                continue
            cp("dve", v8, k8p[:, 64:128], ["ps1"], ["v8"])
            if t == 0:
                P.op("pool", lambda E: E.memset(v8[0:1, :], 0.0), reads=[], writes=["v8"])
            r0 = 8 * (t % 16)
            P.dma("sp", Vc[g][r0:r0 + 8, t // 16, 0:64], v8, reads=["v8"], writes=["Vc%d" % g])
            if MS < 7.6:
                continue
            cp("dve", k8, k8p[:, 0:64], ["ps1"], ["k8"])
            tt("dve", k8q, k8, k8, ALU.mult, ["k8"], ["k8q"])
            P.op("dve", lambda E: E.tensor_reduce(out=k8s[0:8, 0:1], in_=k8q, axis=AX.X, op=ALU.add), reads=["k8q"], writes=["k8s"])
            P.op("act", lambda E: E.activation(out=k8s[:, 0:1], in_=k8s[:, 0:1], func=ACTF.Ln, scale=1.0 / 64, bias=EPS), reads=["k8s"], writes=["k8s"])
            P.op("act", lambda E: E.activation(out=k8s[:, 0:1], in_=k8s[:, 0:1], func=ACTF.Exp, scale=-0.5), reads=["k8s"], writes=["k8s"])
            P.op("dve", lambda E: E.tensor_scalar(out=k8, in0=k8, scalar1=k8s[0:8, 0:1], scalar2=None, op0=ALU.mult), reads=["k8", "k8s"], writes=["k8"])
            tt("dve", k8, k8, gn[0:8, 6, :], ALU.mult, ["k8", "gn"], ["k8"])
            cck = "ccs%d" % b
            cc_, ss_ = ccs[b][:, 0, :], ccs[b][:, 1, :]
            tt("dve", k8r[0], k8[:, 0:32], cc_, ALU.mult, ["k8", cck], ["k8r0"])
            tt("dve", k8r[1], k8[:, 32:64], ss_, ALU.mult, ["k8", cck], ["k8r1"])
            tt("dve", k8b[0:8, 0:32], k8r[0], k8r[1], ALU.subtract, ["k8r0", "k8r1"], ["k8b"])
            tt("dve", k8r[2], k8[:, 32:64], cc_, ALU.mult, ["k8", cck], ["k8r2"])
            tt("dve", k8r[3], k8[:, 0:32], ss_, ALU.mult, ["k8", cck], ["k8r3"])
            tt("dve", k8b[0:8, 32:64], k8r[2], k8r[3], ALU.add, ["k8r2", "k8r3"], ["k8b"])
            if MS < 7.8:
                continue
            pt8 = ps[7][0:64, 256:264]
            P.op("pe", lambda E: E.matmul(pt8, lhsT=k8b[0:8, 0:64], rhs=ident[0:8, 0:8], start=True, stop=True), reads=["k8b", "K_ident"], writes=["ps7"])
            cp("dve", kTc[g][:, 8 * t:8 * t + 8], pt8, ["ps7"], ["kTc%d" % g])
            if MS < 8:
                continue
            kts_c = [0] if t < 16 else [0, 1]
            attention_branch(g, t, 0, kts_c, lambda kt: kTc[g][:, kt * 128:(kt + 1) * 128], lambda kt: Vc[g][:, kt, :], "Vc%d" % g,
                             lambda kt: ((cmask[:, t % 16, :], "cmask") if kt == t // 16 else None), False)
            P.op("dve", lambda E: E.tensor_scalar(out=imp, in0=psI[:, 0:64], scalar1=rden[:, 0:1], scalar2=None, op0=ALU.mult), reads=["ps7", "rden0"], writes=["imp"])
            for h in range(1, 4):
                P.op("dve", lambda E, h=h: E.scalar_tensor_tensor(out=imp, in0=psI[:, h * 64:(h + 1) * 64], scalar=rden[:, h:h + 1], in1=imp, op0=ALU.mult, op1=ALU.add),
                     reads=["ps7", "rden0", "imp"], writes=["imp"])
            if MS < 9:
                continue
            j0 = 64 - 2 * t
            tt("dve", imp2, imp, keepB[:, j0:j0 + 64], ALU.mult, ["imp", "keepB"], ["imp2"])
            tt("dve", imp2, imp2, addB[:, j0:j0 + 64], ALU.add, ["imp2", "addB"], ["imp2"])
            P.op("dve", lambda E: E.memset(imp2[:, 0:1], 1e4), reads=[], writes=["imp2"])
            P.op("dve", lambda E: E.max(out=m8[:, 0:8], in_=imp2), reads=["imp2"], writes=["m8"])
            P.op("dve", lambda E: E.match_replace(out=wk, in_to_replace=m8[:, 0:8], in_values=imp2, imm_value=-2.0), reads=["imp2", "m8"], writes=["wk"])
            P.op("dve", lambda E: E.max(out=m8[:, 8:16], in_=wk), reads=["wk"], writes=["m8"])
            P.op("dve", lambda E: E.tensor_scalar_max(out=thr, in0=m8[:, 15:16], scalar1=0.0), reads=["m8"], writes=["thr"])
            P.op("dve", lambda E: E.tensor_scalar(out=selb[:, 0:64], in0=imp2, scalar1=thr[:, 0:1], scalar2=None, op0=ALU.is_ge), reads=["imp2", "thr"], writes=["selb"])
            psel = ps[7][0:64, 384:512]
            P.op("pe", lambda E: E.matmul(psel, lhsT=selb[:, 0:64], rhs=ident, start=True, stop=True), reads=["selb", "K_ident"], writes=["ps7"])
            cp("dve", selT, psel, ["ps7"], ["selT"])
            if MS < 10:
                continue
            attention_branch(g, t, 1, list(range(t + 1)), lambda kt: kT[g][:, 0, kt * 128:(kt + 1) * 128], lambda kt: V[g][:, kt, 0, :], "V%d" % g,
                             lambda kt: ((tri, "tri") if kt == t else None), True)
            if MS < 11:
                continue
            attention_branch(g, t, 2, list(range(max(0, t - 4), t + 1)), lambda kt: kT[g][:, 1, kt * 128:(kt + 1) * 128], lambda kt: V[g][:, kt, 1, :], "V%d" % g,
                             lambda kt: ((tri, "tri") if kt == t else ((triinv, "triinv") if kt == t - 4 else None)), False)
            if MS < 12:
                continue
            tt("dve", coef.rearrange("p (b h) -> p b h", h=4), sg[g].rearrange("p (h b) -> p b h", b=3), rden.rearrange("p (b h) -> p b h", h=4), ALU.mult,
               ["sg%d" % g, "rden0", "rden1", "rden2"], ["coef"])
            for h in range(4):
                ysl = y32[:, h * 64:(h + 1) * 64]
                P.op("dve", lambda E, h=h, ysl=ysl: E.tensor_scalar(out=ysl, in0=oall[:, 0, h * 66:h * 66 + 64], scalar1=coef[:, h:h + 1], scalar2=None, op0=ALU.mult),
                     reads=["oall0", "coef"], writes=["y32"])
                P.op("dve", lambda E, h=h, ysl=ysl: E.scalar_tensor_tensor(out=ysl, in0=oall[:, 1, h * 66:h * 66 + 64], scalar=coef[:, 4 + h:5 + h], in1=ysl, op0=ALU.mult, op1=ALU.add),
                     reads=["oall1", "coef", "y32"], writes=["y32"])
                P.op("dve", lambda E, h=h, ysl=ysl: E.scalar_tensor_tensor(out=ybf[:, h * 64:(h + 1) * 64], in0=oall[:, 2, h * 66:h * 66 + 64], scalar=coef[:, 8 + h:9 + h], in1=ysl, op0=ALU.mult, op1=ALU.add),
                     reads=["oall2", "coef", "y32"], writes=["ybf"])
            for c in range(2):
                P.op("pe", lambda E, c=c: E.transpose(out=pT2[:, c, :], in_=ybf[:, c * 128:(c + 1) * 128], identity=ident), reads=["ybf", "K_ident"], writes=["ps2"])
            cp("act", mixt[:, 4 + 2 * g:6 + 2 * g, :], pT2, ["ps2"], ["mixt"])
        P.dma("sp", mixT_d[:, :, ts], mixt, reads=["mixt"], writes=["mixT_d%d" % t])


def _perm_in_cols():
    o0, o1 = 512, 1024
    o2 = o1 + 768
    cols = list(range(512))
    for g in range(2):
        for h in range(4):
            cols += list(range(o0 + (g * 4 + h) * 64, o0 + (g * 4 + h + 1) * 64))

        def kvc(kvi, br):
            st = o1 + ((kvi * 3 + br) * 2 + g) * 64
            return list(range(st, st + 64))
        cols += kvc(0, 1) + kvc(0, 2) + kvc(0, 0) + kvc(1, 0) + kvc(1, 1) + kvc(1, 2)
        cols += list(range(o2 + g * 12, o2 + (g + 1) * 12))
    assert len(cols) == INC
    return np.array(cols)


_PROG_CACHE = {}


def _get_prog(n_layers, debug=False):
    key = (n_layers, debug)
    if key not in _PROG_CACHE:
        _PROG_CACHE[key] = build_program(n_layers, debug=debug)
    return _PROG_CACHE[key]


def _layer_inputs(ls, x_b, c, w_mod, b_mod, norm1, norm2, w_in, pool_w, pool_scale, q_norm, k_norm,
                  cmp_pe, cmp_w1, cmp_w2, w_out, w_ff1, w_ff2, b, consts, perm):
    f = np.float32
    ls = list(ls)
    n = len(ls)

    def col(v):
        return np.ascontiguousarray(v.reshape(8, 128).T)
    d = {}
    d["x"] = np.ascontiguousarray(x_b, dtype=f)
    d["c_col"] = col(np.asarray(c[b], f))
    d["w_mod"] = np.ascontiguousarray(w_mod[ls], dtype=f)
    d["b_mod"] = np.ascontiguousarray(b_mod[ls], dtype=f).reshape(n, 1, 6 * D)
    d["n1c"] = np.stack([col(norm1[l]) for l in ls]).astype(f)
    d["n2c"] = np.stack([col(norm2[l]) for l in ls]).astype(f)
    d["w_in"] = np.ascontiguousarray(w_in[ls][:, :, perm], dtype=f)
    d["pool_w"] = np.ascontiguousarray(np.transpose(pool_w[ls], (0, 2, 1, 3)), dtype=f)
    d["pscale"] = np.ascontiguousarray(np.transpose(pool_scale[ls].reshape(n, 4, 128), (0, 2, 1)), dtype=f)
    gains = np.zeros((n, 128, 7, 64), f)
    for i, l in enumerate(ls):
        gains[i, :, 0:4, :] = q_norm[l][None, None, :]
        gains[i, :, 4, :] = k_norm[l, 1][None, :]
        gains[i, :, 5, :] = k_norm[l, 2][None, :]
        gains[i, :, 6, :] = k_norm[l, 0][None, :]
    d["gains"] = gains
    w1 = cmp_w1[ls].reshape(n, 2, 32, 64, 64)
    d["w1"] = np.ascontiguousarray(np.transpose(w1, (0, 3, 1, 2, 4)).reshape(n, 64, 2 * 32 * 64), dtype=f)
    d["w2"] = np.ascontiguousarray(np.transpose(cmp_w2[ls], (0, 2, 1, 3)).reshape(n, 64, 128), dtype=f)
    d["peT"] = np.ascontiguousarray(np.transpose(cmp_pe[ls], (0, 3, 1, 2)).reshape(n, 64, 64), dtype=f)
    d["w_out"] = np.ascontiguousarray(w_out[ls], dtype=f)
    d["w_ff1"] = np.ascontiguousarray(w_ff1[ls], dtype=f)
    d["w_ff2"] = np.ascontiguousarray(w_ff2[ls], dtype=f)
    for name, shape, _, _ in _CONST_SPECS:
        d["k_" + name] = np.ascontiguousarray(consts[name], dtype=f).reshape(shape)
    return d


LAYERS_PER_LAUNCH = 1


def kernel(x, c, w_mod, b_mod, norm1, norm2, w_in, pool_w, pool_scale, q_norm, k_norm,
           cmp_pe, cmp_w1, cmp_w2, w_out, w_ff1, w_ff2):
    args = [np.asarray(a) for a in (c, w_mod, b_mod, norm1, norm2, w_in, pool_w, pool_scale, q_norm, k_norm,
                                    cmp_pe, cmp_w1, cmp_w2, w_out, w_ff1, w_ff2)]
    x = np.asarray(x, np.float32)
    consts = _consts()
    perm = _perm_in_cols()
    nc = _get_prog(LAYERS_PER_LAUNCH)
    xs = [x[b] for b in range(B)]
    for l0 in range(0, L, LAYERS_PER_LAUNCH):
        ls = range(l0, l0 + LAYERS_PER_LAUNCH)
        maps = [_layer_inputs(ls, xs[b], *args, b, consts, perm) for b in range(B)]
        in_maps = [maps[i % B] for i in range(NCORES)]
        res = run_bass_kernel_spmd(nc, in_maps, core_ids=list(range(NCORES)))
        xs = [np.asarray(res.results[b]["y"], np.float32) for b in range(B)]
    return np.stack(xs, 0).astype(np.float32)
```

```python
import numpy as np
import ml_dtypes
import concourse.bass as bass
import concourse.mybir as mybir
from concourse.bass_utils import run_bass_kernel_spmd

F32 = mybir.dt.float32
BF16 = mybir.dt.bfloat16
ALU = mybir.AluOpType
ACTF = mybir.ActivationFunctionType
AX = mybir.AxisListType

NCORES = 8
B, S, D, L = 4, 4096, 1024, 4
NT = S // 128
DFF = 4096
EPS = 1e-6
GW = 652
INC = 512 + 2 * GW
POOL_WINDOWS = (2, 4, 8, 16)
ENGS = ("pe", "act", "dve", "pool", "sp")


class _Rec:
    def __getattr__(self, name):
        def f(*a, **k):
            self.call = (name, a, k)
            return self
        return f


class Prog:
    def __init__(self, nc, n_dma_sems=32):
        self.nc = nc
        self.q = {e: [] for e in ENGS}
        self.cnt = {e: 0 for e in ENGS}
        self.sems = {}
        self._ctx = []
        for e in ENGS:
            cm = nc.semaphore("s_" + e)
            self.sems[e] = cm.__enter__()
            self._ctx.append(cm)
        self.n_dma = n_dma_sems
        self.dma_uses = [0] * n_dma_sems
        self.dma_rr = 0
        for i in range(n_dma_sems):
            cm = nc.semaphore("s_dma%d" % i)
            self.sems["dma%d" % i] = cm.__enter__()
            self._ctx.append(cm)
        self.waited = {e: {} for e in ENGS}
        self.last_w = {}
        self.readers = {}
        self.ninst = 0

    def close(self):
        for cm in reversed(self._ctx):
            cm.__exit__(None, None, None)

    def _deps(self, eng, reads, writes):
        need = {}

        def add(tok):
            s, v = tok
            if eng == "pe" and s == "pe":
                return
            if need.get(s, 0) < v:
                need[s] = v
        for k in reads:
            if k in self.last_w:
                add(self.last_w[k])
        for k in writes:
            if k in self.last_w:
                add(self.last_w[k])
            for tok in self.readers.get(k, ()):
                add(tok)
        out = []
        w = self.waited[eng]
        for s, v in need.items():
            if w.get(s, 0) < v:
                w[s] = v
                out.append((s, v))
        return out

    def _commit(self, tok, reads, writes):
        for k in writes:
            self.last_w[k] = tok
            self.readers[k] = []
        for k in reads:
            if k in writes:
                continue
            self.readers.setdefault(k, []).append(tok)

    def op(self, eng, fn, reads=(), writes=()):
        waits = self._deps(eng, reads, writes)
        self.cnt[eng] += 1
        tok = (eng, self.cnt[eng])
        sems = self.sems
        rec = _Rec()
        fn(rec)
        name, a, k = rec.call

        def run(E, waits=waits, s=sems[eng], name=name, a=a, k=k):
            for (ws, wv) in waits:
                E.wait_ge(sems[ws], wv)
            getattr(E, name)(*a, **k).then_inc(s, 1)
        self.q[eng].append(run)
        self._commit(tok, reads, writes)
        self.ninst += 1 + len(waits)
        return tok

    def dma(self, eng, out, in_, reads=(), writes=(), **kw):
        i = self.dma_rr
        self.dma_rr = (self.dma_rr + 1) % self.n_dma
        sname = "dma%d" % i
        waits = self._deps(eng, reads, writes)
        prev = 16 * self.dma_uses[i]
        if prev and self.waited[eng].get(sname, 0) < prev:
            self.waited[eng][sname] = prev
            waits.append((sname, prev))
        self.dma_uses[i] += 1
        tok = (sname, 16 * self.dma_uses[i])
        sems = self.sems

        def run(E, waits=waits, s=sems[sname]):
            for (ws, wv) in waits:
                E.wait_ge(sems[ws], wv)
            E.dma_start(out=out, in_=in_, **kw).then_inc(s, 16)
        self.q[eng].append(run)
        self._commit(tok, reads, writes)
        self.ninst += 1 + len(waits)
        return tok

    def barrier(self):
        waits = []
        for e in ENGS:
            if self.cnt[e]:
                waits.append((e, self.cnt[e]))
        for i in range(self.n_dma):
            if self.dma_uses[i]:
                waits.append(("dma%d" % i, 16 * self.dma_uses[i]))
        sems = self.sems
        for e in ENGS:
            mine = [(s, v) for (s, v) in waits if self.waited[e].get(s, 0) < v and not (s == e and e == "pe")]
            for (s, v) in mine:
                self.waited[e][s] = v

            def run(E, mine=mine):
                for (ws, wv) in mine:
                    E.wait_ge(sems[ws], wv)
            self.q[e].append(run)
        self.last_w = {}
        self.readers = {}

    def emit(self):
        nc = self.nc
        q = self.q
        with nc.Block() as block:
            @block.tensor
            def _(E):
                for f in q["pe"]:
                    f(E)

            @block.scalar
            def _(E):
                for f in q["act"]:
                    f(E)

            @block.vector
            def _(E):
                for f in q["dve"]:
                    f(E)

            @block.gpsimd
            def _(E):
                for f in q["pool"]:
                    f(E)

            @block.sync
            def _(E):
                for f in q["sp"]:
                    f(E)


class Arena:
    def __init__(self, big, nbytes):
        self.big = big
        self.nbytes = nbytes
        self.off = 0
        self.marks = []

    def alloc(self, shape, dtype, parts=128):
        n = int(np.prod(shape))
        esz = 4 if dtype == F32 else 2
        nb = (n * esz + 31) // 32 * 32
        assert self.off + nb <= self.nbytes, ("SBUF arena overflow", self.off, nb, self.nbytes)
        w0 = self.off // 4
        v = self.big[0:parts, w0:w0 + nb // 4]
        if dtype != F32:
            v = v.bitcast(dtype)
        v = v[:, 0:n]
        self.off += nb
        if len(shape) == 2:
            return v.rearrange("p (a b) -> p a b", b=shape[1])
        if len(shape) == 3:
            return v.rearrange("p (a b c) -> p a b c", b=shape[1], c=shape[2])
        return v

    def mark(self):
        self.marks.append(self.off)

    def release(self):
        self.off = self.marks.pop()


def _consts():
    bf = ml_dtypes.bfloat16
    c = {}
    c["ident"] = np.eye(128, dtype=np.float32)
    inv = (10000.0 ** (-np.arange(32, dtype=np.float32) * 2.0 / 64.0)).astype(np.float32)
    pos = (np.arange(NT)[None, :] * 128 + np.arange(128)[:, None]).astype(np.float32)
    ang = pos[:, :, None] * inv[None, None, :]
    c["cos"] = np.cos(ang).astype(np.float32)
    c["sin"] = np.sin(ang).astype(np.float32)
    m = (np.arange(NT)[None, :] * 8 + np.arange(8)[:, None])
    cpos = (16 * m + 15).astype(np.float32)
    cang = cpos[:, :, None] * inv[None, None, :]
    c["ccos"] = np.cos(cang).astype(np.float32)
    c["csin"] = np.sin(cang).astype(np.float32)
    k = np.arange(128)[:, None]
    q = np.arange(128)[None, :]
    c["tri"] = (k <= q).astype(np.float32)
    c["triinv"] = (k > q).astype(np.float32)
    cm = np.zeros((128, 16, 128), np.float32)
    for tt in range(16):
        i = np.arange(128)[:, None] - 8 * tt
        r = np.arange(128)[None, :]
        vis = (i < 0) | ((i >= 0) & (i <= 7) & (r >= 16 * i + 15))
        cm[:, tt, :] = vis
    c["cmask"] = cm
    ex = np.zeros((64, S), np.float32)
    ex[np.arange(S) // 64, np.arange(S)] = 1.0
    c["expand"] = ex
    r = np.arange(128)[:, None]
    jj = np.arange(128)[None, :]
    cur = 64 + (r >= 64)
    keep = (jj < cur - 1).astype(np.float32)
    add = np.where((jj == cur) | (jj == cur - 1), 1e4, np.where(jj > cur, -1.0, 0.0)).astype(np.float32)
    c["keepB"] = keep
    c["addB"] = add
    n_cmp = (S - 32) // 16 + 1
    cs0 = np.arange(n_cmp) * 16
    ss0 = np.arange(64) * 64
    ov = np.minimum(cs0[:, None] + 32, ss0[None, :] + 64) - np.maximum(cs0[:, None], ss0[None, :])
    ov = np.clip(ov, 0, None).astype(np.float32) / 32.0
    ovs = np.zeros((256, 64), np.float32)
    ovs[1:1 + n_cmp] = ov
    c["ov"] = ovs.reshape(2, 128, 64).transpose(1, 0, 2).copy()
    bands = np.zeros((128, 3, 4, 128), np.float32)
    s = np.arange(128)[:, None]
    t = np.arange(128)[None, :]
    for gi, w in enumerate(POOL_WINDOWS):
        main = ((s <= t) & (s > t - w)).astype(np.float32) / w - (s == t)
        corner = (s >= 129 + t - w).astype(np.float32) / w
        cnt = np.minimum(t + 1, w).astype(np.float32)
        first = ((s <= t) & (s > t - w)).astype(np.float32) / cnt - (s == t)
        bands[:, 0, gi, :] = main
        bands[:, 1, gi, :] = corner
        bands[:, 2, gi, :] = first
    c["bands"] = bands
    c["ones"] = np.ones((128, 128), np.float32)
    return c


_CONST_SPECS = [
    ("ident", [128, 128], BF16, 128), ("cos", [128, NT, 32], F32, 128), ("sin", [128, NT, 32], F32, 128),
    ("ccos", [8, NT, 32], F32, 8), ("csin", [8, NT, 32], F32, 8),
    ("tri", [128, 128], BF16, 128), ("triinv", [128, 128], BF16, 128),
    ("cmask", [128, 16, 128], BF16, 128), ("expand", [64, S], BF16, 64),
    ("keepB", [128, 128], F32, 128), ("addB", [128, 128], F32, 128),
    ("ov", [128, 2, 64], BF16, 128), ("bands", [128, 3, 4, 128], F32, 128), ("ones", [128, 128], F32, 128),
]


def build_program(n_layers, first_layer_norm=True, debug=False, nt=NT, stop=None, mstage=99):
    nc = bass.Bass("TRN2", target_bir_lowering=False)
    LW = n_layers

    def din(name, shape, dt=F32):
        return nc.dram_tensor(name, shape, dt, kind="ExternalInput").ap()

    x_in = din("x", [S, D])
    c_col = din("c_col", [128, 8])
    w_mod = din("w_mod", [LW, D, 6 * D])
    b_mod = din("b_mod", [LW, 1, 6 * D])
    n1c = din("n1c", [LW, 128, 8])
    n2c = din("n2c", [LW, 128, 8])
    w_in = din("w_in", [LW, D, INC])
    pool_w = din("pool_w", [LW, 128, 4, 128])
    pscale = din("pscale", [LW, 128, 4])
    gains = din("gains", [LW, 128, 7, 64])
    w1 = din("w1", [LW, 64, 2 * 32 * 64])
    w2 = din("w2", [LW, 64, 2 * 64])
    peT = din("peT", [LW, 64, 2 * 32])
    w_out = din("w_out", [LW, D, D])
    w_ff1 = din("w_ff1", [LW, D, DFF])
    w_ff2 = din("w_ff2", [LW, DFF, D])
    cd = {name: din("k_" + name, shape) for (name, shape, _, _) in _CONST_SPECS}
    y_out = nc.dram_tensor("y", [S, D], F32, kind="ExternalOutput").ap()
    okind = dict(kind="ExternalOutput") if debug else {}
    hT_d = nc.dram_tensor("hT_d", [128, 8, S], BF16, **okind).ap()
    mixT_d = nc.dram_tensor("mixT_d", [128, 8, S], BF16, **okind).ap()
    xd = nc.dram_tensor("xd", [S, D], F32).ap()

    P = Prog(nc)
    ARENA_BYTES = 196 * 1024
    big_cm = nc.sbuf_tensor("arena", [128, ARENA_BYTES // 4], F32)
    big = big_cm.__enter__()
    A = Arena(big, ARENA_BYTES)
    ps_cms = [nc.psum_tensor("ps%d" % i, [128, 512], F32) for i in range(8)]
    ps = [cm.__enter__() for cm in ps_cms]

    def psv(i, shape, dtype=F32, parts=128):
        v = ps[i][0:parts, :]
        if dtype != F32:
            v = v.bitcast(dtype)
        n = int(np.prod(shape))
        v = v[:, 0:n]
        if len(shape) == 2:
            return v.rearrange("p (a b) -> p a b", b=shape[1])
        if len(shape) == 3:
            return v.rearrange("p (a b c) -> p a b c", b=shape[1], c=shape[2])
        return v

    K = {}
    for (name, shape, dt, parts) in _CONST_SPECS:
        if name in ("expand", "cmask", "cos", "sin", "ccos", "csin", "bands", "keepB", "addB", "ov", "tri", "triinv"):
            continue
        K[name] = A.alloc(shape[1:], dt, parts)
    ident = K["ident"]
    ones = K["ones"]
    cact = A.alloc([8], F32)
    modcols = A.alloc([4, 8], F32)
    s1c = A.alloc([8], F32)
    s2c = A.alloc([8], F32)
    n1t = A.alloc([8], F32)
    n2t = A.alloc([8], F32)
    gate1 = A.alloc([D], F32)
    gate2 = A.alloc([D], F32)
    small = A.alloc([64], F32)
    junk = A.alloc([D], BF16)

    def load_const(name, eng="pool"):
        for (nm, shape, dt, parts) in _CONST_SPECS:
            if nm == name:
                src = cd[name]
                dst = K[name]
                P.dma(eng if dt != F32 else "sp", dst, src, writes=["K_" + name])

    for nm in K:
        load_const(nm)
    P.dma("sp", cact, c_col, writes=["cact"])
    P.op("act", lambda E: E.activation(out=small[:, 0:8], in_=cact, func=ACTF.Exp, scale=-1.0), reads=["cact"], writes=["small"])
    P.op("dve", lambda E: E.tensor_scalar_add(out=small[:, 0:8], in0=small[:, 0:8], scalar1=1.0), reads=["small"], writes=["small"])
    P.op("dve", lambda E: E.reciprocal(out=small[:, 0:8], in_=small[:, 0:8]), reads=["small"], writes=["small"])
    P.op("dve", lambda E: E.tensor_tensor(out=cact, in0=cact, in1=small[:, 0:8], op=ALU.mult), reads=["small", "cact"], writes=["cact"])

    def rstd_from_ss(ss_ap, n, key, scale):
        P.op("act", lambda E: E.activation(out=ss_ap, in_=ss_ap, func=ACTF.Ln, scale=scale, bias=EPS), reads=[key], writes=[key])
        P.op("act", lambda E: E.activation(out=ss_ap, in_=ss_ap, func=ACTF.Exp, scale=-0.5), reads=[key], writes=[key])

    def norm_to_hT(xt, xkey, hT, hkey, scol, bcol, colkeys, xh, tag):
        ss = small[:, 32:33]
        P.op("act", lambda E: E.activation(out=junk, in_=xt, func=ACTF.Square, accum_out=ss), reads=[xkey], writes=["junk", "ss"])
        rstd_from_ss(ss, 1, "ss", 1.0 / D)
        P.op("dve", lambda E: E.tensor_scalar(out=xh, in0=xt, scalar1=ss, scalar2=None, op0=ALU.mult), reads=[xkey, "ss"], writes=["xh" + tag])
        pT = psv(2, [8, 128], BF16)
        for kc in range(8):
            P.op("pe", lambda E, kc=kc: E.transpose(out=pT[:, kc, :], in_=xh[:, kc * 128:(kc + 1) * 128], identity=ident),
                 reads=["xh" + tag, "K_ident"], writes=["ps2"])
        for kc in range(8):
            P.op("act", lambda E, kc=kc: E.activation(out=hT[:, kc, :], in_=pT[:, kc, :], func=ACTF.Identity,
                                                     scale=scol[:, kc:kc + 1], bias=bcol[:, kc:kc + 1]),
                 reads=["ps2"] + colkeys, writes=[hkey])

    for l in range(n_layers):
        x_src = x_in if l == 0 else xd
        x_dst = y_out if l == n_layers - 1 else xd

        P.barrier()
        A.mark()
        modrow = A.alloc([6 * D], F32, parts=1)
        bmrow = A.alloc([6 * D], F32, parts=1)
        wm = [A.alloc([8, 512], F32) for _ in range(2)]
        P.dma("sp", bmrow, b_mod[l], writes=["bmrow"])
        P.dma("sp", n1t, n1c[l], writes=["n1t"])
        P.dma("sp", n2t, n2c[l], writes=["n2t"])
        for ch in range(12):
            buf = wm[ch % 2]
            P.dma("sp", buf, w_mod[l][:, ch * 512:(ch + 1) * 512].rearrange("(k p) n -> p k n", p=128), writes=["wm%d" % (ch % 2)])
            pr = ps[ch % 2][0:1, :]
            for kc in range(8):
                P.op("pe", lambda E, kc=kc, buf=buf, pr=pr: E.matmul(pr, lhsT=cact[:, kc:kc + 1], rhs=buf[:, kc, :], start=(kc == 0), stop=(kc == 7)),
                     reads=["cact", "wm%d" % (ch % 2)], writes=["ps%d" % (ch % 2)])
            P.op("dve", lambda E, ch=ch, pr=pr: E.tensor_tensor(out=modrow[:, ch * 512:(ch + 1) * 512], in0=pr, in1=bmrow[:, ch * 512:(ch + 1) * 512], op=ALU.add),
                 reads=["ps%d" % (ch % 2), "bmrow"], writes=["modrow"])
        pc = ps[2][:, 0:32]
        for vi, off in enumerate((0, 1024, 3072, 4096)):
            for kc in range(8):
                j = vi * 8 + kc
                P.op("pe", lambda E, j=j, off=off, kc=kc: E.matmul(pc[:, j:j + 1], lhsT=modrow[0:1, off + kc * 128: off + (kc + 1) * 128],
                                                                  rhs=ones[0:1, 0:1], start=True, stop=True),
                     reads=["modrow", "K_ones"], writes=["ps2"])
        P.op("dve", lambda E: E.tensor_copy(out=modcols.rearrange("p a b -> p (a b)"), in_=pc), reads=["ps2"], writes=["modcols"])
        P.op("dve", lambda E: E.scalar_tensor_tensor(out=s1c, in0=modcols[:, 1, :], scalar=1.0, in1=n1t, op0=ALU.add, op1=ALU.mult),
             reads=["modcols", "n1t"], writes=["s1c"])
        P.op("dve", lambda E: E.scalar_tensor_tensor(out=s2c, in0=modcols[:, 3, :], scalar=1.0, in1=n2t, op0=ALU.add, op1=ALU.mult),
             reads=["modcols", "n2t"], writes=["s2c"])
        for gi, (gt, off) in enumerate(((gate1, 2048), (gate2, 5120))):
            for h in range(2):
                pb = ps[3 + h]
                P.op("pe", lambda E, off=off, h=h, pb=pb: E.matmul(pb[:, :], lhsT=ones[0:1, :], rhs=modrow[0:1, off + h * 512: off + (h + 1) * 512], start=True, stop=True),
                     reads=["modrow", "K_ones"], writes=["ps%d" % (3 + h)])
                P.op("dve", lambda E, gt=gt, h=h, pb=pb: E.tensor_copy(out=gt[:, h * 512:(h + 1) * 512], in_=pb[:, :]),
                     reads=["ps%d" % (3 + h)], writes=["gate%d" % gi])
        b1c = modcols[:, 0, :]
        b2c = modcols[:, 2, :]
        P.barrier()
        A.release()
        if stop == "mod":
            break

        if True:
            A.mark()
            xts = [A.alloc([D], F32) for _ in range(2)]
            xhs = [A.alloc([D], BF16) for _ in range(2)]
            hTs = [A.alloc([8, 128], BF16) for _ in range(2)]
            for t in range(nt):
                b = t % 2
                P.dma("sp", xts[b], x_src[t * 128:(t + 1) * 128, :], reads=["x_d%d" % t], writes=["xt%d" % b])
                norm_to_hT(xts[b], "xt%d" % b, hTs[b], "hT%d" % b, s1c, b1c, ["s1c", "modcols"], xhs[b], "n%d" % b)
                P.dma("sp", hT_d[:, :, t * 128:(t + 1) * 128], hTs[b], reads=["hT%d" % b], writes=["hT_d%d" % t])
            P.barrier()
            A.release()
        if stop == "norm":
            break

        A.mark()
        mixer_phase(nc, P, A, ps, psv, l, nt, K, cd, dict(
            w_in=w_in, pool_w=pool_w, pscale=pscale, gains=gains, w1=w1, w2=w2, peT=peT,
            hT_d=hT_d, mixT_d=mixT_d, ident=ident, ones=ones, small=small, mstage=mstage))
        P.barrier()
        A.release()
        if stop == "mixer":
            break

        A.mark()
        wo = A.alloc([8, D], BF16)
        f1 = A.alloc([8, DFF], BF16)
        f2 = A.alloc([32, D], BF16)
        for kc in range(8):
            P.dma("pool", wo[:, kc, :], w_out[l][kc * 128:(kc + 1) * 128, :], writes=["wo"])
        for kc in range(8):
            for hh in range(2):
                P.dma("pool", f1[:, kc, hh * 2048:(hh + 1) * 2048], w_ff1[l][kc * 128:(kc + 1) * 128, hh * 2048:(hh + 1) * 2048], writes=["f1"])
        for c4 in range(8):
            P.dma("pool", f2[:, c4 * 4:(c4 + 1) * 4, :], w_ff2[l][c4 * 512:(c4 + 1) * 512, :].rearrange("(c p) n -> p c n", p=128), writes=["f2"])
        mts = [A.alloc([8, 128], BF16) for _ in range(2)]
        xts = [A.alloc([D], F32) for _ in range(2)]
        xh = A.alloc([D], BF16)
        h2T = A.alloc([8, 128], BF16)
        aT = A.alloc([32, 128], BF16)
        rl = [A.alloc([512], F32) for _ in range(2)]
        tmp = A.alloc([512], F32)
        for t in range(nt):
            b = t % 2
            xt = xts[b]
            xk = "xt%d" % b
            P.dma("sp", mts[b], mixT_d[:, :, t * 128:(t + 1) * 128], reads=["mixT_d%d" % t], writes=["mt%d" % b])
            P.dma("sp", xt, x_src[t * 128:(t + 1) * 128, :], reads=["x_d%d" % t], writes=[xk])
            for h in range(2):
                for kc in range(8):
                    P.op("pe", lambda E, h=h, kc=kc, b=b: E.matmul(ps[h][:, :], lhsT=mts[b][:, kc, :], rhs=wo[:, kc, h * 512:(h + 1) * 512], start=(kc == 0), stop=(kc == 7)),
                         reads=["mt%d" % b, "wo"], writes=["ps%d" % h])
                P.op("dve", lambda E, h=h: E.tensor_tensor(out=tmp, in0=ps[h][:, :], in1=gate1[:, h * 512:(h + 1) * 512], op=ALU.mult),
                     reads=["ps%d" % h, "gate0"], writes=["tmp"])
                P.op("dve", lambda E, h=h, xt=xt: E.tensor_tensor(out=xt[:, h * 512:(h + 1) * 512], in0=xt[:, h * 512:(h + 1) * 512], in1=tmp, op=ALU.add),
                     reads=["tmp", xk], writes=[xk])
            norm_to_hT(xt, xk, h2T, "h2T", s2c, b2c, ["s2c", "modcols"], xh, "f")
            for c4 in range(8):
                pf = ps[3 + (c4 % 2)]
                for cc in range(4):
                    c = c4 * 4 + cc
                    for kc in range(8):
                        P.op("pe", lambda E, c=c, cc=cc, kc=kc, pf=pf: E.matmul(pf[:, cc * 128:(cc + 1) * 128], lhsT=f1[:, kc, c * 128:(c + 1) * 128], rhs=h2T[:, kc, :],
                                                                               start=(kc == 0 and cc == 0), stop=(kc == 7 and cc == 3), skip_group_check=True),
                             reads=["h2T", "f1"], writes=["ps%d" % (3 + c4 % 2)])
                r = rl[c4 % 2]
                P.op("act", lambda E, pf=pf, r=r: E.activation(out=r, in_=pf[:, :], func=ACTF.Relu), reads=["ps%d" % (3 + c4 % 2)], writes=["rl%d" % (c4 % 2)])
                P.op("pool", lambda E, r=r, c4=c4: E.tensor_tensor(out=aT[:, c4 * 4:(c4 + 1) * 4, :].rearrange("p a b -> p (a b)"), in0=r, in1=r, op=ALU.mult),
                     reads=["rl%d" % (c4 % 2)], writes=["aT%d" % c4])
            for h in range(2):
                for c in range(32):
                    P.op("pe", lambda E, h=h, c=c: E.matmul(ps[h][:, :], lhsT=aT[:, c, :], rhs=f2[:, c, h * 512:(h + 1) * 512], start=(c == 0), stop=(c == 31)),
                         reads=["aT%d" % (c // 4), "f2"], writes=["ps%d" % h])
                P.op("dve", lambda E, h=h: E.tensor_tensor(out=tmp, in0=ps[h][:, :], in1=gate2[:, h * 512:(h + 1) * 512], op=ALU.mult),
                     reads=["ps%d" % h, "gate1"], writes=["tmp"])
                P.op("dve", lambda E, h=h, xt=xt: E.tensor_tensor(out=xt[:, h * 512:(h + 1) * 512], in0=xt[:, h * 512:(h + 1) * 512], in1=tmp, op=ALU.add),
                     reads=["tmp", xk], writes=[xk])
            P.dma("sp", x_dst[t * 128:(t + 1) * 128, :], xt, reads=[xk], writes=["x_d%d" % t])
            if l < n_layers - 1:
                pass
        P.barrier()
        A.release()
        if l < n_layers - 1:
            pass

    P.barrier()
    P.emit()
    P.close()
    for cm in reversed(ps_cms):
        cm.__exit__(None, None, None)
    big_cm.__exit__(None, None, None)
    return nc


def mixer_phase(nc, P, A, ps, psv, l, nt, K, cd, W):
    ident = W["ident"]
    hT_d, mixT_d = W["hT_d"], W["mixT_d"]
    MS = W.get("mstage", 99)

    def cp(eng, out, in_, reads, writes):
        if eng == "act":
            P.op("act", lambda E: E.activation(out=out, in_=in_, func=ACTF.Identity), reads, writes)
        else:
            P.op(eng, lambda E: E.tensor_copy(out=out, in_=in_), reads, writes)

    def tt(eng, out, a, b, op, reads, writes):
        P.op(eng, lambda E: E.tensor_tensor(out=out, in0=a, in1=b, op=op), reads, writes)

    tri = A.alloc([128], BF16)
    triinv = A.alloc([128], BF16)
    cmask = A.alloc([16, 128], BF16)
    expand = A.alloc([S], BF16, parts=64)
    keepB = A.alloc([128], F32)
    addB = A.alloc([128], F32)
    ov = A.alloc([2, 64], BF16)
    bands = A.alloc([3, 4, 128], F32)
    P.dma("pool", tri, cd["tri"], writes=["tri"])
    P.dma("pool", triinv, cd["triinv"], writes=["triinv"])
    P.dma("pool", cmask, cd["cmask"], writes=["cmask"])
    for hh in range(2):
        P.dma("pool", expand[:, hh * 2048:(hh + 1) * 2048], cd["expand"][:, hh * 2048:(hh + 1) * 2048], writes=["expand"])
    P.dma("sp", keepB, cd["keepB"], writes=["keepB"])
    P.dma("sp", addB, cd["addB"], writes=["addB"])
    P.dma("pool", ov, cd["ov"], writes=["ov"])
    P.dma("sp", bands, cd["bands"], writes=["bands"])
    win = A.alloc([8, INC], BF16)
    for kc in range(8):
        P.dma("pool", win[:, kc, :], W["w_in"][l][kc * 128:(kc + 1) * 128, :], writes=["win"])
    pw = A.alloc([4, 128], BF16)
    P.dma("pool", pw, W["pool_w"][l], writes=["pw"])
    psc = A.alloc([4], F32)
    P.dma("sp", psc, W["pscale"][l], writes=["psc"])
    gn = A.alloc([7, 64], F32)
    P.dma("sp", gn, W["gains"][l], writes=["gn"])
    W1 = A.alloc([2, 32, 64], BF16, parts=64)
    for kv in range(2):
        P.dma("pool", W1[:, kv, :, :].rearrange("p a b -> p (a b)"), W["w1"][l][:, kv * 2048:(kv + 1) * 2048], writes=["W1"])
    W2 = A.alloc([2, 64], BF16, parts=64)
    P.dma("pool", W2.rearrange("p a b -> p (a b)"), W["w2"][l], writes=["W2"])
    peT = A.alloc([2, 32], BF16, parts=64)
    P.dma("pool", peT.rearrange("p a b -> p (a b)"), W["peT"][l], writes=["peT"])
    cbias = A.alloc([2], F32, parts=64)
    pb = ps[0][0:64, 0:2]
    for kv in range(2):
        for p in range(32):
            P.op("pe", lambda E, kv=kv, p=p: E.matmul(pb[:, kv:kv + 1], lhsT=W1[:, kv, p, :], rhs=peT[:, kv, p:p + 1], start=(p == 0), stop=(p == 31)),
                 reads=["W1", "peT"], writes=["ps0"])
    cp("dve", cbias, pb, ["ps0"], ["cbias"])
    if MS < 2:
        return
    kT = [A.alloc([2, S], BF16, parts=64) for _ in range(2)]
    craw = [A.alloc([2, 144], BF16, parts=64) for _ in range(2)]
    V = [A.alloc([nt * 2, 66], BF16).rearrange("p (t k) c -> p t k c", k=2) for _ in range(2)]
    kTc = [A.alloc([256], BF16, parts=64) for _ in range(2)]
    Vc = [A.alloc([2, 66], BF16) for _ in range(2)]
    for g in range(2):
        P.op("pool", lambda E, g=g: E.memset(V[g], 1.0), writes=["V%d" % g])
        P.op("pool", lambda E, g=g: E.memset(Vc[g], 0.0), writes=["Vc%d" % g])
        P.op("pool", lambda E, g=g: E.memset(Vc[g][:, :, 64:65], 1.0), writes=["Vc%d" % g])
        P.op("pool", lambda E, g=g: E.memset(Vc[g][0:1, 0, 64:65], 0.0), writes=["Vc%d" % g])
        P.op("pool", lambda E, g=g: E.memset(kTc[g], 0.0), writes=["kTc%d" % g])
        P.op("pool", lambda E, g=g: E.memset(craw[g], 0.0), writes=["craw%d" % g])
    hTt = [A.alloc([8, 128], BF16) for _ in range(2)]
    cs = [A.alloc([2, 32], F32) for _ in range(2)]
    ccs = [A.alloc([2, 32], F32, parts=8) for _ in range(2)]
    u = [A.alloc([512], F32) for _ in range(2)]
    pj = A.alloc([2 * GW], F32)
    sq = A.alloc([6, 64], F32)
    st = A.alloc([8], F32)
    xn = A.alloc([6, 64], F32)
    ra = A.alloc([6, 32], F32)
    rb = A.alloc([6, 32], F32)
    rc = A.alloc([6, 32], F32)
    rd = A.alloc([6, 32], F32)
    tb = [A.alloc([9, 64], BF16) for _ in range(2)]
    qT = [A.alloc([512], BF16, parts=64) for _ in range(2)]
    sg = [A.alloc([12], F32) for _ in range(2)]
    Eb = [A.alloc([512], BF16) for _ in range(3)]
    msk = A.alloc([128], BF16)
    oall = A.alloc([3, 264], F32)
    rden = A.alloc([12], F32)
    coef = A.alloc([12], F32)
    imp = A.alloc([64], F32)
    imp2 = A.alloc([64], F32)
    wk = A.alloc([64], F32)
    m8 = A.alloc([16], F32)
    thr = A.alloc([1], F32)
    selb = A.alloc([128], BF16)
    selT = A.alloc([128], BF16, parts=64)
    y32 = A.alloc([256], F32)
    ybf = A.alloc([256], BF16)
    mixt = A.alloc([8, 128], BF16)
    pooledT = A.alloc([512], BF16)
    zc = A.alloc([16], F32)
    ec = A.alloc([16], F32)
    P.op("pool", lambda E: E.memset(zc, 0.0), writes=["zc"])
    sTc = A.alloc([16], BF16, parts=64)
    k8 = A.alloc([64], F32, parts=8)
    k8q = A.alloc([64], F32, parts=8)
    k8s = A.alloc([4], F32)
    P.op("pool", lambda E: E.memset(k8s, 1.0), writes=["k8s"])
    k8r = [A.alloc([32], F32, parts=8) for _ in range(4)]
    k8b = A.alloc([128], BF16)
    v8 = A.alloc([64], BF16, parts=8)

    pT128 = psv(2, [8, 128], BF16)
    pT = pT128[0:64]
    P.op("pool", lambda E: E.memset(selb, 0.0), writes=["selb"])
    P.op("pool", lambda E: E.memset(k8b, 0.0), writes=["k8b"])
    for g_ in range(2):
        P.op("pool", lambda E, g_=g_: E.memset(tb[g_], 0.0), writes=["tb%d" % g_])
    pT2 = psv(2, [2, 128], BF16)
    psO = ps[6]
    psI = ps[7]

    def bc_h(ap128):
        return ap128.unsqueeze(1).to_broadcast([128, 4, 128])

    def e4(buf):
        return buf.rearrange("p (h q) -> p h q", h=4)

    def attention_branch(g, t, br, kts, kTsrc, vsrc, vkey, masks, use_sel):
        nk = len(kts)

        def scores(i):
            kt = kts[i]
            bank = 3 + i % 2
            P.op("pe", lambda E: E.matmul(ps[bank][:, :], lhsT=kTsrc(kt), rhs=qT[g], start=True, stop=True),
                 reads=["kcache%d" % g, "kTc%d" % g, "qT%d" % g], writes=["ps%d" % bank])
            if use_sel:
                slot = ps[5][:, (i % 2) * 128:(i % 2 + 1) * 128]
                P.op("pe", lambda E: E.matmul(slot, lhsT=expand[:, kt * 128:(kt + 1) * 128], rhs=selT, start=True, stop=True),
                     reads=["expand", "selT"], writes=["ps5"])
        scores(0)
        for i, kt in enumerate(kts):
            if i + 1 < nk:
                scores(i + 1)
            bank = 3 + i % 2
            E_ = Eb[i % 3]
            ek = "Eb%d" % (i % 3)
            P.op("act", lambda E, bank=bank, E_=E_: E.activation(out=E_, in_=ps[bank][:, :], func=ACTF.Exp, scale=0.125),
                 reads=["ps%d" % bank], writes=[ek])
            m = masks(kt)
            if use_sel:
                slot = ps[5][:, (i % 2) * 128:(i % 2 + 1) * 128]
                sk = "ps5"
                if m is not None:
                    tt("dve", msk, slot, m[0], ALU.mult, [sk, m[1]], ["msk"])
                    tt("pool", e4(E_), e4(E_), bc_h(msk), ALU.mult, [ek, "msk"], [ek])
                else:
                    tt("dve", e4(E_), e4(E_), bc_h(slot), ALU.mult, [ek, sk], [ek])
            elif m is not None:
                tt("pool", e4(E_), e4(E_), bc_h(m[0]), ALU.mult, [ek, m[1]], [ek])
            for h in range(4):
                first = (i == 0 and h == 0)
                last = (i == nk - 1 and h == 3)
                P.op("pe", lambda E, h=h, kt=kt, E_=E_, first=first, last=last: E.matmul(
                    psO[:, h * 66:(h + 1) * 66], lhsT=E_[:, h * 128:(h + 1) * 128], rhs=vsrc(kt), start=first, stop=last, skip_group_check=True),
                    reads=[ek, vkey], writes=["ps6"])
                if br == 0:
                    P.op("pe", lambda E, h=h, kt=kt, E_=E_, first=first, last=last: E.matmul(
                        psI[:, h * 64:(h + 1) * 64], lhsT=E_[:, h * 128:(h + 1) * 128], rhs=ov[:, kt, :], start=first, stop=last, skip_group_check=True),
                        reads=[ek, "ov"], writes=["ps7"])
        cp("act", oall[:, br, :], psO[:, 0:264], ["ps6"], ["oall%d" % br])
        P.op("dve", lambda E: E.tensor_scalar_max(out=rden[:, br * 4:(br + 1) * 4], in0=oall[:, br, :].rearrange("p (h c) -> p h c", c=66)[:, :, 64], scalar1=1e-30),
             reads=["oall%d" % br], writes=["rden%d" % br])
        P.op("dve", lambda E: E.reciprocal(out=rden[:, br * 4:(br + 1) * 4], in_=rden[:, br * 4:(br + 1) * 4]),
             reads=["rden%d" % br], writes=["rden%d" % br])

    for t in range(nt):
        b = t % 2
        ts = slice(t * 128, (t + 1) * 128)
        P.dma("sp", hTt[b], hT_d[:, :, ts], reads=["hT_d%d" % t], writes=["hTt%d" % b])
        P.dma("sp", cs[b][:, 0, :], cd["cos"][:, t, :], writes=["cs%d" % b])
        P.dma("sp", cs[b][:, 1, :], cd["sin"][:, t, :], writes=["cs%d" % b])
        P.dma("sp", ccs[b][:, 0, :], cd["ccos"][:, t, :], writes=["ccs%d" % b])
        P.dma("sp", ccs[b][:, 1, :], cd["csin"][:, t, :], writes=["ccs%d" % b])
        chunks = [(0, 512, u[b], "u%d" % b), (512, 448, pj[:, 0:448], "pj0"), (960, 204, pj[:, 448:652], "pj0"),
                  (1164, 448, pj[:, 652:1100], "pj1"), (1612, 204, pj[:, 1100:1304], "pj1")]
        for ci, (c0, wd, dst, dk) in enumerate(chunks):
            bank = ci % 2
            for kc in range(8):
                P.op("pe", lambda E, kc=kc, c0=c0, wd=wd, bank=bank: E.matmul(ps[bank][:, 0:wd], lhsT=hTt[b][:, kc, :], rhs=win[:, kc, c0:c0 + wd],
                                                                               start=(kc == 0), stop=(kc == 7)),
                     reads=["hTt%d" % b, "win"], writes=["ps%d" % bank])
            cp("act" if ci % 2 == 0 else "dve", dst, ps[bank][:, 0:wd], ["ps%d" % bank], [dk])
        if MS < 4:
            continue
        for g in range(2):
            base = g * GW
            pk = "pj%d" % g
            qk = pj[:, base:base + 384].rearrange("p (s d) -> p s d", d=64)
            tt("dve", sq, qk, qk, ALU.mult, [pk], ["sq"])
            P.op("dve", lambda E: E.tensor_reduce(out=st[:, 0:6], in_=sq, axis=AX.X, op=ALU.add), reads=["sq"], writes=["st"])
            P.op("act", lambda E: E.activation(out=st[:, 0:6], in_=st[:, 0:6], func=ACTF.Ln, scale=1.0 / 64, bias=EPS), reads=["st"], writes=["st"])
            P.op("act", lambda E: E.activation(out=st[:, 0:6], in_=st[:, 0:6], func=ACTF.Exp, scale=-0.5), reads=["st"], writes=["st"])
            tt("dve", xn, qk, st[:, 0:6].unsqueeze(2).to_broadcast([128, 6, 64]), ALU.mult, [pk, "st"], ["xn"])
            tt("pool", xn, xn, gn[:, 0:6, :], ALU.mult, ["xn", "gn"], ["xn"])
            cosb = cs[b][:, 0, :].unsqueeze(1).to_broadcast([128, 6, 32])
            sinb = cs[b][:, 1, :].unsqueeze(1).to_broadcast([128, 6, 32])
            x1 = xn[:, :, 0:32]
            x2 = xn[:, :, 32:64]
            ck = "cs%d" % b
            tbg = tb[g]
            tk = "tb%d" % g
            tt("dve", ra, x1, cosb, ALU.mult, ["xn", ck], ["ra"])
            tt("pool", rb, x2, sinb, ALU.mult, ["xn", ck], ["rb"])
            tt("dve", tbg[:, 0:6, 0:32], ra, rb, ALU.subtract, ["ra", "rb"], [tk])
            tt("pool", rc, x2, cosb, ALU.mult, ["xn", ck], ["rc"])
            tt("dve", rd, x1, sinb, ALU.mult, ["xn", ck], ["rd"])
            tt("pool", tbg[:, 0:6, 32:64], rc, rd, ALU.add, ["rc", "rd"], [tk])
            cp("act", tbg[:, 6:8, :], pj[:, base + 384:base + 512].rearrange("p (s d) -> p s d", d=64), [pk], [tk])
            cp("dve", V[g][:, t, :, 0:64], pj[:, base + 512:base + 640].rearrange("p (s d) -> p s d", d=64), [pk], ["V%d" % g])
            sgg = sg[g]
            P.op("act", lambda E, sgg=sgg, base=base: E.activation(out=sgg, in_=pj[:, base + 640:base + 652], func=ACTF.Exp, scale=-1.0),
                 reads=[pk], writes=["sg%d" % g])
            P.op("dve", lambda E, sgg=sgg: E.tensor_scalar_add(out=sgg, in0=sgg, scalar1=1.0), reads=["sg%d" % g], writes=["sg%d" % g])
            P.op("dve", lambda E, sgg=sgg: E.reciprocal(out=sgg, in_=sgg), reads=["sg%d" % g], writes=["sg%d" % g])
        if MS < 5:
            continue
        psP = ps[0]
        for gp in range(4):
            kind = 2 if t == 0 else 0
            P.op("pe", lambda E, gp=gp, kind=kind: E.matmul(psP[:, gp * 128:(gp + 1) * 128], lhsT=u[b][:, gp * 128:(gp + 1) * 128], rhs=bands[:, kind, gp, :],
                                                            start=True, stop=(t == 0), skip_group_check=True),
                 reads=["u%d" % b, "bands"], writes=["ps0"])
            if t > 0:
                P.op("pe", lambda E, gp=gp: E.matmul(psP[:, gp * 128:(gp + 1) * 128], lhsT=u[1 - b][:, gp * 128:(gp + 1) * 128], rhs=bands[:, 1, gp, :],
                                                     start=False, stop=True, skip_group_check=True),
                     reads=["u%d" % (1 - b), "bands"], writes=["ps0"])
        if MS < 5.3:
            continue
        cp("act", pooledT, psP[:, :], ["ps0"], ["pooledT"])
        if MS < 5.6:
            continue
        psY = ps[1]
        for gp in range(4):
            P.op("pe", lambda E, gp=gp: E.matmul(psY[:, gp * 128:(gp + 1) * 128], lhsT=pw[:, gp, :], rhs=pooledT[:, gp * 128:(gp + 1) * 128], start=True, stop=True),
                 reads=["pw", "pooledT"], writes=["ps1"])
        if MS < 5.8:
            continue
        for gp in range(4):
            P.op("act", lambda E, gp=gp: E.activation(out=mixt[:, gp, :], in_=psY[:, gp * 128:(gp + 1) * 128], func=ACTF.Identity, scale=psc[:, gp:gp + 1]),
                 reads=["ps1", "psc"], writes=["mixt"])
        if MS < 6:
            continue
        for g in range(2):
            tbg = tb[g]
            tk = "tb%d" % g
            for s_ in range(8):
                P.op("pe", lambda E, s_=s_, tbg=tbg: E.transpose(out=pT128[:, s_, :], in_=tbg[:, s_:s_ + 2, :].rearrange("p a b -> p (a b)"), identity=ident), reads=[tk, "K_ident"], writes=["ps2"])
            if MS < 6.2:
                continue
            cp("dve", qT[g].rearrange("p (a b) -> p a b", a=4), pT[:, 0:4, :], ["ps2"], ["qT%d" % g])
            if MS < 6.4:
                continue
            cp("dve", kT[g][:, :, ts], pT[:, 4:6, :], ["ps2"], ["kcache%d" % g])
            if MS < 6.6:
                continue
            if t > 0:
                cp("dve", craw[g][:, :, 0:16], craw[g][:, :, 128:144], ["craw%d" % g], ["craw%d" % g])
            cp("dve", craw[g][:, :, 16:144], pT[:, 6:8, :], ["ps2"], ["craw%d" % g])
            if MS < 7:
                continue
            preT = ps[0][0:64, 0:16]
            for kv in range(2):
                for p in range(32):
                    P.op("pe", lambda E, kv=kv, p=p: E.matmul(preT[:, kv * 8:(kv + 1) * 8], lhsT=W1[:, kv, p, :], rhs=craw[g][:, kv, p:p + 113:16],
                                                              start=(p == 0), stop=(p == 31), skip_group_check=True),
                         reads=["W1", "craw%d" % g], writes=["ps0"])
            if MS < 7.2:
                continue
            for kv in range(2):
                P.op("dve", lambda E, kv=kv: E.tensor_scalar(out=zc[0:64, kv * 8:(kv + 1) * 8], in0=preT[:, kv * 8:(kv + 1) * 8], scalar1=cbias[:, kv:kv + 1],
                                                             scalar2=None, op0=ALU.add), reads=["ps0", "cbias"], writes=["zc"])
            P.op("act", lambda E: E.activation(out=ec, in_=zc, func=ACTF.Exp, scale=-1.0), reads=["zc"], writes=["ec"])
            P.op("dve", lambda E: E.tensor_scalar_add(out=ec[0:64], in0=ec[0:64], scalar1=1.0), reads=["ec"], writes=["ec"])
            P.op("dve", lambda E: E.reciprocal(out=ec[0:64], in_=ec[0:64]), reads=["ec"], writes=["ec"])
            tt("dve", sTc, zc[0:64], ec[0:64], ALU.mult, ["zc", "ec"], ["sTc"])
            if MS < 7.4:
                continue
            k8p = ps[1][0:8, 0:128]
            for kv in range(2):
                P.op("pe", lambda E, kv=kv: E.matmul(k8p[:, kv * 64:(kv + 1) * 64], lhsT=sTc[:, kv * 8:(kv + 1) * 8], rhs=W2[:, kv, :], start=True, stop=True),
                     reads=["sTc", "W2"], writes=["ps1"])
            if MS < 7.5:
                continue
            cp("dve", v8, k8p[:, 64:128], ["ps1"], ["v8"])
            if t == 0:
                P.op("pool", lambda E: E.memset(v8[0:1, :], 0.0), reads=[], writes=["v8"])
            r0 = 8 * (t % 16)
            P.dma("sp", Vc[g][r0:r0 + 8, t // 16, 0:64], v8, reads=["v8"], writes=["Vc%d" % g])
            if MS < 7.6:
                continue
            cp("dve", k8, k8p[:, 0:64], ["ps1"], ["k8"])
            tt("dve", k8q, k8, k8, ALU.mult, ["k8"], ["k8q"])
            P.op("dve", lambda E: E.tensor_reduce(out=k8s[0:8, 0:1], in_=k8q, axis=AX.X, op=ALU.add), reads=["k8q"], writes=["k8s"])
            P.op("act", lambda E: E.activation(out=k8s[:, 0:1], in_=k8s[:, 0:1], func=ACTF.Ln, scale=1.0 / 64, bias=EPS), reads=["k8s"], writes=["k8s"])
            P.op("act", lambda E: E.activation(out=k8s[:, 0:1], in_=k8s[:, 0:1], func=ACTF.Exp, scale=-0.5), reads=["k8s"], writes=["k8s"])
            P.op("dve", lambda E: E.tensor_scalar(out=k8, in0=k8, scalar1=k8s[0:8, 0:1], scalar2=None, op0=ALU.mult), reads=["k8", "k8s"], writes=["k8"])
            tt("dve", k8, k8, gn[0:8, 6, :], ALU.mult, ["k8", "gn"], ["k8"])
            cck = "ccs%d" % b
            cc_, ss_ = ccs[b][:, 0, :], ccs[b][:, 1, :]
            tt("dve", k8r[0], k8[:, 0:32], cc_, ALU.mult, ["k8", cck], ["k8r0"])
            tt("dve", k8r[1], k8[:, 32:64], ss_, ALU.mult, ["k8", cck], ["k8r1"])
            tt("dve", k8b[0:8, 0:32], k8r[0], k8r[1], ALU.subtract, ["k8r0", "k8r1"], ["k8b"])
            tt("dve", k8r[2], k8[:, 32:64], cc_, ALU.mult, ["k8", cck], ["k8r2"])
            tt("dve", k8r[3], k8[:, 0:32], ss_, ALU.mult, ["k8", cck], ["k8r3"])
            tt("dve", k8b[0:8, 32:64], k8r[2], k8r[3], ALU.add, ["k8r2", "k8r3"], ["k8b"])
            if MS < 7.8:
                continue
            pt8 = ps[7][0:64, 256:264]
            P.op("pe", lambda E: E.matmul(pt8, lhsT=k8b[0:8, 0:64], rhs=ident[0:8, 0:8], start=True, stop=True), reads=["k8b", "K_ident"], writes=["ps7"])
            cp("dve", kTc[g][:, 8 * t:8 * t + 8], pt8, ["ps7"], ["kTc%d" % g])
            if MS < 8:
                continue
            kts_c = [0] if t < 16 else [0, 1]
            attention_branch(g, t, 0, kts_c, lambda kt: kTc[g][:, kt * 128:(kt + 1) * 128], lambda kt: Vc[g][:, kt, :], "Vc%d" % g,
                             lambda kt: ((cmask[:, t % 16, :], "cmask") if kt == t // 16 else None), False)
            P.op("dve", lambda E: E.tensor_scalar(out=imp, in0=psI[:, 0:64], scalar1=rden[:, 0:1], scalar2=None, op0=ALU.mult), reads=["ps7", "rden0"], writes=["imp"])
            for h in range(1, 4):
                P.op("dve", lambda E, h=h: E.scalar_tensor_tensor(out=imp, in0=psI[:, h * 64:(h + 1) * 64], scalar=rden[:, h:h + 1], in1=imp, op0=ALU.mult, op1=ALU.add),
                     reads=["ps7", "rden0", "imp"], writes=["imp"])
            if MS < 9:
                continue
            j0 = 64 - 2 * t
            tt("dve", imp2, imp, keepB[:, j0:j0 + 64], ALU.mult, ["imp", "keepB"], ["imp2"])
            tt("dve", imp2, imp2, addB[:, j0:j0 + 64], ALU.add, ["imp2", "addB"], ["imp2"])
            P.op("dve", lambda E: E.memset(imp2[:, 0:1], 1e4), reads=[], writes=["imp2"])
            P.op("dve", lambda E: E.max(out=m8[:, 0:8], in_=imp2), reads=["imp2"], writes=["m8"])
            P.op("dve", lambda E: E.match_replace(out=wk, in_to_replace=m8[:, 0:8], in_values=imp2, imm_value=-2.0), reads=["imp2", "m8"], writes=["wk"])
            P.op("dve", lambda E: E.max(out=m8[:, 8:16], in_=wk), reads=["wk"], writes=["m8"])
            P.op("dve", lambda E: E.tensor_scalar_max(out=thr, in0=m8[:, 15:16], scalar1=0.0), reads=["m8"], writes=["thr"])
            P.op("dve", lambda E: E.tensor_scalar(out=selb[:, 0:64], in0=imp2, scalar1=thr[:, 0:1], scalar2=None, op0=ALU.is_ge), reads=["imp2", "thr"], writes=["selb"])
            psel = ps[7][0:64, 384:512]
            P.op("pe", lambda E: E.matmul(psel, lhsT=selb[:, 0:64], rhs=ident, start=True, stop=True), reads=["selb", "K_ident"], writes=["ps7"])
            cp("dve", selT, psel, ["ps7"], ["selT"])
            if MS < 10:
                continue
            attention_branch(g, t, 1, list(range(t + 1)), lambda kt: kT[g][:, 0, kt * 128:(kt + 1) * 128], lambda kt: V[g][:, kt, 0, :], "V%d" % g,
                             lambda kt: ((tri, "tri") if kt == t else None), True)
            if MS < 11:
                continue
            attention_branch(g, t, 2, list(range(max(0, t - 4), t + 1)), lambda kt: kT[g][:, 1, kt * 128:(kt + 1) * 128], lambda kt: V[g][:, kt, 1, :], "V%d" % g,
                             lambda kt: ((tri, "tri") if kt == t else ((triinv, "triinv") if kt == t - 4 else None)), False)
            if MS < 12:
                continue
            tt("dve", coef.rearrange("p (b h) -> p b h", h=4), sg[g].rearrange("p (h b) -> p b h", b=3), rden.rearrange("p (b h) -> p b h", h=4), ALU.mult,
               ["sg%d" % g, "rden0", "rden1", "rden2"], ["coef"])
            for h in range(4):
                ysl = y32[:, h * 64:(h + 1) * 64]
                P.op("dve", lambda E, h=h, ysl=ysl: E.tensor_scalar(out=ysl, in0=oall[:, 0, h * 66:h * 66 + 64], scalar1=coef[:, h:h + 1], scalar2=None, op0=ALU.mult),
                     reads=["oall0", "coef"], writes=["y32"])
                P.op("dve", lambda E, h=h, ysl=ysl: E.scalar_tensor_tensor(out=ysl, in0=oall[:, 1, h * 66:h * 66 + 64], scalar=coef[:, 4 + h:5 + h], in1=ysl, op0=ALU.mult, op1=ALU.add),
                     reads=["oall1", "coef", "y32"], writes=["y32"])
                P.op("dve", lambda E, h=h, ysl=ysl: E.scalar_tensor_tensor(out=ybf[:, h * 64:(h + 1) * 64], in0=oall[:, 2, h * 66:h * 66 + 64], scalar=coef[:, 8 + h:9 + h], in1=ysl, op0=ALU.mult, op1=ALU.add),
                     reads=["oall2", "coef", "y32"], writes=["ybf"])
            for c in range(2):
                P.op("pe", lambda E, c=c: E.transpose(out=pT2[:, c, :], in_=ybf[:, c * 128:(c + 1) * 128], identity=ident), reads=["ybf", "K_ident"], writes=["ps2"])
            cp("act", mixt[:, 4 + 2 * g:6 + 2 * g, :], pT2, ["ps2"], ["mixt"])
        P.dma("sp", mixT_d[:, :, ts], mixt, reads=["mixt"], writes=["mixT_d%d" % t])


def _perm_in_cols():
    o0, o1 = 512, 1024
    o2 = o1 + 768
    cols = list(range(512))
    for g in range(2):
        for h in range(4):
            cols += list(range(o0 + (g * 4 + h) * 64, o0 + (g * 4 + h + 1) * 64))

        def kvc(kvi, br):
            st = o1 + ((kvi * 3 + br) * 2 + g) * 64
            return list(range(st, st + 64))
        cols += kvc(0, 1) + kvc(0, 2) + kvc(0, 0) + kvc(1, 0) + kvc(1, 1) + kvc(1, 2)
        cols += list(range(o2 + g * 12, o2 + (g + 1) * 12))
    assert len(cols) == INC
    return np.array(cols)


_PROG_CACHE = {}


def _get_prog(n_layers, debug=False):
    key = (n_layers, debug)
    if key not in _PROG_CACHE:
        _PROG_CACHE[key] = build_program(n_layers, debug=debug)
    return _PROG_CACHE[key]


def _layer_inputs(ls, x_b, c, w_mod, b_mod, norm1, norm2, w_in, pool_w, pool_scale, q_norm, k_norm,
                  cmp_pe, cmp_w1, cmp_w2, w_out, w_ff1, w_ff2, b, consts, perm):
    f = np.float32
    ls = list(ls)
    n = len(ls)

    def col(v):
        return np.ascontiguousarray(v.reshape(8, 128).T)
    d = {}
    d["x"] = np.ascontiguousarray(x_b, dtype=f)
    d["c_col"] = col(np.asarray(c[b], f))
    d["w_mod"] = np.ascontiguousarray(w_mod[ls], dtype=f)
    d["b_mod"] = np.ascontiguousarray(b_mod[ls], dtype=f).reshape(n, 1, 6 * D)
    d["n1c"] = np.stack([col(norm1[l]) for l in ls]).astype(f)
    d["n2c"] = np.stack([col(norm2[l]) for l in ls]).astype(f)
    d["w_in"] = np.ascontiguousarray(w_in[ls][:, :, perm], dtype=f)
    d["pool_w"] = np.ascontiguousarray(np.transpose(pool_w[ls], (0, 2, 1, 3)), dtype=f)
    d["pscale"] = np.ascontiguousarray(np.transpose(pool_scale[ls].reshape(n, 4, 128), (0, 2, 1)), dtype=f)
    gains = np.zeros((n, 128, 7, 64), f)
    for i, l in enumerate(ls):
        gains[i, :, 0:4, :] = q_norm[l][None, None, :]
        gains[i, :, 4, :] = k_norm[l, 1][None, :]
        gains[i, :, 5, :] = k_norm[l, 2][None, :]
        gains[i, :, 6, :] = k_norm[l, 0][None, :]
    d["gains"] = gains
    w1 = cmp_w1[ls].reshape(n, 2, 32, 64, 64)
    d["w1"] = np.ascontiguousarray(np.transpose(w1, (0, 3, 1, 2, 4)).reshape(n, 64, 2 * 32 * 64), dtype=f)
    d["w2"] = np.ascontiguousarray(np.transpose(cmp_w2[ls], (0, 2, 1, 3)).reshape(n, 64, 128), dtype=f)
    d["peT"] = np.ascontiguousarray(np.transpose(cmp_pe[ls], (0, 3, 1, 2)).reshape(n, 64, 64), dtype=f)
    d["w_out"] = np.ascontiguousarray(w_out[ls], dtype=f)
    d["w_ff1"] = np.ascontiguousarray(w_ff1[ls], dtype=f)
    d["w_ff2"] = np.ascontiguousarray(w_ff2[ls], dtype=f)
    for name, shape, _, _ in _CONST_SPECS:
        d["k_" + name] = np.ascontiguousarray(consts[name], dtype=f).reshape(shape)
    return d


LAYERS_PER_LAUNCH = 4


def kernel(x, c, w_mod, b_mod, norm1, norm2, w_in, pool_w, pool_scale, q_norm, k_norm,
           cmp_pe, cmp_w1, cmp_w2, w_out, w_ff1, w_ff2):
    args = [np.asarray(a) for a in (c, w_mod, b_mod, norm1, norm2, w_in, pool_w, pool_scale, q_norm, k_norm,
                                    cmp_pe, cmp_w1, cmp_w2, w_out, w_ff1, w_ff2)]
    x = np.asarray(x, np.float32)
    consts = _consts()
    perm = _perm_in_cols()
    nc = _get_prog(LAYERS_PER_LAUNCH)
    xs = [x[b] for b in range(B)]
    for l0 in range(0, L, LAYERS_PER_LAUNCH):
        ls = range(l0, l0 + LAYERS_PER_LAUNCH)
        maps = [_layer_inputs(ls, xs[b], *args, b, consts, perm) for b in range(B)]
        in_maps = [maps[i % B] for i in range(NCORES)]
        res = run_bass_kernel_spmd(nc, in_maps, core_ids=list(range(NCORES)))
        xs = [np.asarray(res.results[b]["y"], np.float32) for b in range(B)]
    return np.stack(xs, 0).astype(np.float32)
```

```python
import numpy as np
import ml_dtypes
import concourse.bass as bass
import concourse.mybir as mybir
from concourse.bass_utils import run_bass_kernel_spmd

F32 = mybir.dt.float32
BF16 = mybir.dt.bfloat16
ALU = mybir.AluOpType
ACTF = mybir.ActivationFunctionType
AX = mybir.AxisListType

NCORES = 8
B, S, D, L = 4, 4096, 1024, 4
NT = S // 128
DFF = 4096
EPS = 1e-6
GW = 652
INC = 512 + 2 * GW
POOL_WINDOWS = (2, 4, 8, 16)
ENGS = ("pe", "act", "dve", "pool", "sp")


class _Rec:
    def __getattr__(self, name):
        def f(*a, **k):
            self.call = (name, a, k)
            return self
        return f


class Prog:
    def __init__(self, nc, n_dma_sems=32):
        self.nc = nc
        self.q = {e: [] for e in ENGS}
        self.cnt = {e: 0 for e in ENGS}
        self.sems = {}
        self._ctx = []
        for e in ENGS:
            cm = nc.semaphore("s_" + e)
            self.sems[e] = cm.__enter__()
            self._ctx.append(cm)
        self.n_dma = n_dma_sems
        self.dma_uses = [0] * n_dma_sems
        self.dma_rr = 0
        for i in range(n_dma_sems):
            cm = nc.semaphore("s_dma%d" % i)
            self.sems["dma%d" % i] = cm.__enter__()
            self._ctx.append(cm)
        self.waited = {e: {} for e in ENGS}
        self.last_w = {}
        self.readers = {}
        self.ninst = 0

    def close(self):
        for cm in reversed(self._ctx):
            cm.__exit__(None, None, None)

    def _deps(self, eng, reads, writes):
        need = {}

        def add(tok):
            s, v = tok
            if eng == "pe" and s == "pe":
                return
            if need.get(s, 0) < v:
                need[s] = v
        for k in reads:
            if k in self.last_w:
                add(self.last_w[k])
        for k in writes:
            if k in self.last_w:
                add(self.last_w[k])
            for tok in self.readers.get(k, ()):
                add(tok)
        out = []
        w = self.waited[eng]
        for s, v in need.items():
            if w.get(s, 0) < v:
                w[s] = v
                out.append((s, v))
        return out

    def _commit(self, tok, reads, writes):
        for k in writes:
            self.last_w[k] = tok
            self.readers[k] = []
        for k in reads:
            if k in writes:
                continue
            self.readers.setdefault(k, []).append(tok)

    def op(self, eng, fn, reads=(), writes=()):
        waits = self._deps(eng, reads, writes)
        self.cnt[eng] += 1
        tok = (eng, self.cnt[eng])
        sems = self.sems
        rec = _Rec()
        fn(rec)
        name, a, k = rec.call

        def run(E, waits=waits, s=sems[eng], name=name, a=a, k=k):
            for (ws, wv) in waits:
                E.wait_ge(sems[ws], wv)
            getattr(E, name)(*a, **k).then_inc(s, 1)
        self.q[eng].append(run)
        self._commit(tok, reads, writes)
        self.ninst += 1 + len(waits)
        return tok

    def dma(self, eng, out, in_, reads=(), writes=(), **kw):
        i = self.dma_rr
        self.dma_rr = (self.dma_rr + 1) % self.n_dma
        sname = "dma%d" % i
        waits = self._deps(eng, reads, writes)
        prev = 16 * self.dma_uses[i]
        if prev and self.waited[eng].get(sname, 0) < prev:
            self.waited[eng][sname] = prev
            waits.append((sname, prev))
        self.dma_uses[i] += 1
        tok = (sname, 16 * self.dma_uses[i])
        sems = self.sems

        def run(E, waits=waits, s=sems[sname]):
            for (ws, wv) in waits:
                E.wait_ge(sems[ws], wv)
            E.dma_start(out=out, in_=in_, **kw).then_inc(s, 16)
        self.q[eng].append(run)
        self._commit(tok, reads, writes)
        self.ninst += 1 + len(waits)
        return tok

    def barrier(self):
        waits = []
        for e in ENGS:
            if self.cnt[e]:
                waits.append((e, self.cnt[e]))
        for i in range(self.n_dma):
            if self.dma_uses[i]:
                waits.append(("dma%d" % i, 16 * self.dma_uses[i]))
        sems = self.sems
        for e in ENGS:
            mine = [(s, v) for (s, v) in waits if self.waited[e].get(s, 0) < v and not (s == e and e == "pe")]
            for (s, v) in mine:
                self.waited[e][s] = v

            def run(E, mine=mine):
                for (ws, wv) in mine:
                    E.wait_ge(sems[ws], wv)
            self.q[e].append(run)
        self.last_w = {}
        self.readers = {}

    def emit(self):
        nc = self.nc
        q = self.q
        with nc.Block() as block:
            @block.tensor
            def _(E):
                for f in q["pe"]:
                    f(E)

            @block.scalar
            def _(E):
                for f in q["act"]:
                    f(E)

            @block.vector
            def _(E):
                for f in q["dve"]:
                    f(E)

            @block.gpsimd
            def _(E):
                for f in q["pool"]:
                    f(E)

            @block.sync
            def _(E):
                for f in q["sp"]:
                    f(E)


class Arena:
    def __init__(self, big, nbytes):
        self.big = big
        self.nbytes = nbytes
        self.off = 0
        self.marks = []

    def alloc(self, shape, dtype, parts=128):
        n = int(np.prod(shape))
        esz = 4 if dtype == F32 else 2
        nb = (n * esz + 31) // 32 * 32
        assert self.off + nb <= self.nbytes, ("SBUF arena overflow", self.off, nb, self.nbytes)
        w0 = self.off // 4
        v = self.big[0:parts, w0:w0 + nb // 4]
        if dtype != F32:
            v = v.bitcast(dtype)
        v = v[:, 0:n]
        self.off += nb
        if len(shape) == 2:
            return v.rearrange("p (a b) -> p a b", b=shape[1])
        if len(shape) == 3:
            return v.rearrange("p (a b c) -> p a b c", b=shape[1], c=shape[2])
        return v

    def mark(self):
        self.marks.append(self.off)

    def release(self):
        self.off = self.marks.pop()


def _consts():
    bf = ml_dtypes.bfloat16
    c = {}
    c["ident"] = np.eye(128, dtype=np.float32)
    inv = (10000.0 ** (-np.arange(32, dtype=np.float32) * 2.0 / 64.0)).astype(np.float32)
    pos = (np.arange(NT)[None, :] * 128 + np.arange(128)[:, None]).astype(np.float32)
    ang = pos[:, :, None] * inv[None, None, :]
    c["cos"] = np.cos(ang).astype(np.float32)
    c["sin"] = np.sin(ang).astype(np.float32)
    m = (np.arange(NT)[None, :] * 8 + np.arange(8)[:, None])
    cpos = (16 * m + 15).astype(np.float32)
    cang = cpos[:, :, None] * inv[None, None, :]
    c["ccos"] = np.cos(cang).astype(np.float32)
    c["csin"] = np.sin(cang).astype(np.float32)
    k = np.arange(128)[:, None]
    q = np.arange(128)[None, :]
    c["tri"] = (k <= q).astype(np.float32)
    c["triinv"] = (k > q).astype(np.float32)
    cm = np.zeros((128, 16, 128), np.float32)
    for tt in range(16):
        i = np.arange(128)[:, None] - 8 * tt
        r = np.arange(128)[None, :]
        vis = (i < 0) | ((i >= 0) & (i <= 7) & (r >= 16 * i + 15))
        cm[:, tt, :] = vis
    c["cmask"] = cm
    ex = np.zeros((64, S), np.float32)
    ex[np.arange(S) // 64, np.arange(S)] = 1.0
    c["expand"] = ex
    r = np.arange(128)[:, None]
    jj = np.arange(128)[None, :]
    cur = 64 + (r >= 64)
    keep = (jj < cur - 1).astype(np.float32)
    add = np.where((jj == cur) | (jj == cur - 1), 1e4, np.where(jj > cur, -1.0, 0.0)).astype(np.float32)
    c["keepB"] = keep
    c["addB"] = add
    n_cmp = (S - 32) // 16 + 1
    cs0 = np.arange(n_cmp) * 16
    ss0 = np.arange(64) * 64
    ov = np.minimum(cs0[:, None] + 32, ss0[None, :] + 64) - np.maximum(cs0[:, None], ss0[None, :])
    ov = np.clip(ov, 0, None).astype(np.float32) / 32.0
    ovs = np.zeros((256, 64), np.float32)
    ovs[1:1 + n_cmp] = ov
    c["ov"] = ovs.reshape(2, 128, 64).transpose(1, 0, 2).copy()
    bands = np.zeros((128, 3, 4, 128), np.float32)
    s = np.arange(128)[:, None]
    t = np.arange(128)[None, :]
    for gi, w in enumerate(POOL_WINDOWS):
        main = ((s <= t) & (s > t - w)).astype(np.float32) / w - (s == t)
        corner = (s >= 129 + t - w).astype(np.float32) / w
        cnt = np.minimum(t + 1, w).astype(np.float32)
        first = ((s <= t) & (s > t - w)).astype(np.float32) / cnt - (s == t)
        bands[:, 0, gi, :] = main
        bands[:, 1, gi, :] = corner
        bands[:, 2, gi, :] = first
    c["bands"] = bands
    c["ones"] = np.ones((128, 128), np.float32)
    return c


_CONST_SPECS = [
    ("ident", [128, 128], BF16, 128), ("cos", [128, NT, 32], F32, 128), ("sin", [128, NT, 32], F32, 128),
    ("ccos", [8, NT, 32], F32, 8), ("csin", [8, NT, 32], F32, 8),
    ("tri", [128, 128], BF16, 128), ("triinv", [128, 128], BF16, 128),
    ("cmask", [128, 16, 128], BF16, 128), ("expand", [64, S], BF16, 64),
    ("keepB", [128, 128], F32, 128), ("addB", [128, 128], F32, 128),
    ("ov", [128, 2, 64], BF16, 128), ("bands", [128, 3, 4, 128], F32, 128), ("ones", [128, 128], F32, 128),
]


def build_program(n_layers, first_layer_norm=True, debug=False, nt=NT, stop=None, mstage=99):
    nc = bass.Bass("TRN2", target_bir_lowering=False)
    LW = n_layers

    def din(name, shape, dt=F32):
        return nc.dram_tensor(name, shape, dt, kind="ExternalInput").ap()

    x_in = din("x", [S, D])
    c_col = din("c_col", [128, 8])
    w_mod = din("w_mod", [LW, D, 6 * D])
    b_mod = din("b_mod", [LW, 1, 6 * D])
    n1c = din("n1c", [LW, 128, 8])
    n2c = din("n2c", [LW, 128, 8])
    w_in = din("w_in", [LW, D, INC])
    pool_w = din("pool_w", [LW, 128, 4, 128])
    pscale = din("pscale", [LW, 128, 4])
    gains = din("gains", [LW, 128, 13, 64])
    w1 = din("w1", [LW, 64, 2 * 32 * 64])
    w2 = din("w2", [LW, 64, 2 * 64])
    peT = din("peT", [LW, 64, 2 * 32])
    w_out = din("w_out", [LW, D, D])
    w_ff1 = din("w_ff1", [LW, D, DFF])
    w_ff2 = din("w_ff2", [LW, DFF, D])
    cd = {name: din("k_" + name, shape) for (name, shape, _, _) in _CONST_SPECS}
    y_out = nc.dram_tensor("y", [S, D], F32, kind="ExternalOutput").ap()
    okind = dict(kind="ExternalOutput") if debug else {}
    hT_d = nc.dram_tensor("hT_d", [128, 8, S], BF16, **okind).ap()
    mixT_d = nc.dram_tensor("mixT_d", [128, 8, S], BF16, **okind).ap()
    xd = nc.dram_tensor("xd", [S, D], F32).ap()

    P = Prog(nc)
    ARENA_BYTES = 196 * 1024
    big_cm = nc.sbuf_tensor("arena", [128, ARENA_BYTES // 4], F32)
    big = big_cm.__enter__()
    A = Arena(big, ARENA_BYTES)
    ps_cms = [nc.psum_tensor("ps%d" % i, [128, 512], F32) for i in range(8)]
    ps = [cm.__enter__() for cm in ps_cms]

    def psv(i, shape, dtype=F32, parts=128):
        v = ps[i][0:parts, :]
        if dtype != F32:
            v = v.bitcast(dtype)
        n = int(np.prod(shape))
        v = v[:, 0:n]
        if len(shape) == 2:
            return v.rearrange("p (a b) -> p a b", b=shape[1])
        if len(shape) == 3:
            return v.rearrange("p (a b c) -> p a b c", b=shape[1], c=shape[2])
        return v

    K = {}
    for (name, shape, dt, parts) in _CONST_SPECS:
        if name in ("expand", "cmask", "cos", "sin", "ccos", "csin", "bands", "keepB", "addB", "ov", "tri", "triinv"):
            continue
        K[name] = A.alloc(shape[1:], dt, parts)
    ident = K["ident"]
    ones = K["ones"]
    cact = A.alloc([8], F32)
    modcols = A.alloc([4, 8], F32)
    s1c = A.alloc([8], F32)
    s2c = A.alloc([8], F32)
    n1t = A.alloc([8], F32)
    n2t = A.alloc([8], F32)
    gate1 = A.alloc([D], F32)
    gate2 = A.alloc([D], F32)
    small = A.alloc([64], F32)
    junk = A.alloc([D], BF16)

    def load_const(name, eng="pool"):
        for (nm, shape, dt, parts) in _CONST_SPECS:
            if nm == name:
                src = cd[name]
                dst = K[name]
                P.dma(eng if dt != F32 else "sp", dst, src, writes=["K_" + name])

    for nm in K:
        load_const(nm)
    P.dma("sp", cact, c_col, writes=["cact"])
    P.op("act", lambda E: E.activation(out=small[:, 0:8], in_=cact, func=ACTF.Exp, scale=-1.0), reads=["cact"], writes=["small"])
    P.op("dve", lambda E: E.tensor_scalar_add(out=small[:, 0:8], in0=small[:, 0:8], scalar1=1.0), reads=["small"], writes=["small"])
    P.op("dve", lambda E: E.reciprocal(out=small[:, 0:8], in_=small[:, 0:8]), reads=["small"], writes=["small"])
    P.op("dve", lambda E: E.tensor_tensor(out=cact, in0=cact, in1=small[:, 0:8], op=ALU.mult), reads=["small", "cact"], writes=["cact"])

    def rstd_from_ss(ss_ap, n, key, scale):
        P.op("act", lambda E: E.activation(out=ss_ap, in_=ss_ap, func=ACTF.Ln, scale=scale, bias=EPS), reads=[key], writes=[key])
        P.op("act", lambda E: E.activation(out=ss_ap, in_=ss_ap, func=ACTF.Exp, scale=-0.5), reads=[key], writes=[key])

    def norm_to_hT(xt, xkey, hT, hkey, scol, bcol, colkeys, xh, tag):
        ss = small[:, 32:33]
        P.op("act", lambda E: E.activation(out=junk, in_=xt, func=ACTF.Square, accum_out=ss), reads=[xkey], writes=["junk", "ss"])
        rstd_from_ss(ss, 1, "ss", 1.0 / D)
        P.op("dve", lambda E: E.tensor_scalar(out=xh, in0=xt, scalar1=ss, scalar2=None, op0=ALU.mult), reads=[xkey, "ss"], writes=["xh" + tag])
        pT = psv(2, [8, 128], BF16)
        for kc in range(8):
            P.op("pe", lambda E, kc=kc: E.transpose(out=pT[:, kc, :], in_=xh[:, kc * 128:(kc + 1) * 128], identity=ident),
                 reads=["xh" + tag, "K_ident"], writes=["ps2"])
        for kc in range(8):
            P.op("act", lambda E, kc=kc: E.activation(out=hT[:, kc, :], in_=pT[:, kc, :], func=ACTF.Identity,
                                                     scale=scol[:, kc:kc + 1], bias=bcol[:, kc:kc + 1]),
                 reads=["ps2"] + colkeys, writes=[hkey])

    for l in range(n_layers):
        x_src = x_in if l == 0 else xd
        x_dst = y_out if l == n_layers - 1 else xd

        P.barrier()
        A.mark()
        modrow = A.alloc([6 * D], F32, parts=1)
        bmrow = A.alloc([6 * D], F32, parts=1)
        wm = [A.alloc([8, 512], F32) for _ in range(2)]
        P.dma("sp", bmrow, b_mod[l], writes=["bmrow"])
        P.dma("sp", n1t, n1c[l], writes=["n1t"])
        P.dma("sp", n2t, n2c[l], writes=["n2t"])
        for ch in range(12):
            buf = wm[ch % 2]
            P.dma("sp", buf, w_mod[l][:, ch * 512:(ch + 1) * 512].rearrange("(k p) n -> p k n", p=128), writes=["wm%d" % (ch % 2)])
            pr = ps[ch % 2][0:1, :]
            for kc in range(8):
                P.op("pe", lambda E, kc=kc, buf=buf, pr=pr: E.matmul(pr, lhsT=cact[:, kc:kc + 1], rhs=buf[:, kc, :], start=(kc == 0), stop=(kc == 7)),
                     reads=["cact", "wm%d" % (ch % 2)], writes=["ps%d" % (ch % 2)])
            P.op("dve", lambda E, ch=ch, pr=pr: E.tensor_tensor(out=modrow[:, ch * 512:(ch + 1) * 512], in0=pr, in1=bmrow[:, ch * 512:(ch + 1) * 512], op=ALU.add),
                 reads=["ps%d" % (ch % 2), "bmrow"], writes=["modrow"])
        pc = ps[2][:, 0:32]
        for vi, off in enumerate((0, 1024, 3072, 4096)):
            for kc in range(8):
                j = vi * 8 + kc
                P.op("pe", lambda E, j=j, off=off, kc=kc: E.matmul(pc[:, j:j + 1], lhsT=modrow[0:1, off + kc * 128: off + (kc + 1) * 128],
                                                                  rhs=ones[0:1, 0:1], start=True, stop=True),
                     reads=["modrow", "K_ones"], writes=["ps2"])
        P.op("dve", lambda E: E.tensor_copy(out=modcols.rearrange("p a b -> p (a b)"), in_=pc), reads=["ps2"], writes=["modcols"])
        P.op("dve", lambda E: E.scalar_tensor_tensor(out=s1c, in0=modcols[:, 1, :], scalar=1.0, in1=n1t, op0=ALU.add, op1=ALU.mult),
             reads=["modcols", "n1t"], writes=["s1c"])
        P.op("dve", lambda E: E.scalar_tensor_tensor(out=s2c, in0=modcols[:, 3, :], scalar=1.0, in1=n2t, op0=ALU.add, op1=ALU.mult),
             reads=["modcols", "n2t"], writes=["s2c"])
        for gi, (gt, off) in enumerate(((gate1, 2048), (gate2, 5120))):
            for h in range(2):
                pb = ps[3 + h]
                P.op("pe", lambda E, off=off, h=h, pb=pb: E.matmul(pb[:, :], lhsT=ones[0:1, :], rhs=modrow[0:1, off + h * 512: off + (h + 1) * 512], start=True, stop=True),
                     reads=["modrow", "K_ones"], writes=["ps%d" % (3 + h)])
                P.op("dve", lambda E, gt=gt, h=h, pb=pb: E.tensor_copy(out=gt[:, h * 512:(h + 1) * 512], in_=pb[:, :]),
                     reads=["ps%d" % (3 + h)], writes=["gate%d" % gi])
        b1c = modcols[:, 0, :]
        b2c = modcols[:, 2, :]
        P.barrier()
        A.release()
        if stop == "mod":
            break

        if True:
            A.mark()
            xts = [A.alloc([D], F32) for _ in range(2)]
            xhs = [A.alloc([D], BF16) for _ in range(2)]
            hTs = [A.alloc([8, 128], BF16) for _ in range(2)]
            for t in range(nt):
                b = t % 2
                P.dma("sp", xts[b], x_src[t * 128:(t + 1) * 128, :], reads=["x_d%d" % t], writes=["xt%d" % b])
                norm_to_hT(xts[b], "xt%d" % b, hTs[b], "hT%d" % b, s1c, b1c, ["s1c", "modcols"], xhs[b], "n%d" % b)
                P.dma("sp", hT_d[:, :, t * 128:(t + 1) * 128], hTs[b], reads=["hT%d" % b], writes=["hT_d%d" % t])
            P.barrier()
            A.release()
        if stop == "norm":
            break

        A.mark()
        mixer_phase(nc, P, A, ps, psv, l, nt, K, cd, dict(
            w_in=w_in, pool_w=pool_w, pscale=pscale, gains=gains, w1=w1, w2=w2, peT=peT,
            hT_d=hT_d, mixT_d=mixT_d, ident=ident, ones=ones, small=small, mstage=mstage))
        P.barrier()
        A.release()
        if stop == "mixer":
            break

        A.mark()
        wo = A.alloc([8, D], BF16)
        f1 = A.alloc([8, DFF], BF16)
        f2 = A.alloc([32, D], BF16)
        for kc in range(8):
            P.dma("pool", wo[:, kc, :], w_out[l][kc * 128:(kc + 1) * 128, :], writes=["wo"])
        for kc in range(8):
            for hh in range(2):
                P.dma("pool", f1[:, kc, hh * 2048:(hh + 1) * 2048], w_ff1[l][kc * 128:(kc + 1) * 128, hh * 2048:(hh + 1) * 2048], writes=["f1"])
        for c4 in range(8):
            P.dma("pool", f2[:, c4 * 4:(c4 + 1) * 4, :], w_ff2[l][c4 * 512:(c4 + 1) * 512, :].rearrange("(c p) n -> p c n", p=128), writes=["f2"])
        mts = [A.alloc([8, 128], BF16) for _ in range(2)]
        xts = [A.alloc([D], F32) for _ in range(2)]
        xh = A.alloc([D], BF16)
        h2T = A.alloc([8, 128], BF16)
        aT = A.alloc([32, 128], BF16)
        rl = [A.alloc([512], F32) for _ in range(2)]
        tmp = A.alloc([512], F32)
        for t in range(nt):
            b = t % 2
            xt = xts[b]
            xk = "xt%d" % b
            P.dma("sp", mts[b], mixT_d[:, :, t * 128:(t + 1) * 128], reads=["mixT_d%d" % t], writes=["mt%d" % b])
            P.dma("sp", xt, x_src[t * 128:(t + 1) * 128, :], reads=["x_d%d" % t], writes=[xk])
            for h in range(2):
                for kc in range(8):
                    P.op("pe", lambda E, h=h, kc=kc, b=b: E.matmul(ps[h][:, :], lhsT=mts[b][:, kc, :], rhs=wo[:, kc, h * 512:(h + 1) * 512], start=(kc == 0), stop=(kc == 7)),
                         reads=["mt%d" % b, "wo"], writes=["ps%d" % h])
                P.op("dve", lambda E, h=h: E.tensor_tensor(out=tmp, in0=ps[h][:, :], in1=gate1[:, h * 512:(h + 1) * 512], op=ALU.mult),
                     reads=["ps%d" % h, "gate0"], writes=["tmp"])
                P.op("dve", lambda E, h=h, xt=xt: E.tensor_tensor(out=xt[:, h * 512:(h + 1) * 512], in0=xt[:, h * 512:(h + 1) * 512], in1=tmp, op=ALU.add),
                     reads=["tmp", xk], writes=[xk])
            norm_to_hT(xt, xk, h2T, "h2T", s2c, b2c, ["s2c", "modcols"], xh, "f")
            for c4 in range(8):
                pf = ps[3 + (c4 % 2)]
                for cc in range(4):
                    c = c4 * 4 + cc
                    for kc in range(8):
                        P.op("pe", lambda E, c=c, cc=cc, kc=kc, pf=pf: E.matmul(pf[:, cc * 128:(cc + 1) * 128], lhsT=f1[:, kc, c * 128:(c + 1) * 128], rhs=h2T[:, kc, :],
                                                                               start=(kc == 0 and cc == 0), stop=(kc == 7 and cc == 3), skip_group_check=True),
                             reads=["h2T", "f1"], writes=["ps%d" % (3 + c4 % 2)])
                r = rl[c4 % 2]
                P.op("act", lambda E, pf=pf, r=r: E.activation(out=r, in_=pf[:, :], func=ACTF.Relu), reads=["ps%d" % (3 + c4 % 2)], writes=["rl%d" % (c4 % 2)])
                P.op("pool", lambda E, r=r, c4=c4: E.tensor_tensor(out=aT[:, c4 * 4:(c4 + 1) * 4, :].rearrange("p a b -> p (a b)"), in0=r, in1=r, op=ALU.mult),
                     reads=["rl%d" % (c4 % 2)], writes=["aT%d" % c4])
            for h in range(2):
                for c in range(32):
                    P.op("pe", lambda E, h=h, c=c: E.matmul(ps[h][:, :], lhsT=aT[:, c, :], rhs=f2[:, c, h * 512:(h + 1) * 512], start=(c == 0), stop=(c == 31)),
                         reads=["aT%d" % (c // 4), "f2"], writes=["ps%d" % h])
                P.op("dve", lambda E, h=h: E.tensor_tensor(out=tmp, in0=ps[h][:, :], in1=gate2[:, h * 512:(h + 1) * 512], op=ALU.mult),
                     reads=["ps%d" % h, "gate1"], writes=["tmp"])
                P.op("dve", lambda E, h=h, xt=xt: E.tensor_tensor(out=xt[:, h * 512:(h + 1) * 512], in0=xt[:, h * 512:(h + 1) * 512], in1=tmp, op=ALU.add),
                     reads=["tmp", xk], writes=[xk])
            P.dma("sp", x_dst[t * 128:(t + 1) * 128, :], xt, reads=[xk], writes=["x_d%d" % t])
            if l < n_layers - 1:
                pass
        P.barrier()
        A.release()
        if l < n_layers - 1:
            pass

    P.barrier()
    P.emit()
    P.close()
    for cm in reversed(ps_cms):
        cm.__exit__(None, None, None)
    big_cm.__exit__(None, None, None)
    return nc


def mixer_phase(nc, P, A, ps, psv, l, nt, K, cd, W):
    ident = W["ident"]
    hT_d, mixT_d = W["hT_d"], W["mixT_d"]

    def cp(eng, out, in_, reads, writes):
        if eng == "act":
            P.op("act", lambda E: E.activation(out=out, in_=in_, func=ACTF.Identity), reads, writes)
        else:
            P.op(eng, lambda E: E.tensor_copy(out=out, in_=in_), reads, writes)

    def tt(eng, out, a, b, op, reads, writes):
        P.op(eng, lambda E: E.tensor_tensor(out=out, in0=a, in1=b, op=op), reads, writes)

    tri = A.alloc([128], BF16)
    triinv = A.alloc([128], BF16)
    cmask = A.alloc([16, 128], BF16)
    expand = A.alloc([S], BF16, parts=64)
    keepB = A.alloc([128], F32)
    addB = A.alloc([128], F32)
    ov = A.alloc([2, 64], BF16)
    bands = A.alloc([3, 4, 128], F32)
    P.dma("pool", tri, cd["tri"], writes=["tri"])
    P.dma("pool", triinv, cd["triinv"], writes=["triinv"])
    P.dma("pool", cmask, cd["cmask"], writes=["cmask"])
    for hh in range(2):
        P.dma("pool", expand[:, hh * 2048:(hh + 1) * 2048], cd["expand"][:, hh * 2048:(hh + 1) * 2048], writes=["expand"])
    P.dma("sp", keepB, cd["keepB"], writes=["keepB"])
    P.dma("sp", addB, cd["addB"], writes=["addB"])
    P.dma("pool", ov, cd["ov"], writes=["ov"])
    P.dma("sp", bands, cd["bands"], writes=["bands"])
    win = A.alloc([8, INC], BF16)
    for kc in range(8):
        P.dma("pool", win[:, kc, :], W["w_in"][l][kc * 128:(kc + 1) * 128, :], writes=["win"])
    pw = A.alloc([4, 128], BF16)
    P.dma("pool", pw, W["pool_w"][l], writes=["pw"])
    psc = A.alloc([4], F32)
    P.dma("sp", psc, W["pscale"][l], writes=["psc"])
    gn = A.alloc([13, 64], F32)
    P.dma("sp", gn, W["gains"][l], writes=["gn"])
    W1 = A.alloc([2, 32, 64], BF16, parts=64)
    for kv in range(2):
        P.dma("pool", W1[:, kv, :, :].rearrange("p a b -> p (a b)"), W["w1"][l][:, kv * 2048:(kv + 1) * 2048], writes=["W1"])
    W2 = A.alloc([2, 64], BF16, parts=64)
    P.dma("pool", W2.rearrange("p a b -> p (a b)"), W["w2"][l], writes=["W2"])
    peT = A.alloc([2, 32], BF16, parts=64)
    P.dma("pool", peT.rearrange("p a b -> p (a b)"), W["peT"][l], writes=["peT"])
    cbias = A.alloc([2], F32, parts=64)
    pb = ps[0][0:64, 0:2]
    for kv in range(2):
        for p in range(32):
            P.op("pe", lambda E, kv=kv, p=p: E.matmul(pb[:, kv:kv + 1], lhsT=W1[:, kv, p, :], rhs=peT[:, kv, p:p + 1], start=(p == 0), stop=(p == 31)),
                 reads=["W1", "peT"], writes=["ps0"])
    cp("dve", cbias, pb, ["ps0"], ["cbias"])
    kT = [A.alloc([2, S], BF16, parts=64) for _ in range(2)]
    craw = [A.alloc([2, 144], BF16, parts=64) for _ in range(2)]
    Vall = A.alloc([2 * nt * 2, 66], BF16).rearrange("p (g t k) c -> p g t k c", g=2, k=2)
    kTc = [A.alloc([256], BF16, parts=64) for _ in range(2)]
    Vc = [A.alloc([2, 66], BF16) for _ in range(2)]
    P.op("pool", lambda E: E.memset(Vall, 1.0), writes=["V"])
    for g in range(2):
        P.op("pool", lambda E, g=g: E.memset(Vc[g], 0.0), writes=["Vc%d" % g])
        P.op("pool", lambda E, g=g: E.memset(Vc[g][:, :, 64:65], 1.0), writes=["Vc%d" % g])
        P.op("pool", lambda E, g=g: E.memset(Vc[g][0:1, 0, 64:65], 0.0), writes=["Vc%d" % g])
        P.op("pool", lambda E, g=g: E.memset(kTc[g], 0.0), writes=["kTc%d" % g])
        P.op("pool", lambda E, g=g: E.memset(craw[g], 0.0), writes=["craw%d" % g])
    hTt = [A.alloc([8, 128], BF16) for _ in range(2)]
    cs = [A.alloc([2, 32], F32) for _ in range(2)]
    ccs = [A.alloc([2, 32], F32, parts=8) for _ in range(2)]
    u = [A.alloc([512], F32) for _ in range(2)]
    pj = A.alloc([2 * GW], F32)
    sq = A.alloc([12, 64], F32)
    st = A.alloc([12], F32)
    xn = A.alloc([12, 64], F32)
    ra = A.alloc([12, 32], F32)
    rb = A.alloc([12, 32], F32)
    rc = A.alloc([12, 32], F32)
    rd = A.alloc([12, 32], F32)
    tball = A.alloc([2, 9, 64], BF16)
    sgall = A.alloc([24], F32)
    mixt = A.alloc([8, 128], BF16)
    pooledT = A.alloc([512], BF16)
    P.op("pool", lambda E: E.memset(tball, 0.0), writes=["tb"])

    class GB:
        pass
    G = []
    for g in range(2):
        o = GB()
        o.qT = A.alloc([512], BF16, parts=64)
        o.Eb = [A.alloc([512], BF16) for _ in range(2)]
        o.msk = A.alloc([128], BF16)
        o.oall = A.alloc([3, 264], F32)
        o.rden = A.alloc([12], F32)
        o.coef = A.alloc([12], F32)
        o.imp = A.alloc([64], F32)
        o.imp2 = A.alloc([64], F32)
        o.wk = A.alloc([64], F32)
        o.m8 = A.alloc([16], F32)
        o.thr = A.alloc([1], F32)
        o.selb = A.alloc([128], BF16)
        o.selT = A.alloc([128], BF16, parts=64)
        o.y32 = A.alloc([256], F32)
        o.ybf = A.alloc([256], BF16)
        o.zc = A.alloc([16], F32)
        o.ec = A.alloc([16], F32)
        o.sTc = A.alloc([16], BF16, parts=64)
        o.k8 = A.alloc([64], F32, parts=8)
        o.k8q = A.alloc([64], F32, parts=8)
        o.k8s = A.alloc([4], F32)
        o.k8r = [A.alloc([32], F32, parts=8) for _ in range(4)]
        o.k8b = A.alloc([128], BF16)
        o.v8 = A.alloc([64], BF16, parts=8)
        P.op("pool", lambda E, o=o: E.memset(o.zc, 0.0), writes=["zc%d" % g])
        P.op("pool", lambda E, o=o: E.memset(o.k8s, 1.0), writes=["k8s%d" % g])
        P.op("pool", lambda E, o=o: E.memset(o.selb, 0.0), writes=["selb%d" % g])
        P.op("pool", lambda E, o=o: E.memset(o.k8b, 0.0), writes=["k8b%d" % g])
        G.append(o)

    def bc_h(ap128):
        return ap128.unsqueeze(1).to_broadcast([128, 4, 128])

    def e4(buf):
        return buf.rearrange("p (h q) -> p h q", h=4)

    def attention_branch(g, t, br, kts, kTsrc, vsrc, vkey, masks, use_sel):
        o = G[g]
        B0 = 4 * g
        sbank, obank, ibank, mbank = B0, B0 + 1, B0 + 2, B0 + 3
        psO = ps[obank]
        psI = ps[ibank]
        nk = len(kts)

        def scores(i):
            kt = kts[i]
            P.op("pe", lambda E: E.matmul(ps[sbank][:, :], lhsT=kTsrc(kt), rhs=o.qT, start=True, stop=True),
                 reads=["kcache%d" % g, "kTc%d" % g, "qT%d" % g], writes=["ps%d" % sbank])
            if use_sel:
                slot = ps[mbank][:, (i % 2) * 128:(i % 2 + 1) * 128]
                P.op("pe", lambda E: E.matmul(slot, lhsT=expand[:, kt * 128:(kt + 1) * 128], rhs=o.selT, start=True, stop=True),
                     reads=["expand", "selT%d" % g], writes=["ps%d" % mbank])
        scores(0)
        yield
        for i, kt in enumerate(kts):
            E_ = o.Eb[i % 2]
            ek = "Eb%d_%d" % (g, i % 2)
            P.op("act", lambda E, E_=E_: E.activation(out=E_, in_=ps[sbank][:, :], func=ACTF.Exp, scale=0.125),
                 reads=["ps%d" % sbank], writes=[ek])
            m = masks(kt)
            if use_sel:
                slot = ps[mbank][:, (i % 2) * 128:(i % 2 + 1) * 128]
                sk = "ps%d" % mbank
                if m is not None:
                    tt("dve", o.msk, slot, m[0], ALU.mult, [sk, m[1]], ["msk%d" % g])
                    tt("pool", e4(E_), e4(E_), bc_h(o.msk), ALU.mult, [ek, "msk%d" % g], [ek])
                else:
                    tt("dve", e4(E_), e4(E_), bc_h(slot), ALU.mult, [ek, sk], [ek])
            elif m is not None:
                tt("pool", e4(E_), e4(E_), bc_h(m[0]), ALU.mult, [ek, m[1]], [ek])
            if i + 1 < nk:
                scores(i + 1)
            yield
            for h in range(4):
                first = (i == 0 and h == 0)
                last = (i == nk - 1 and h == 3)
                P.op("pe", lambda E, h=h, kt=kt, E_=E_, first=first, last=last: E.matmul(
                    psO[:, h * 66:(h + 1) * 66], lhsT=E_[:, h * 128:(h + 1) * 128], rhs=vsrc(kt), start=first, stop=last, skip_group_check=True),
                    reads=[ek, vkey], writes=["ps%d" % obank])
                if br == 0:
                    P.op("pe", lambda E, h=h, kt=kt, E_=E_, first=first, last=last: E.matmul(
                        psI[:, h * 64:(h + 1) * 64], lhsT=E_[:, h * 128:(h + 1) * 128], rhs=ov[:, kt, :], start=first, stop=last, skip_group_check=True),
                        reads=[ek, "ov"], writes=["ps%d" % ibank])
            yield
        cp("act", o.oall[:, br, :], psO[:, 0:264], ["ps%d" % obank], ["oall%d_%d" % (g, br)])
        P.op("dve", lambda E: E.tensor_scalar_max(out=o.rden[:, br * 4:(br + 1) * 4], in0=o.oall[:, br, :].rearrange("p (h c) -> p h c", c=66)[:, :, 64], scalar1=1e-30),
             reads=["oall%d_%d" % (g, br)], writes=["rden%d_%d" % (g, br)])
        P.op("dve", lambda E: E.reciprocal(out=o.rden[:, br * 4:(br + 1) * 4], in_=o.rden[:, br * 4:(br + 1) * 4]),
             reads=["rden%d_%d" % (g, br)], writes=["rden%d_%d" % (g, br)])
        yield

    def group_stream(g, t, b):
        o = G[g]
        B0 = 4 * g
        sbank, obank, ibank, mbank = B0, B0 + 1, B0 + 2, B0 + 3
        ts = slice(t * 128, (t + 1) * 128)
        tbg = tball[:, g, :, :]
        pT128 = psv(sbank, [8, 128], BF16)
        pT = pT128[0:64]
        pT2 = psv(sbank, [2, 128], BF16)
        sbk = "ps%d" % sbank
        for s_ in range(8):
            P.op("pe", lambda E, s_=s_: E.transpose(out=pT128[:, s_, :], in_=tbg[:, s_:s_ + 2, :].rearrange("p a b -> p (a b)"), identity=ident),
                 reads=["tb", "K_ident"], writes=[sbk])
        cp("dve", o.qT.rearrange("p (a b) -> p a b", a=4), pT[:, 0:4, :], [sbk], ["qT%d" % g])
        cp("dve", kT[g][:, :, ts], pT[:, 4:6, :], [sbk], ["kcache%d" % g])
        if t > 0:
            cp("dve", craw[g][:, :, 0:16], craw[g][:, :, 128:144], ["craw%d" % g], ["craw%d" % g])
        cp("dve", craw[g][:, :, 16:144], pT[:, 6:8, :], [sbk], ["craw%d" % g])
        yield
        ibk = "ps%d" % ibank
        mbk = "ps%d" % mbank
        preT = ps[ibank][0:64, 272:288]
        for kv in range(2):
            for p in range(32):
                P.op("pe", lambda E, kv=kv, p=p: E.matmul(preT[:, kv * 8:(kv + 1) * 8], lhsT=W1[:, kv, p, :], rhs=craw[g][:, kv, p:p + 113:16],
                                                          start=(p == 0), stop=(p == 31), skip_group_check=True),
                     reads=["W1", "craw%d" % g], writes=[ibk])
        yield
        zk, ekk = "zc%d" % g, "ec%d" % g
        for kv in range(2):
            P.op("dve", lambda E, kv=kv: E.tensor_scalar(out=o.zc[0:64, kv * 8:(kv + 1) * 8], in0=preT[:, kv * 8:(kv + 1) * 8], scalar1=cbias[:, kv:kv + 1],
                                                         scalar2=None, op0=ALU.add), reads=[ibk, "cbias"], writes=[zk])
        P.op("act", lambda E: E.activation(out=o.ec, in_=o.zc, func=ACTF.Exp, scale=-1.0), reads=[zk], writes=[ekk])
        yield
        P.op("dve", lambda E: E.tensor_scalar_add(out=o.ec[0:64], in0=o.ec[0:64], scalar1=1.0), reads=[ekk], writes=[ekk])
        P.op("dve", lambda E: E.reciprocal(out=o.ec[0:64], in_=o.ec[0:64]), reads=[ekk], writes=[ekk])
        tt("dve", o.sTc, o.zc[0:64], o.ec[0:64], ALU.mult, [zk, ekk], ["sTc%d" % g])
        yield
        k8p = ps[mbank][0:8, 256:384]
        for kv in range(2):
            P.op("pe", lambda E, kv=kv: E.matmul(k8p[:, kv * 64:(kv + 1) * 64], lhsT=o.sTc[:, kv * 8:(kv + 1) * 8], rhs=W2[:, kv, :], start=True, stop=True),
                 reads=["sTc%d" % g, "W2"], writes=[mbk])
        yield
        cp("dve", o.v8, k8p[:, 64:128], [mbk], ["v8%d" % g])
        if t == 0:
            P.op("pool", lambda E: E.memset(o.v8[0:1, :], 0.0), reads=[], writes=["v8%d" % g])
        r0 = 8 * (t % 16)
        P.dma("sp", Vc[g][r0:r0 + 8, t // 16, 0:64], o.v8, reads=["v8%d" % g], writes=["Vc%d" % g])
        k8k = "k8%d" % g
        cp("dve", o.k8, k8p[:, 0:64], [mbk], [k8k])
        tt("dve", o.k8q, o.k8, o.k8, ALU.mult, [k8k], ["k8q%d" % g])
        P.op("dve", lambda E: E.tensor_reduce(out=o.k8s[0:8, 0:1], in_=o.k8q, axis=AX.X, op=ALU.add), reads=["k8q%d" % g], writes=["k8s%d" % g])
        yield
        P.op("act", lambda E: E.activation(out=o.k8s[:, 0:1], in_=o.k8s[:, 0:1], func=ACTF.Ln, scale=1.0 / 64, bias=EPS), reads=["k8s%d" % g], writes=["k8s%d" % g])
        P.op("act", lambda E: E.activation(out=o.k8s[:, 0:1], in_=o.k8s[:, 0:1], func=ACTF.Exp, scale=-0.5), reads=["k8s%d" % g], writes=["k8s%d" % g])
        yield
        P.op("dve", lambda E: E.tensor_scalar(out=o.k8, in0=o.k8, scalar1=o.k8s[0:8, 0:1], scalar2=None, op0=ALU.mult), reads=[k8k, "k8s%d" % g], writes=[k8k])
        tt("dve", o.k8, o.k8, gn[0:8, 12, :], ALU.mult, [k8k, "gn"], [k8k])
        yield
        cck = "ccs%d" % b
        cc_, ss_ = ccs[b][:, 0, :], ccs[b][:, 1, :]
        kr = ["k8r%d_%d" % (g, i) for i in range(4)]
        tt("dve", o.k8r[0], o.k8[:, 0:32], cc_, ALU.mult, [k8k, cck], [kr[0]])
        tt("pool", o.k8r[1], o.k8[:, 32:64], ss_, ALU.mult, [k8k, cck], [kr[1]])
        tt("dve", o.k8r[2], o.k8[:, 32:64], cc_, ALU.mult, [k8k, cck], [kr[2]])
        tt("pool", o.k8r[3], o.k8[:, 0:32], ss_, ALU.mult, [k8k, cck], [kr[3]])
        yield
        tt("dve", o.k8b[0:8, 0:32], o.k8r[0], o.k8r[1], ALU.subtract, [kr[0], kr[1]], ["k8b%d" % g])
        tt("dve", o.k8b[0:8, 32:64], o.k8r[2], o.k8r[3], ALU.add, [kr[2], kr[3]], ["k8b%d" % g])
        yield
        pt8 = ps[ibank][0:64, 256:264]
        P.op("pe", lambda E: E.matmul(pt8, lhsT=o.k8b[0:8, 0:64], rhs=ident[0:8, 0:8], start=True, stop=True), reads=["k8b%d" % g, "K_ident"], writes=[ibk])
        cp("dve", kTc[g][:, 8 * t:8 * t + 8], pt8, [ibk], ["kTc%d" % g])
        yield
        kts_c = [0] if t < 16 else [0, 1]
        yield from attention_branch(g, t, 0, kts_c, lambda kt: kTc[g][:, kt * 128:(kt + 1) * 128], lambda kt: Vc[g][:, kt, :], "Vc%d" % g,
                                    lambda kt: ((cmask[:, t % 16, :], "cmask") if kt == t // 16 else None), False)
        psI = ps[ibank]
        rk = "rden%d_0" % g
        P.op("dve", lambda E: E.tensor_scalar(out=o.imp, in0=psI[:, 0:64], scalar1=o.rden[:, 0:1], scalar2=None, op0=ALU.mult), reads=[ibk, rk], writes=["imp%d" % g])
        for h in range(1, 4):
            P.op("dve", lambda E, h=h: E.scalar_tensor_tensor(out=o.imp, in0=psI[:, h * 64:(h + 1) * 64], scalar=o.rden[:, h:h + 1], in1=o.imp, op0=ALU.mult, op1=ALU.add),
                 reads=[ibk, rk, "imp%d" % g], writes=["imp%d" % g])
            if h == 2:
                yield
        j0 = 64 - 2 * t
        i2 = "imp2%d" % g
        tt("dve", o.imp2, o.imp, keepB[:, j0:j0 + 64], ALU.mult, ["imp%d" % g, "keepB"], [i2])
        tt("dve", o.imp2, o.imp2, addB[:, j0:j0 + 64], ALU.add, [i2, "addB"], [i2])
        yield
        P.op("dve", lambda E: E.memset(o.imp2[:, 0:1], 1e4), reads=[], writes=[i2])
        P.op("dve", lambda E: E.max(out=o.m8[:, 0:8], in_=o.imp2), reads=[i2], writes=["m8%d" % g])
        yield
        P.op("dve", lambda E: E.match_replace(out=o.wk, in_to_replace=o.m8[:, 0:8], in_values=o.imp2, imm_value=-2.0), reads=[i2, "m8%d" % g], writes=["wk%d" % g])
        P.op("dve", lambda E: E.max(out=o.m8[:, 8:16], in_=o.wk), reads=["wk%d" % g], writes=["m8%d" % g])
        yield
        P.op("dve", lambda E: E.tensor_scalar_max(out=o.thr, in0=o.m8[:, 15:16], scalar1=0.0), reads=["m8%d" % g], writes=["thr%d" % g])
        P.op("dve", lambda E: E.tensor_scalar(out=o.selb[:, 0:64], in0=o.imp2, scalar1=o.thr[:, 0:1], scalar2=None, op0=ALU.is_ge), reads=[i2, "thr%d" % g], writes=["selb%d" % g])
        yield
        psel = ps[mbank][0:64, 384:512]
        P.op("pe", lambda E: E.matmul(psel, lhsT=o.selb[:, 0:64], rhs=ident, start=True, stop=True), reads=["selb%d" % g, "K_ident"], writes=[mbk])
        cp("dve", o.selT, psel, [mbk], ["selT%d" % g])
        yield
        yield from attention_branch(g, t, 1, list(range(t + 1)), lambda kt: kT[g][:, 0, kt * 128:(kt + 1) * 128], lambda kt: Vall[:, g, kt, 0, :], "V",
                                    lambda kt: ((tri, "tri") if kt == t else None), True)
        yield from attention_branch(g, t, 2, list(range(max(0, t - 4), t + 1)), lambda kt: kT[g][:, 1, kt * 128:(kt + 1) * 128], lambda kt: Vall[:, g, kt, 1, :], "V",
                                    lambda kt: ((tri, "tri") if kt == t else ((triinv, "triinv") if kt == t - 4 else None)), False)
        ck = "coef%d" % g
        tt("dve", o.coef.rearrange("p (b h) -> p b h", h=4), sgall[:, g * 12:(g + 1) * 12].rearrange("p (h b) -> p b h", b=3), o.rden.rearrange("p (b h) -> p b h", h=4), ALU.mult,
           ["sg", "rden%d_0" % g, "rden%d_1" % g, "rden%d_2" % g], [ck])
        yield
        yk = "y32%d" % g
        for h in range(4):
            ysl = o.y32[:, h * 64:(h + 1) * 64]
            P.op("dve", lambda E, h=h, ysl=ysl: E.tensor_scalar(out=ysl, in0=o.oall[:, 0, h * 66:h * 66 + 64], scalar1=o.coef[:, h:h + 1], scalar2=None, op0=ALU.mult),
                 reads=["oall%d_0" % g, ck], writes=[yk + "_%d" % h])
        yield
        for h in range(4):
            ysl = o.y32[:, h * 64:(h + 1) * 64]
            P.op("dve", lambda E, h=h, ysl=ysl: E.scalar_tensor_tensor(out=ysl, in0=o.oall[:, 1, h * 66:h * 66 + 64], scalar=o.coef[:, 4 + h:5 + h], in1=ysl, op0=ALU.mult, op1=ALU.add),
                 reads=["oall%d_1" % g, ck, yk + "_%d" % h], writes=[yk + "_%d" % h])
        yield
        for h in range(4):
            ysl = o.y32[:, h * 64:(h + 1) * 64]
            P.op("dve", lambda E, h=h, ysl=ysl: E.scalar_tensor_tensor(out=o.ybf[:, h * 64:(h + 1) * 64], in0=o.oall[:, 2, h * 66:h * 66 + 64], scalar=o.coef[:, 8 + h:9 + h], in1=ysl, op0=ALU.mult, op1=ALU.add),
                 reads=["oall%d_2" % g, ck, yk + "_%d" % h], writes=["ybf%d" % g])
        yield
        for c in range(2):
            P.op("pe", lambda E, c=c: E.transpose(out=pT2[:, c, :], in_=o.ybf[:, c * 128:(c + 1) * 128], identity=ident), reads=["ybf%d" % g, "K_ident"], writes=[sbk])
        cp("act", mixt[:, 4 + 2 * g:6 + 2 * g, :], pT2, [sbk], ["mixt"])
        yield

    for t in range(nt):
        b = t % 2
        ts = slice(t * 128, (t + 1) * 128)
        P.dma("sp", hTt[b], hT_d[:, :, ts], reads=["hT_d%d" % t], writes=["hTt%d" % b])
        P.dma("sp", cs[b][:, 0, :], cd["cos"][:, t, :], writes=["cs%d" % b])
        P.dma("sp", cs[b][:, 1, :], cd["sin"][:, t, :], writes=["cs%d" % b])
        P.dma("sp", ccs[b][:, 0, :], cd["ccos"][:, t, :], writes=["ccs%d" % b])
        P.dma("sp", ccs[b][:, 1, :], cd["csin"][:, t, :], writes=["ccs%d" % b])
        chunks = [(0, 512, u[b], "u%d" % b), (512, 512, pj[:, 0:512], "pj"), (1024, 512, pj[:, 512:1024], "pj"), (1536, 280, pj[:, 1024:1304], "pj")]
        for ci, (c0, wd, dst, dk) in enumerate(chunks):
            bank = 4 * (ci % 2)
            for kc in range(8):
                P.op("pe", lambda E, kc=kc, c0=c0, wd=wd, bank=bank: E.matmul(ps[bank][:, 0:wd], lhsT=hTt[b][:, kc, :], rhs=win[:, kc, c0:c0 + wd],
                                                                               start=(kc == 0), stop=(kc == 7)),
                     reads=["hTt%d" % b, "win"], writes=["ps%d" % bank])
            cp("act" if ci % 2 == 0 else "dve", dst, ps[bank][:, 0:wd], ["ps%d" % bank], [dk])
        qk = pj[:, 0:768].rearrange("p (s d) -> p s d", d=64)
        tt("dve", sq, qk, qk, ALU.mult, ["pj"], ["sq"])
        P.op("dve", lambda E: E.tensor_reduce(out=st, in_=sq, axis=AX.X, op=ALU.add), reads=["sq"], writes=["st"])
        P.op("act", lambda E: E.activation(out=st, in_=st, func=ACTF.Ln, scale=1.0 / 64, bias=EPS), reads=["st"], writes=["st"])
        P.op("act", lambda E: E.activation(out=st, in_=st, func=ACTF.Exp, scale=-0.5), reads=["st"], writes=["st"])
        cp("pool", tball[:, :, 6:8, :], pj[:, 768:1024].rearrange("p (g s d) -> p g s d", g=2, d=64), ["pj"], ["tb"])
        cp("pool", Vall[:, :, t, :, 0:64], pj[:, 1024:1280].rearrange("p (g s d) -> p g s d", g=2, d=64), ["pj"], ["V"])
        P.op("act", lambda E: E.activation(out=sgall, in_=pj[:, 1280:1304], func=ACTF.Exp, scale=-1.0), reads=["pj"], writes=["sg"])
        tt("dve", xn, qk, st.unsqueeze(2).to_broadcast([128, 12, 64]), ALU.mult, ["pj", "st"], ["xn"])
        P.op("pool", lambda E: E.tensor_scalar_add(out=sgall, in0=sgall, scalar1=1.0), reads=["sg"], writes=["sg"])
        tt("dve", xn, xn, gn[:, 0:12, :], ALU.mult, ["xn", "gn"], ["xn"])
        P.op("dve", lambda E: E.reciprocal(out=sgall, in_=sgall), reads=["sg"], writes=["sg"])
        cosb = cs[b][:, 0, :].unsqueeze(1).to_broadcast([128, 12, 32])
        sinb = cs[b][:, 1, :].unsqueeze(1).to_broadcast([128, 12, 32])
        x1 = xn[:, :, 0:32]
        x2 = xn[:, :, 32:64]
        ck = "cs%d" % b
        tt("dve", ra, x1, cosb, ALU.mult, ["xn", ck], ["ra"])
        tt("pool", rb, x2, sinb, ALU.mult, ["xn", ck], ["rb"])
        tt("dve", rc, x2, cosb, ALU.mult, ["xn", ck], ["rc"])
        tt("pool", rd, x1, sinb, ALU.mult, ["xn", ck], ["rd"])
        g4 = lambda a: a.rearrange("p (g s) d -> p g s d", g=2)
        tt("dve", tball[:, :, 0:6, 0:32], g4(ra), g4(rb), ALU.subtract, ["ra", "rb"], ["tb"])
        tt("pool", tball[:, :, 0:6, 32:64], g4(rc), g4(rd), ALU.add, ["rc", "rd"], ["tb"])
        psP = ps[1]
        for gp in range(4):
            kind = 2 if t == 0 else 0
            P.op("pe", lambda E, gp=gp, kind=kind: E.matmul(psP[:, gp * 128:(gp + 1) * 128], lhsT=u[b][:, gp * 128:(gp + 1) * 128], rhs=bands[:, kind, gp, :],
                                                            start=True, stop=(t == 0), skip_group_check=True),
                 reads=["u%d" % b, "bands"], writes=["ps1"])
            if t > 0:
                P.op("pe", lambda E, gp=gp: E.matmul(psP[:, gp * 128:(gp + 1) * 128], lhsT=u[1 - b][:, gp * 128:(gp + 1) * 128], rhs=bands[:, 1, gp, :],
                                                     start=False, stop=True, skip_group_check=True),
                     reads=["u%d" % (1 - b), "bands"], writes=["ps1"])
        cp("act", pooledT, psP[:, :], ["ps1"], ["pooledT"])
        psY = ps[5]
        for gp in range(4):
            P.op("pe", lambda E, gp=gp: E.matmul(psY[:, gp * 128:(gp + 1) * 128], lhsT=pw[:, gp, :], rhs=pooledT[:, gp * 128:(gp + 1) * 128], start=True, stop=True),
                 reads=["pw", "pooledT"], writes=["ps5"])
        for gp in range(4):
            P.op("act", lambda E, gp=gp: E.activation(out=mixt[:, gp, :], in_=psY[:, gp * 128:(gp + 1) * 128], func=ACTF.Identity, scale=psc[:, gp:gp + 1]),
                 reads=["ps5", "psc"], writes=["mixt"])
        gens = [group_stream(0, t, b), group_stream(1, t, b)]
        while gens:
            for gen in list(gens):
                try:
                    next(gen)
                except StopIteration:
                    gens.remove(gen)
        P.dma("sp", mixT_d[:, :, ts], mixt, reads=["mixt"], writes=["mixT_d%d" % t])


def _perm_in_cols():
    o0, o1 = 512, 1024
    o2 = o1 + 768

    def kvc(g, kvi, br):
        st = o1 + ((kvi * 3 + br) * 2 + g) * 64
        return list(range(st, st + 64))
    cols = list(range(512))
    for g in range(2):
        for h in range(4):
            cols += list(range(o0 + (g * 4 + h) * 64, o0 + (g * 4 + h + 1) * 64))
        cols += kvc(g, 0, 1) + kvc(g, 0, 2)
    for g in range(2):
        cols += kvc(g, 0, 0) + kvc(g, 1, 0)
    for g in range(2):
        cols += kvc(g, 1, 1) + kvc(g, 1, 2)
    cols += list(range(o2, o2 + 24))
    assert len(cols) == INC
    return np.array(cols)


_PROG_CACHE = {}


def _get_prog(n_layers, debug=False):
    key = (n_layers, debug)
    if key not in _PROG_CACHE:
        _PROG_CACHE[key] = build_program(n_layers, debug=debug)
    return _PROG_CACHE[key]


def _layer_inputs(ls, x_b, c, w_mod, b_mod, norm1, norm2, w_in, pool_w, pool_scale, q_norm, k_norm,
                  cmp_pe, cmp_w1, cmp_w2, w_out, w_ff1, w_ff2, b, consts, perm):
    f = np.float32
    ls = list(ls)
    n = len(ls)

    def col(v):
        return np.ascontiguousarray(v.reshape(8, 128).T)
    d = {}
    d["x"] = np.ascontiguousarray(x_b, dtype=f)
    d["c_col"] = col(np.asarray(c[b], f))
    d["w_mod"] = np.ascontiguousarray(w_mod[ls], dtype=f)
    d["b_mod"] = np.ascontiguousarray(b_mod[ls], dtype=f).reshape(n, 1, 6 * D)
    d["n1c"] = np.stack([col(norm1[l]) for l in ls]).astype(f)
    d["n2c"] = np.stack([col(norm2[l]) for l in ls]).astype(f)
    d["w_in"] = np.ascontiguousarray(w_in[ls][:, :, perm], dtype=f)
    d["pool_w"] = np.ascontiguousarray(np.transpose(pool_w[ls], (0, 2, 1, 3)), dtype=f)
    d["pscale"] = np.ascontiguousarray(np.transpose(pool_scale[ls].reshape(n, 4, 128), (0, 2, 1)), dtype=f)
    gains = np.zeros((n, 128, 13, 64), f)
    for i, l in enumerate(ls):
        for g in range(2):
            gains[i, :, 6 * g:6 * g + 4, :] = q_norm[l][None, None, :]
            gains[i, :, 6 * g + 4, :] = k_norm[l, 1][None, :]
            gains[i, :, 6 * g + 5, :] = k_norm[l, 2][None, :]
        gains[i, :, 12, :] = k_norm[l, 0][None, :]
    d["gains"] = gains
    w1 = cmp_w1[ls].reshape(n, 2, 32, 64, 64)
    d["w1"] = np.ascontiguousarray(np.transpose(w1, (0, 3, 1, 2, 4)).reshape(n, 64, 2 * 32 * 64), dtype=f)
    d["w2"] = np.ascontiguousarray(np.transpose(cmp_w2[ls], (0, 2, 1, 3)).reshape(n, 64, 128), dtype=f)
    d["peT"] = np.ascontiguousarray(np.transpose(cmp_pe[ls], (0, 3, 1, 2)).reshape(n, 64, 64), dtype=f)
    d["w_out"] = np.ascontiguousarray(w_out[ls], dtype=f)
    d["w_ff1"] = np.ascontiguousarray(w_ff1[ls], dtype=f)
    d["w_ff2"] = np.ascontiguousarray(w_ff2[ls], dtype=f)
    for name, shape, _, _ in _CONST_SPECS:
        d["k_" + name] = np.ascontiguousarray(consts[name], dtype=f).reshape(shape)
    return d


LAYERS_PER_LAUNCH = 4


def kernel(x, c, w_mod, b_mod, norm1, norm2, w_in, pool_w, pool_scale, q_norm, k_norm,
           cmp_pe, cmp_w1, cmp_w2, w_out, w_ff1, w_ff2):
    args = [np.asarray(a) for a in (c, w_mod, b_mod, norm1, norm2, w_in, pool_w, pool_scale, q_norm, k_norm,
                                    cmp_pe, cmp_w1, cmp_w2, w_out, w_ff1, w_ff2)]
    x = np.asarray(x, np.float32)
    consts = _consts()
    perm = _perm_in_cols()
    nc = _get_prog(LAYERS_PER_LAUNCH)
    xs = [x[b] for b in range(B)]
    for l0 in range(0, L, LAYERS_PER_LAUNCH):
        ls = range(l0, l0 + LAYERS_PER_LAUNCH)
        maps = [_layer_inputs(ls, xs[b], *args, b, consts, perm) for b in range(B)]
        in_maps = [maps[i % B] for i in range(NCORES)]
        res = run_bass_kernel_spmd(nc, in_maps, core_ids=list(range(NCORES)))
        xs = [np.asarray(res.results[b]["y"], np.float32) for b in range(B)]
    return np.stack(xs, 0).astype(np.float32)
```

```python
import numpy as np
import ml_dtypes
import concourse.bass as bass
import concourse.mybir as mybir
from concourse.bass_utils import run_bass_kernel_spmd

F32 = mybir.dt.float32
BF16 = mybir.dt.bfloat16
ALU = mybir.AluOpType
ACTF = mybir.ActivationFunctionType
AX = mybir.AxisListType

NCORES = 8
B, S, D, L = 4, 4096, 1024, 4
NT = S // 128
DFF = 4096
EPS = 1e-6
GW = 652
INC = 512 + 2 * GW
POOL_WINDOWS = (2, 4, 8, 16)
ENGS = ("pe", "act", "dve", "pool", "sp")


class _Rec:
    def __getattr__(self, name):
        def f(*a, **k):
            self.call = (name, a, k)
            return self
        return f


class Prog:
    def __init__(self, nc, n_dma_sems=32):
        self.nc = nc
        self.q = {e: [] for e in ENGS}
        self.cnt = {e: 0 for e in ENGS}
        self.sems = {}
        self._ctx = []
        for e in ENGS:
            cm = nc.semaphore("s_" + e)
            self.sems[e] = cm.__enter__()
            self._ctx.append(cm)
        self.n_dma = n_dma_sems
        self.dma_uses = [0] * n_dma_sems
        self.dma_rr = 0
        for i in range(n_dma_sems):
            cm = nc.semaphore("s_dma%d" % i)
            self.sems["dma%d" % i] = cm.__enter__()
            self._ctx.append(cm)
        self.waited = {e: {} for e in ENGS}
        self.last_w = {}
        self.readers = {}
        self.ninst = 0

    def close(self):
        for cm in reversed(self._ctx):
            cm.__exit__(None, None, None)

    def _deps(self, eng, reads, writes):
        need = {}

        def add(tok):
            s, v = tok
            if eng == "pe" and s == "pe":
                return
            if need.get(s, 0) < v:
                need[s] = v
        for k in reads:
            if k in self.last_w:
                add(self.last_w[k])
        for k in writes:
            if k in self.last_w:
                add(self.last_w[k])
            for tok in self.readers.get(k, ()):
                add(tok)
        out = []
        w = self.waited[eng]
        for s, v in need.items():
            if w.get(s, 0) < v:
                w[s] = v
                out.append((s, v))
        return out

    def _commit(self, tok, reads, writes):
        for k in writes:
            self.last_w[k] = tok
            self.readers[k] = []
        for k in reads:
            if k in writes:
                continue
            self.readers.setdefault(k, []).append(tok)

    def op(self, eng, fn, reads=(), writes=()):
        waits = self._deps(eng, reads, writes)
        self.cnt[eng] += 1
        tok = (eng, self.cnt[eng])
        sems = self.sems
        rec = _Rec()
        fn(rec)
        name, a, k = rec.call

        def run(E, waits=waits, s=sems[eng], name=name, a=a, k=k):
            for (ws, wv) in waits:
                E.wait_ge(sems[ws], wv)
            getattr(E, name)(*a, **k).then_inc(s, 1)
        self.q[eng].append(run)
        self._commit(tok, reads, writes)
        self.ninst += 1 + len(waits)
        return tok

    def dma(self, eng, out, in_, reads=(), writes=(), **kw):
        i = self.dma_rr
        self.dma_rr = (self.dma_rr + 1) % self.n_dma
        sname = "dma%d" % i
        waits = self._deps(eng, reads, writes)
        prev = 16 * self.dma_uses[i]
        if prev and self.waited[eng].get(sname, 0) < prev:
            self.waited[eng][sname] = prev
            waits.append((sname, prev))
        self.dma_uses[i] += 1
        tok = (sname, 16 * self.dma_uses[i])
        sems = self.sems

        def run(E, waits=waits, s=sems[sname]):
            for (ws, wv) in waits:
                E.wait_ge(sems[ws], wv)
            E.dma_start(out=out, in_=in_, **kw).then_inc(s, 16)
        self.q[eng].append(run)
        self._commit(tok, reads, writes)
        self.ninst += 1 + len(waits)
        return tok

    def barrier(self):
        waits = []
        for e in ENGS:
            if self.cnt[e]:
                waits.append((e, self.cnt[e]))
        for i in range(self.n_dma):
            if self.dma_uses[i]:
                waits.append(("dma%d" % i, 16 * self.dma_uses[i]))
        sems = self.sems
        for e in ENGS:
            mine = [(s, v) for (s, v) in waits if self.waited[e].get(s, 0) < v and not (s == e and e == "pe")]
            for (s, v) in mine:
                self.waited[e][s] = v

            def run(E, mine=mine):
                for (ws, wv) in mine:
                    E.wait_ge(sems[ws], wv)
            self.q[e].append(run)
        self.last_w = {}
        self.readers = {}

    def emit(self):
        nc = self.nc
        q = self.q
        with nc.Block() as block:
            @block.tensor
            def _(E):
                for f in q["pe"]:
                    f(E)

            @block.scalar
            def _(E):
                for f in q["act"]:
                    f(E)

            @block.vector
            def _(E):
                for f in q["dve"]:
                    f(E)

            @block.gpsimd
            def _(E):
                for f in q["pool"]:
                    f(E)

            @block.sync
            def _(E):
                for f in q["sp"]:
                    f(E)


class Arena:
    def __init__(self, big, nbytes):
        self.big = big
        self.nbytes = nbytes
        self.off = 0
        self.marks = []

    def alloc(self, shape, dtype, parts=128):
        n = int(np.prod(shape))
        esz = 4 if dtype == F32 else 2
        nb = (n * esz + 31) // 32 * 32
        assert self.off + nb <= self.nbytes, ("SBUF arena overflow", self.off, nb, self.nbytes)
        w0 = self.off // 4
        v = self.big[0:parts, w0:w0 + nb // 4]
        if dtype != F32:
            v = v.bitcast(dtype)
        v = v[:, 0:n]
        self.off += nb
        if len(shape) == 2:
            return v.rearrange("p (a b) -> p a b", b=shape[1])
        if len(shape) == 3:
            return v.rearrange("p (a b c) -> p a b c", b=shape[1], c=shape[2])
        return v

    def mark(self):
        self.marks.append(self.off)

    def release(self):
        self.off = self.marks.pop()


def _consts():
    bf = ml_dtypes.bfloat16
    c = {}
    c["ident"] = np.eye(128, dtype=np.float32)
    inv = (10000.0 ** (-np.arange(32, dtype=np.float32) * 2.0 / 64.0)).astype(np.float32)
    pos = (np.arange(NT)[None, :] * 128 + np.arange(128)[:, None]).astype(np.float32)
    ang = pos[:, :, None] * inv[None, None, :]
    c["cos"] = np.cos(ang).astype(np.float32)
    c["sin"] = np.sin(ang).astype(np.float32)
    m = (np.arange(NT)[None, :] * 8 + np.arange(8)[:, None])
    cpos = (16 * m + 15).astype(np.float32)
    cang = cpos[:, :, None] * inv[None, None, :]
    c["ccos"] = np.cos(cang).astype(np.float32)
    c["csin"] = np.sin(cang).astype(np.float32)
    k = np.arange(128)[:, None]
    q = np.arange(128)[None, :]
    c["tri"] = (k <= q).astype(np.float32)
    c["triinv"] = (k > q).astype(np.float32)
    cm = np.zeros((128, 16, 128), np.float32)
    for tt in range(16):
        i = np.arange(128)[:, None] - 8 * tt
        r = np.arange(128)[None, :]
        vis = (i < 0) | ((i >= 0) & (i <= 7) & (r >= 16 * i + 15))
        cm[:, tt, :] = vis
    c["cmask"] = cm
    ex = np.zeros((64, S), np.float32)
    ex[np.arange(S) // 64, np.arange(S)] = 1.0
    c["expand"] = ex
    r = np.arange(128)[:, None]
    jj = np.arange(128)[None, :]
    cur = 64 + (r >= 64)
    keep = (jj < cur - 1).astype(np.float32)
    add = np.where((jj == cur) | (jj == cur - 1), 1e4, np.where(jj > cur, -1.0, 0.0)).astype(np.float32)
    c["keepB"] = keep
    c["addB"] = add
    n_cmp = (S - 32) // 16 + 1
    cs0 = np.arange(n_cmp) * 16
    ss0 = np.arange(64) * 64
    ov = np.minimum(cs0[:, None] + 32, ss0[None, :] + 64) - np.maximum(cs0[:, None], ss0[None, :])
    ov = np.clip(ov, 0, None).astype(np.float32) / 32.0
    ovs = np.zeros((256, 64), np.float32)
    ovs[1:1 + n_cmp] = ov
    c["ov"] = ovs.reshape(2, 128, 64).transpose(1, 0, 2).copy()
    bands = np.zeros((128, 3, 4, 128), np.float32)
    s = np.arange(128)[:, None]
    t = np.arange(128)[None, :]
    for gi, w in enumerate(POOL_WINDOWS):
        main = ((s <= t) & (s > t - w)).astype(np.float32) / w - (s == t)
        corner = (s >= 129 + t - w).astype(np.float32) / w
        cnt = np.minimum(t + 1, w).astype(np.float32)
        first = ((s <= t) & (s > t - w)).astype(np.float32) / cnt - (s == t)
        bands[:, 0, gi, :] = main
        bands[:, 1, gi, :] = corner
        bands[:, 2, gi, :] = first
    c["bands"] = bands
    c["ones"] = np.ones((128, 128), np.float32)
    return c


_CONST_SPECS = [
    ("ident", [128, 128], BF16, 128), ("cos", [128, NT, 32], F32, 128), ("sin", [128, NT, 32], F32, 128),
    ("ccos", [8, NT, 32], F32, 8), ("csin", [8, NT, 32], F32, 8),
    ("tri", [128, 128], BF16, 128), ("triinv", [128, 128], BF16, 128),
    ("cmask", [128, 16, 128], BF16, 128), ("expand", [64, S], BF16, 64),
    ("keepB", [128, 128], F32, 128), ("addB", [128, 128], F32, 128),
    ("ov", [128, 2, 64], BF16, 128), ("bands", [128, 3, 4, 128], F32, 128), ("ones", [128, 128], F32, 128),
]


def build_program(n_layers, first_layer_norm=True, debug=False, nt=NT, stop=None, mstage=99):
    nc = bass.Bass("TRN2", target_bir_lowering=False)
    LW = n_layers

    def din(name, shape, dt=F32):
        return nc.dram_tensor(name, shape, dt, kind="ExternalInput").ap()

    x_in = din("x", [S, D])
    c_col = din("c_col", [128, 8])
    w_mod = din("w_mod", [LW, D, 6 * D])
    b_mod = din("b_mod", [LW, 1, 6 * D])
    n1c = din("n1c", [LW, 128, 8])
    n2c = din("n2c", [LW, 128, 8])
    w_in = din("w_in", [LW, D, INC])
    pool_w = din("pool_w", [LW, 128, 4, 128])
    pscale = din("pscale", [LW, 128, 4])
    gains = din("gains", [LW, 128, 13, 64])
    w1 = din("w1", [LW, 64, 2 * 32 * 64])
    w2 = din("w2", [LW, 64, 2 * 64])
    peT = din("peT", [LW, 64, 2 * 32])
    w_out = din("w_out", [LW, D, D])
    w_ff1 = din("w_ff1", [LW, D, DFF])
    w_ff2 = din("w_ff2", [LW, DFF, D])
    cd = {name: din("k_" + name, shape) for (name, shape, _, _) in _CONST_SPECS}
    y_out = nc.dram_tensor("y", [S, D], F32, kind="ExternalOutput").ap()
    okind = dict(kind="ExternalOutput") if debug else {}
    hT_d = nc.dram_tensor("hT_d", [128, 8, S], BF16, **okind).ap()
    mixT_d = nc.dram_tensor("mixT_d", [128, 8, S], BF16, **okind).ap()
    xd = nc.dram_tensor("xd", [S, D], F32).ap()

    P = Prog(nc)
    ARENA_BYTES = 196 * 1024
    big_cm = nc.sbuf_tensor("arena", [128, ARENA_BYTES // 4], F32)
    big = big_cm.__enter__()
    A = Arena(big, ARENA_BYTES)
    ps_cms = [nc.psum_tensor("ps%d" % i, [128, 512], F32) for i in range(8)]
    ps = [cm.__enter__() for cm in ps_cms]

    def psv(i, shape, dtype=F32, parts=128):
        v = ps[i][0:parts, :]
        if dtype != F32:
            v = v.bitcast(dtype)
        n = int(np.prod(shape))
        v = v[:, 0:n]
        if len(shape) == 2:
            return v.rearrange("p (a b) -> p a b", b=shape[1])
        if len(shape) == 3:
            return v.rearrange("p (a b c) -> p a b c", b=shape[1], c=shape[2])
        return v

    K = {}
    for (name, shape, dt, parts) in _CONST_SPECS:
        if name in ("expand", "cmask", "cos", "sin", "ccos", "csin", "bands", "keepB", "addB", "ov", "tri", "triinv"):
            continue
        K[name] = A.alloc(shape[1:], dt, parts)
    ident = K["ident"]
    ones = K["ones"]
    cact = A.alloc([8], F32)
    modcols = A.alloc([4, 8], F32)
    s1c = A.alloc([8], F32)
    s2c = A.alloc([8], F32)
    n1t = A.alloc([8], F32)
    n2t = A.alloc([8], F32)
    gate1 = A.alloc([D], F32)
    gate2 = A.alloc([D], F32)
    small = A.alloc([64], F32)
    junk = A.alloc([D], BF16)

    def load_const(name, eng="pool"):
        for (nm, shape, dt, parts) in _CONST_SPECS:
            if nm == name:
                src = cd[name]
                dst = K[name]
                P.dma(eng if dt != F32 else "sp", dst, src, writes=["K_" + name])

    for nm in K:
        load_const(nm)
    P.dma("sp", cact, c_col, writes=["cact"])
    P.op("act", lambda E: E.activation(out=small[:, 0:8], in_=cact, func=ACTF.Exp, scale=-1.0), reads=["cact"], writes=["small"])
    P.op("dve", lambda E: E.tensor_scalar_add(out=small[:, 0:8], in0=small[:, 0:8], scalar1=1.0), reads=["small"], writes=["small"])
    P.op("dve", lambda E: E.reciprocal(out=small[:, 0:8], in_=small[:, 0:8]), reads=["small"], writes=["small"])
    P.op("dve", lambda E: E.tensor_tensor(out=cact, in0=cact, in1=small[:, 0:8], op=ALU.mult), reads=["small", "cact"], writes=["cact"])

    def rstd_from_ss(ss_ap, n, key, scale):
        P.op("act", lambda E: E.activation(out=ss_ap, in_=ss_ap, func=ACTF.Ln, scale=scale, bias=EPS), reads=[key], writes=[key])
        P.op("act", lambda E: E.activation(out=ss_ap, in_=ss_ap, func=ACTF.Exp, scale=-0.5), reads=[key], writes=[key])

    def norm_to_hT(xt, xkey, hT, hkey, scol, bcol, colkeys, xh, tag):
        ss = small[:, 32:33]
        P.op("act", lambda E: E.activation(out=junk, in_=xt, func=ACTF.Square, accum_out=ss), reads=[xkey], writes=["junk", "ss"])
        rstd_from_ss(ss, 1, "ss", 1.0 / D)
        P.op("dve", lambda E: E.tensor_scalar(out=xh, in0=xt, scalar1=ss, scalar2=None, op0=ALU.mult), reads=[xkey, "ss"], writes=["xh" + tag])
        pT = psv(2, [8, 128], BF16)
        for kc in range(8):
            P.op("pe", lambda E, kc=kc: E.transpose(out=pT[:, kc, :], in_=xh[:, kc * 128:(kc + 1) * 128], identity=ident),
                 reads=["xh" + tag, "K_ident"], writes=["ps2"])
        for kc in range(8):
            P.op("act", lambda E, kc=kc: E.activation(out=hT[:, kc, :], in_=pT[:, kc, :], func=ACTF.Identity,
                                                     scale=scol[:, kc:kc + 1], bias=bcol[:, kc:kc + 1]),
                 reads=["ps2"] + colkeys, writes=[hkey])

    for l in range(n_layers):
        x_src = x_in if l == 0 else xd
        x_dst = y_out if l == n_layers - 1 else xd

        P.barrier()
        A.mark()
        modrow = A.alloc([6 * D], F32, parts=1)
        bmrow = A.alloc([6 * D], F32, parts=1)
        wm = [A.alloc([8, 512], F32) for _ in range(2)]
        P.dma("sp", bmrow, b_mod[l], writes=["bmrow"])
        P.dma("sp", n1t, n1c[l], writes=["n1t"])
        P.dma("sp", n2t, n2c[l], writes=["n2t"])
        for ch in range(12):
            buf = wm[ch % 2]
            P.dma("sp", buf, w_mod[l][:, ch * 512:(ch + 1) * 512].rearrange("(k p) n -> p k n", p=128), writes=["wm%d" % (ch % 2)])
            pr = ps[ch % 2][0:1, :]
            for kc in range(8):
                P.op("pe", lambda E, kc=kc, buf=buf, pr=pr: E.matmul(pr, lhsT=cact[:, kc:kc + 1], rhs=buf[:, kc, :], start=(kc == 0), stop=(kc == 7)),
                     reads=["cact", "wm%d" % (ch % 2)], writes=["ps%d" % (ch % 2)])
            P.op("dve", lambda E, ch=ch, pr=pr: E.tensor_tensor(out=modrow[:, ch * 512:(ch + 1) * 512], in0=pr, in1=bmrow[:, ch * 512:(ch + 1) * 512], op=ALU.add),
                 reads=["ps%d" % (ch % 2), "bmrow"], writes=["modrow"])
        pc = ps[2][:, 0:32]
        for vi, off in enumerate((0, 1024, 3072, 4096)):
            for kc in range(8):
                j = vi * 8 + kc
                P.op("pe", lambda E, j=j, off=off, kc=kc: E.matmul(pc[:, j:j + 1], lhsT=modrow[0:1, off + kc * 128: off + (kc + 1) * 128],
                                                                  rhs=ones[0:1, 0:1], start=True, stop=True),
                     reads=["modrow", "K_ones"], writes=["ps2"])
        P.op("dve", lambda E: E.tensor_copy(out=modcols.rearrange("p a b -> p (a b)"), in_=pc), reads=["ps2"], writes=["modcols"])
        P.op("dve", lambda E: E.scalar_tensor_tensor(out=s1c, in0=modcols[:, 1, :], scalar=1.0, in1=n1t, op0=ALU.add, op1=ALU.mult),
             reads=["modcols", "n1t"], writes=["s1c"])
        P.op("dve", lambda E: E.scalar_tensor_tensor(out=s2c, in0=modcols[:, 3, :], scalar=1.0, in1=n2t, op0=ALU.add, op1=ALU.mult),
             reads=["modcols", "n2t"], writes=["s2c"])
        for gi, (gt, off) in enumerate(((gate1, 2048), (gate2, 5120))):
            for h in range(2):
                pb = ps[3 + h]
                P.op("pe", lambda E, off=off, h=h, pb=pb: E.matmul(pb[:, :], lhsT=ones[0:1, :], rhs=modrow[0:1, off + h * 512: off + (h + 1) * 512], start=True, stop=True),
                     reads=["modrow", "K_ones"], writes=["ps%d" % (3 + h)])
                P.op("dve", lambda E, gt=gt, h=h, pb=pb: E.tensor_copy(out=gt[:, h * 512:(h + 1) * 512], in_=pb[:, :]),
                     reads=["ps%d" % (3 + h)], writes=["gate%d" % gi])
        b1c = modcols[:, 0, :]
        b2c = modcols[:, 2, :]
        P.barrier()
        A.release()
        if stop == "mod":
            break

        if True:
            A.mark()
            xts = [A.alloc([D], F32) for _ in range(2)]
            xhs = [A.alloc([D], BF16) for _ in range(2)]
            hTs = [A.alloc([8, 128], BF16) for _ in range(2)]
            for t in range(nt):
                b = t % 2
                P.dma("sp", xts[b], x_src[t * 128:(t + 1) * 128, :], reads=["x_d%d" % t], writes=["xt%d" % b])
                norm_to_hT(xts[b], "xt%d" % b, hTs[b], "hT%d" % b, s1c, b1c, ["s1c", "modcols"], xhs[b], "n%d" % b)
                P.dma("sp", hT_d[:, :, t * 128:(t + 1) * 128], hTs[b], reads=["hT%d" % b], writes=["hT_d%d" % t])
            P.barrier()
            A.release()
        if stop == "norm":
            break

        A.mark()
        mixer_phase(nc, P, A, ps, psv, l, nt, K, cd, dict(
            w_in=w_in, pool_w=pool_w, pscale=pscale, gains=gains, w1=w1, w2=w2, peT=peT,
            hT_d=hT_d, mixT_d=mixT_d, ident=ident, ones=ones, small=small, mstage=mstage))
        P.barrier()
        A.release()
        if stop == "mixer":
            break

        A.mark()
        wo = A.alloc([8, D], BF16)
        f1 = A.alloc([8, DFF], BF16)
        f2 = A.alloc([32, D], BF16)
        for kc in range(8):
            P.dma("pool", wo[:, kc, :], w_out[l][kc * 128:(kc + 1) * 128, :], writes=["wo"])
        for kc in range(8):
            for hh in range(2):
                P.dma("pool", f1[:, kc, hh * 2048:(hh + 1) * 2048], w_ff1[l][kc * 128:(kc + 1) * 128, hh * 2048:(hh + 1) * 2048], writes=["f1"])
        for c4 in range(8):
            P.dma("pool", f2[:, c4 * 4:(c4 + 1) * 4, :], w_ff2[l][c4 * 512:(c4 + 1) * 512, :].rearrange("(c p) n -> p c n", p=128), writes=["f2"])
        mts = [A.alloc([8, 128], BF16) for _ in range(2)]
        xts = [A.alloc([D], F32) for _ in range(2)]
        xh = A.alloc([D], BF16)
        h2T = A.alloc([8, 128], BF16)
        aT = A.alloc([32, 128], BF16)
        rl = [A.alloc([512], F32) for _ in range(2)]
        tmp = A.alloc([512], F32)
        for t in range(nt):
            b = t % 2
            xt = xts[b]
            xk = "xt%d" % b
            P.dma("sp", mts[b], mixT_d[:, :, t * 128:(t + 1) * 128], reads=["mixT_d%d" % t], writes=["mt%d" % b])
            P.dma("sp", xt, x_src[t * 128:(t + 1) * 128, :], reads=["x_d%d" % t], writes=[xk])
            for h in range(2):
                for kc in range(8):
                    P.op("pe", lambda E, h=h, kc=kc, b=b: E.matmul(ps[h][:, :], lhsT=mts[b][:, kc, :], rhs=wo[:, kc, h * 512:(h + 1) * 512], start=(kc == 0), stop=(kc == 7)),
                         reads=["mt%d" % b, "wo"], writes=["ps%d" % h])
                P.op("dve", lambda E, h=h: E.tensor_tensor(out=tmp, in0=ps[h][:, :], in1=gate1[:, h * 512:(h + 1) * 512], op=ALU.mult),
                     reads=["ps%d" % h, "gate0"], writes=["tmp"])
                P.op("dve", lambda E, h=h, xt=xt: E.tensor_tensor(out=xt[:, h * 512:(h + 1) * 512], in0=xt[:, h * 512:(h + 1) * 512], in1=tmp, op=ALU.add),
                     reads=["tmp", xk], writes=[xk])
            norm_to_hT(xt, xk, h2T, "h2T", s2c, b2c, ["s2c", "modcols"], xh, "f")
            for c4 in range(8):
                pf = ps[3 + (c4 % 2)]
                for cc in range(4):
                    c = c4 * 4 + cc
                    for kc in range(8):
                        P.op("pe", lambda E, c=c, cc=cc, kc=kc, pf=pf: E.matmul(pf[:, cc * 128:(cc + 1) * 128], lhsT=f1[:, kc, c * 128:(c + 1) * 128], rhs=h2T[:, kc, :],
                                                                               start=(kc == 0 and cc == 0), stop=(kc == 7 and cc == 3), skip_group_check=True),
                             reads=["h2T", "f1"], writes=["ps%d" % (3 + c4 % 2)])
                r = rl[c4 % 2]
                P.op("act", lambda E, pf=pf, r=r: E.activation(out=r, in_=pf[:, :], func=ACTF.Relu), reads=["ps%d" % (3 + c4 % 2)], writes=["rl%d" % (c4 % 2)])
                P.op("pool", lambda E, r=r, c4=c4: E.tensor_tensor(out=aT[:, c4 * 4:(c4 + 1) * 4, :].rearrange("p a b -> p (a b)"), in0=r, in1=r, op=ALU.mult),
                     reads=["rl%d" % (c4 % 2)], writes=["aT%d" % c4])
            for h in range(2):
                for c in range(32):
                    P.op("pe", lambda E, h=h, c=c: E.matmul(ps[h][:, :], lhsT=aT[:, c, :], rhs=f2[:, c, h * 512:(h + 1) * 512], start=(c == 0), stop=(c == 31)),
                         reads=["aT%d" % (c // 4), "f2"], writes=["ps%d" % h])
                P.op("dve", lambda E, h=h: E.tensor_tensor(out=tmp, in0=ps[h][:, :], in1=gate2[:, h * 512:(h + 1) * 512], op=ALU.mult),
                     reads=["ps%d" % h, "gate1"], writes=["tmp"])
                P.op("dve", lambda E, h=h, xt=xt: E.tensor_tensor(out=xt[:, h * 512:(h + 1) * 512], in0=xt[:, h * 512:(h + 1) * 512], in1=tmp, op=ALU.add),
                     reads=["tmp", xk], writes=[xk])
            P.dma("sp", x_dst[t * 128:(t + 1) * 128, :], xt, reads=[xk], writes=["x_d%d" % t])
            if l < n_layers - 1:
                pass
        P.barrier()
        A.release()
        if l < n_layers - 1:
            pass

    P.barrier()
    P.emit()
    P.close()
    for cm in reversed(ps_cms):
        cm.__exit__(None, None, None)
    big_cm.__exit__(None, None, None)
    return nc


def mixer_phase(nc, P, A, ps, psv, l, nt, K, cd, W):
    ident = W["ident"]
    hT_d, mixT_d = W["hT_d"], W["mixT_d"]

    def cp(eng, out, in_, reads, writes):
        if eng == "act":
            P.op("act", lambda E: E.activation(out=out, in_=in_, func=ACTF.Identity), reads, writes)
        else:
            P.op(eng, lambda E: E.tensor_copy(out=out, in_=in_), reads, writes)

    def tt(eng, out, a, b, op, reads, writes):
        P.op(eng, lambda E: E.tensor_tensor(out=out, in0=a, in1=b, op=op), reads, writes)

    tri = A.alloc([128], BF16)
    triinv = A.alloc([128], BF16)
    cmask = A.alloc([16, 128], BF16)
    expand = A.alloc([S], BF16, parts=64)
    keepB = A.alloc([128], F32)
    addB = A.alloc([128], F32)
    ov = A.alloc([2, 64], BF16)
    bands = A.alloc([3, 4, 128], F32)
    P.dma("pool", tri, cd["tri"], writes=["tri"])
    P.dma("pool", triinv, cd["triinv"], writes=["triinv"])
    P.dma("pool", cmask, cd["cmask"], writes=["cmask"])
    for hh in range(2):
        P.dma("pool", expand[:, hh * 2048:(hh + 1) * 2048], cd["expand"][:, hh * 2048:(hh + 1) * 2048], writes=["expand"])
    P.dma("sp", keepB, cd["keepB"], writes=["keepB"])
    P.dma("sp", addB, cd["addB"], writes=["addB"])
    P.dma("pool", ov, cd["ov"], writes=["ov"])
    P.dma("sp", bands, cd["bands"], writes=["bands"])
    win = A.alloc([8, INC], BF16)
    for kc in range(8):
        P.dma("pool", win[:, kc, :], W["w_in"][l][kc * 128:(kc + 1) * 128, :], writes=["win"])
    pw = A.alloc([4, 128], BF16)
    P.dma("pool", pw, W["pool_w"][l], writes=["pw"])
    psc = A.alloc([4], F32)
    P.dma("sp", psc, W["pscale"][l], writes=["psc"])
    gn = A.alloc([13, 64], F32)
    P.dma("sp", gn, W["gains"][l], writes=["gn"])
    W1 = A.alloc([2, 32, 64], BF16, parts=64)
    for kv in range(2):
        P.dma("pool", W1[:, kv, :, :].rearrange("p a b -> p (a b)"), W["w1"][l][:, kv * 2048:(kv + 1) * 2048], writes=["W1"])
    W2 = A.alloc([2, 64], BF16, parts=64)
    P.dma("pool", W2.rearrange("p a b -> p (a b)"), W["w2"][l], writes=["W2"])
    peT = A.alloc([2, 32], BF16, parts=64)
    P.dma("pool", peT.rearrange("p a b -> p (a b)"), W["peT"][l], writes=["peT"])
    cbias = A.alloc([2], F32, parts=64)
    pb = ps[0][0:64, 0:2]
    for kv in range(2):
        for p in range(32):
            P.op("pe", lambda E, kv=kv, p=p: E.matmul(pb[:, kv:kv + 1], lhsT=W1[:, kv, p, :], rhs=peT[:, kv, p:p + 1], start=(p == 0), stop=(p == 31)),
                 reads=["W1", "peT"], writes=["ps0"])
    cp("dve", cbias, pb, ["ps0"], ["cbias"])
    kT = [A.alloc([2, S], BF16, parts=64) for _ in range(2)]
    craw = [A.alloc([2, 144], BF16, parts=64) for _ in range(2)]
    Vall = A.alloc([2 * nt * 2, 66], BF16).rearrange("p (g t k) c -> p g t k c", g=2, k=2)
    kTc = [A.alloc([256], BF16, parts=64) for _ in range(2)]
    Vc = [A.alloc([2, 66], BF16) for _ in range(2)]
    P.op("pool", lambda E: E.memset(Vall, 1.0), writes=["V"])
    for g in range(2):
        P.op("pool", lambda E, g=g: E.memset(Vc[g], 0.0), writes=["Vc%d" % g])
        P.op("pool", lambda E, g=g: E.memset(Vc[g][:, :, 64:65], 1.0), writes=["Vc%d" % g])
        P.op("pool", lambda E, g=g: E.memset(Vc[g][0:1, 0, 64:65], 0.0), writes=["Vc%d" % g])
        P.op("pool", lambda E, g=g: E.memset(kTc[g], 0.0), writes=["kTc%d" % g])
        P.op("pool", lambda E, g=g: E.memset(craw[g], 0.0), writes=["craw%d" % g])
    hTt = [A.alloc([8, 128], BF16) for _ in range(2)]
    cs = [A.alloc([2, 32], F32) for _ in range(2)]
    ccs = [A.alloc([2, 32], F32, parts=8) for _ in range(2)]
    u = [A.alloc([512], F32) for _ in range(2)]
    pj = A.alloc([2 * GW], F32)
    sq = A.alloc([12, 64], F32)
    st = A.alloc([12], F32)
    xn = A.alloc([12, 64], F32)
    ra = A.alloc([12, 32], F32)
    rb = A.alloc([12, 32], F32)
    rc = A.alloc([12, 32], F32)
    rd = A.alloc([12, 32], F32)
    tball = A.alloc([2, 9, 64], BF16)
    sgalls = [A.alloc([24], F32) for _ in range(2)]
    mixts = [A.alloc([8, 128], BF16) for _ in range(2)]
    pooledT = A.alloc([512], BF16)
    P.op("pool", lambda E: E.memset(tball, 0.0), writes=["tb"])

    class GB:
        pass
    G = []
    for g in range(2):
        o = GB()
        o.qT = A.alloc([512], BF16, parts=64)
        o.Eb = [A.alloc([512], BF16) for _ in range(2)]
        o.msk = A.alloc([128], BF16)
        o.oall = A.alloc([3, 264], F32)
        o.rden = A.alloc([12], F32)
        o.coef = A.alloc([12], F32)
        o.imp = A.alloc([64], F32)
        o.imp2 = A.alloc([64], F32)
        o.wk = A.alloc([64], F32)
        o.m8 = A.alloc([16], F32)
        o.thr = A.alloc([1], F32)
        o.selb = A.alloc([128], BF16)
        o.selT = A.alloc([128], BF16, parts=64)
        o.y32 = A.alloc([256], F32)
        o.ybf = A.alloc([256], BF16)
        o.zc = A.alloc([16], F32)
        o.ec = A.alloc([16], F32)
        o.sTc = A.alloc([16], BF16, parts=64)
        o.k8 = A.alloc([64], F32, parts=8)
        o.k8q = A.alloc([64], F32, parts=8)
        o.k8s = A.alloc([4], F32)
        o.k8r = [A.alloc([32], F32, parts=8) for _ in range(4)]
        o.k8b = A.alloc([128], BF16)
        o.v8 = A.alloc([64], BF16, parts=8)
        P.op("pool", lambda E, o=o: E.memset(o.zc, 0.0), writes=["zc%d" % g])
        P.op("pool", lambda E, o=o: E.memset(o.k8s, 1.0), writes=["k8s%d" % g])
        P.op("pool", lambda E, o=o: E.memset(o.selb, 0.0), writes=["selb%d" % g])
        P.op("pool", lambda E, o=o: E.memset(o.k8b, 0.0), writes=["k8b%d" % g])
        G.append(o)

    def bc_h(ap128):
        return ap128.unsqueeze(1).to_broadcast([128, 4, 128])

    def e4(buf):
        return buf.rearrange("p (h q) -> p h q", h=4)

    def attention_branch(g, t, br, kts, kTsrc, vsrc, vkey, masks, use_sel):
        o = G[g]
        B0 = 3 * g
        sbank, obank, ibank, mbank = B0, B0 + 1, B0 + 2, B0 + 2
        psO = ps[obank]
        psI = ps[ibank]
        nk = len(kts)

        def scores(i):
            kt = kts[i]
            P.op("pe", lambda E: E.matmul(ps[sbank][:, :], lhsT=kTsrc(kt), rhs=o.qT, start=True, stop=True),
                 reads=["kcache%d" % g, "kTc%d" % g, "qT%d" % g], writes=["ps%d" % sbank])
            if use_sel:
                slot = ps[mbank][:, 256:384]
                P.op("pe", lambda E: E.matmul(slot, lhsT=expand[:, kt * 128:(kt + 1) * 128], rhs=o.selT, start=True, stop=True),
                     reads=["expand", "selT%d" % g], writes=["ps%d" % mbank])
        scores(0)
        yield
        for i, kt in enumerate(kts):
            E_ = o.Eb[i % 2]
            ek = "Eb%d_%d" % (g, i % 2)
            P.op("act", lambda E, E_=E_: E.activation(out=E_, in_=ps[sbank][:, :], func=ACTF.Exp, scale=0.125),
                 reads=["ps%d" % sbank], writes=[ek])
            m = masks(kt)
            if use_sel:
                slot = ps[mbank][:, 256:384]
                sk = "ps%d" % mbank
                if m is not None:
                    tt("dve", o.msk, slot, m[0], ALU.mult, [sk, m[1]], ["msk%d" % g])
                    tt("pool", e4(E_), e4(E_), bc_h(o.msk), ALU.mult, [ek, "msk%d" % g], [ek])
                else:
                    tt("dve", e4(E_), e4(E_), bc_h(slot), ALU.mult, [ek, sk], [ek])
            elif m is not None:
                tt("pool", e4(E_), e4(E_), bc_h(m[0]), ALU.mult, [ek, m[1]], [ek])
            if i + 1 < nk:
                scores(i + 1)
            yield
            for h in range(4):
                first = (i == 0 and h == 0)
                last = (i == nk - 1 and h == 3)
                P.op("pe", lambda E, h=h, kt=kt, E_=E_, first=first, last=last: E.matmul(
                    psO[:, h * 66:(h + 1) * 66], lhsT=E_[:, h * 128:(h + 1) * 128], rhs=vsrc(kt), start=first, stop=last, skip_group_check=True),
                    reads=[ek, vkey], writes=["ps%d" % obank])
                if br == 0:
                    P.op("pe", lambda E, h=h, kt=kt, E_=E_, first=first, last=last: E.matmul(
                        psI[:, h * 64:(h + 1) * 64], lhsT=E_[:, h * 128:(h + 1) * 128], rhs=ov[:, kt, :], start=first, stop=last, skip_group_check=True),
                        reads=[ek, "ov"], writes=["ps%d" % ibank])
            yield
        cp("act", o.oall[:, br, :], psO[:, 0:264], ["ps%d" % obank], ["oall%d_%d" % (g, br)])
        P.op("dve", lambda E: E.tensor_scalar_max(out=o.rden[:, br * 4:(br + 1) * 4], in0=o.oall[:, br, :].rearrange("p (h c) -> p h c", c=66)[:, :, 64], scalar1=1e-30),
             reads=["oall%d_%d" % (g, br)], writes=["rden%d_%d" % (g, br)])
        P.op("dve", lambda E: E.reciprocal(out=o.rden[:, br * 4:(br + 1) * 4], in_=o.rden[:, br * 4:(br + 1) * 4]),
             reads=["rden%d_%d" % (g, br)], writes=["rden%d_%d" % (g, br)])
        yield

    def group_stream(g, t, b):
        o = G[g]
        B0 = 3 * g
        sbank, obank, ibank, mbank = B0, B0 + 1, B0 + 2, B0 + 2
        ts = slice(t * 128, (t + 1) * 128)
        tbg = tball[:, g, :, :]
        obk = "ps%d" % obank
        sgall = sgalls[b]
        mixt = mixts[b]
        pT128 = psv(sbank, [8, 128], BF16)
        pT = pT128[0:64]
        pT2 = psv(sbank, [2, 128], BF16)
        sbk = "ps%d" % sbank
        for s_ in range(8):
            P.op("pe", lambda E, s_=s_: E.transpose(out=pT128[:, s_, :], in_=tbg[:, s_:s_ + 2, :].rearrange("p a b -> p (a b)"), identity=ident),
                 reads=["tb", "K_ident"], writes=[sbk])
        cp("dve", o.qT.rearrange("p (a b) -> p a b", a=4), pT[:, 0:4, :], [sbk], ["qT%d" % g])
        cp("dve", kT[g][:, :, ts], pT[:, 4:6, :], [sbk], ["kcache%d" % g])
        if t > 0:
            cp("dve", craw[g][:, :, 0:16], craw[g][:, :, 128:144], ["craw%d" % g], ["craw%d" % g])
        cp("dve", craw[g][:, :, 16:144], pT[:, 6:8, :], [sbk], ["craw%d" % g])
        yield
        ibk = "ps%d" % ibank
        mbk = "ps%d" % mbank
        preT = ps[obank][0:64, 272:288]
        for kv in range(2):
            for p in range(32):
                P.op("pe", lambda E, kv=kv, p=p: E.matmul(preT[:, kv * 8:(kv + 1) * 8], lhsT=W1[:, kv, p, :], rhs=craw[g][:, kv, p:p + 113:16],
                                                          start=(p == 0), stop=(p == 31), skip_group_check=True),
                     reads=["W1", "craw%d" % g], writes=[obk])
        yield
        zk, ekk = "zc%d" % g, "ec%d" % g
        for kv in range(2):
            P.op("dve", lambda E, kv=kv: E.tensor_scalar(out=o.zc[0:64, kv * 8:(kv + 1) * 8], in0=preT[:, kv * 8:(kv + 1) * 8], scalar1=cbias[:, kv:kv + 1],
                                                         scalar2=None, op0=ALU.add), reads=[obk, "cbias"], writes=[zk])
        P.op("act", lambda E: E.activation(out=o.ec, in_=o.zc, func=ACTF.Exp, scale=-1.0), reads=[zk], writes=[ekk])
        yield
        P.op("dve", lambda E: E.tensor_scalar_add(out=o.ec[0:64], in0=o.ec[0:64], scalar1=1.0), reads=[ekk], writes=[ekk])
        P.op("dve", lambda E: E.reciprocal(out=o.ec[0:64], in_=o.ec[0:64]), reads=[ekk], writes=[ekk])
        tt("dve", o.sTc, o.zc[0:64], o.ec[0:64], ALU.mult, [zk, ekk], ["sTc%d" % g])
        yield
        k8p = ps[obank][0:8, 288:416]
        for kv in range(2):
            P.op("pe", lambda E, kv=kv: E.matmul(k8p[:, kv * 64:(kv + 1) * 64], lhsT=o.sTc[:, kv * 8:(kv + 1) * 8], rhs=W2[:, kv, :], start=True, stop=True),
                 reads=["sTc%d" % g, "W2"], writes=[obk])
        yield
        cp("dve", o.v8, k8p[:, 64:128], [obk], ["v8%d" % g])
        if t == 0:
            P.op("pool", lambda E: E.memset(o.v8[0:1, :], 0.0), reads=[], writes=["v8%d" % g])
        r0 = 8 * (t % 16)
        P.dma("sp", Vc[g][r0:r0 + 8, t // 16, 0:64], o.v8, reads=["v8%d" % g], writes=["Vc%d" % g])
        k8k = "k8%d" % g
        cp("dve", o.k8, k8p[:, 0:64], [obk], [k8k])
        tt("dve", o.k8q, o.k8, o.k8, ALU.mult, [k8k], ["k8q%d" % g])
        P.op("dve", lambda E: E.tensor_reduce(out=o.k8s[0:8, 0:1], in_=o.k8q, axis=AX.X, op=ALU.add), reads=["k8q%d" % g], writes=["k8s%d" % g])
        yield
        P.op("act", lambda E: E.activation(out=o.k8s[:, 0:1], in_=o.k8s[:, 0:1], func=ACTF.Ln, scale=1.0 / 64, bias=EPS), reads=["k8s%d" % g], writes=["k8s%d" % g])
        P.op("act", lambda E: E.activation(out=o.k8s[:, 0:1], in_=o.k8s[:, 0:1], func=ACTF.Exp, scale=-0.5), reads=["k8s%d" % g], writes=["k8s%d" % g])
        yield
        P.op("dve", lambda E: E.tensor_scalar(out=o.k8, in0=o.k8, scalar1=o.k8s[0:8, 0:1], scalar2=None, op0=ALU.mult), reads=[k8k, "k8s%d" % g], writes=[k8k])
        tt("dve", o.k8, o.k8, gn[0:8, 12, :], ALU.mult, [k8k, "gn"], [k8k])
        yield
        cck = "ccs%d" % b
        cc_, ss_ = ccs[b][:, 0, :], ccs[b][:, 1, :]
        kr = ["k8r%d_%d" % (g, i) for i in range(4)]
        tt("dve", o.k8r[0], o.k8[:, 0:32], cc_, ALU.mult, [k8k, cck], [kr[0]])
        tt("pool", o.k8r[1], o.k8[:, 32:64], ss_, ALU.mult, [k8k, cck], [kr[1]])
        tt("dve", o.k8r[2], o.k8[:, 32:64], cc_, ALU.mult, [k8k, cck], [kr[2]])
        tt("pool", o.k8r[3], o.k8[:, 0:32], ss_, ALU.mult, [k8k, cck], [kr[3]])
        yield
        tt("dve", o.k8b[0:8, 0:32], o.k8r[0], o.k8r[1], ALU.subtract, [kr[0], kr[1]], ["k8b%d" % g])
        tt("dve", o.k8b[0:8, 32:64], o.k8r[2], o.k8r[3], ALU.add, [kr[2], kr[3]], ["k8b%d" % g])
        yield
        pt8 = ps[obank][0:64, 416:424]
        P.op("pe", lambda E: E.matmul(pt8, lhsT=o.k8b[0:8, 0:64], rhs=ident[0:8, 0:8], start=True, stop=True), reads=["k8b%d" % g, "K_ident"], writes=[obk])
        cp("dve", kTc[g][:, 8 * t:8 * t + 8], pt8, [obk], ["kTc%d" % g])
        yield
        kts_c = [0] if t < 16 else [0, 1]
        yield from attention_branch(g, t, 0, kts_c, lambda kt: kTc[g][:, kt * 128:(kt + 1) * 128], lambda kt: Vc[g][:, kt, :], "Vc%d" % g,
                                    lambda kt: ((cmask[:, t % 16, :], "cmask") if kt == t // 16 else None), False)
        psI = ps[ibank]
        rk = "rden%d_0" % g
        P.op("dve", lambda E: E.tensor_scalar(out=o.imp, in0=psI[:, 0:64], scalar1=o.rden[:, 0:1], scalar2=None, op0=ALU.mult), reads=[ibk, rk], writes=["imp%d" % g])
        for h in range(1, 4):
            P.op("dve", lambda E, h=h: E.scalar_tensor_tensor(out=o.imp, in0=psI[:, h * 64:(h + 1) * 64], scalar=o.rden[:, h:h + 1], in1=o.imp, op0=ALU.mult, op1=ALU.add),
                 reads=[ibk, rk, "imp%d" % g], writes=["imp%d" % g])
            if h == 2:
                yield
        j0 = 64 - 2 * t
        i2 = "imp2%d" % g
        tt("dve", o.imp2, o.imp, keepB[:, j0:j0 + 64], ALU.mult, ["imp%d" % g, "keepB"], [i2])
        tt("dve", o.imp2, o.imp2, addB[:, j0:j0 + 64], ALU.add, [i2, "addB"], [i2])
        yield
        P.op("dve", lambda E: E.memset(o.imp2[:, 0:1], 1e4), reads=[], writes=[i2])
        P.op("dve", lambda E: E.max(out=o.m8[:, 0:8], in_=o.imp2), reads=[i2], writes=["m8%d" % g])
        yield
        P.op("dve", lambda E: E.match_replace(out=o.wk, in_to_replace=o.m8[:, 0:8], in_values=o.imp2, imm_value=-2.0), reads=[i2, "m8%d" % g], writes=["wk%d" % g])
        P.op("dve", lambda E: E.max(out=o.m8[:, 8:16], in_=o.wk), reads=["wk%d" % g], writes=["m8%d" % g])
        yield
        P.op("dve", lambda E: E.tensor_scalar_max(out=o.thr, in0=o.m8[:, 15:16], scalar1=0.0), reads=["m8%d" % g], writes=["thr%d" % g])
        P.op("dve", lambda E: E.tensor_scalar(out=o.selb[:, 0:64], in0=o.imp2, scalar1=o.thr[:, 0:1], scalar2=None, op0=ALU.is_ge), reads=[i2, "thr%d" % g], writes=["selb%d" % g])
        yield
        psel = ps[mbank][0:64, 384:512]
        P.op("pe", lambda E: E.matmul(psel, lhsT=o.selb[:, 0:64], rhs=ident, start=True, stop=True), reads=["selb%d" % g, "K_ident"], writes=[mbk])
        cp("dve", o.selT, psel, [mbk], ["selT%d" % g])
        yield
        yield from attention_branch(g, t, 1, list(range(t + 1)), lambda kt: kT[g][:, 0, kt * 128:(kt + 1) * 128], lambda kt: Vall[:, g, kt, 0, :], "V",
                                    lambda kt: ((tri, "tri") if kt == t else None), True)
        yield from attention_branch(g, t, 2, list(range(max(0, t - 4), t + 1)), lambda kt: kT[g][:, 1, kt * 128:(kt + 1) * 128], lambda kt: Vall[:, g, kt, 1, :], "V",
                                    lambda kt: ((tri, "tri") if kt == t else ((triinv, "triinv") if kt == t - 4 else None)), False)
        ck = "coef%d" % g
        tt("dve", o.coef.rearrange("p (b h) -> p b h", h=4), sgall[:, g * 12:(g + 1) * 12].rearrange("p (h b) -> p b h", b=3), o.rden.rearrange("p (b h) -> p b h", h=4), ALU.mult,
           ["sg%d" % b, "rden%d_0" % g, "rden%d_1" % g, "rden%d_2" % g], [ck])
        yield
        yk = "y32%d" % g
        for h in range(4):
            ysl = o.y32[:, h * 64:(h + 1) * 64]
            P.op("dve", lambda E, h=h, ysl=ysl: E.tensor_scalar(out=ysl, in0=o.oall[:, 0, h * 66:h * 66 + 64], scalar1=o.coef[:, h:h + 1], scalar2=None, op0=ALU.mult),
                 reads=["oall%d_0" % g, ck], writes=[yk + "_%d" % h])
        yield
        for h in range(4):
            ysl = o.y32[:, h * 64:(h + 1) * 64]
            P.op("dve", lambda E, h=h, ysl=ysl: E.scalar_tensor_tensor(out=ysl, in0=o.oall[:, 1, h * 66:h * 66 + 64], scalar=o.coef[:, 4 + h:5 + h], in1=ysl, op0=ALU.mult, op1=ALU.add),
                 reads=["oall%d_1" % g, ck, yk + "_%d" % h], writes=[yk + "_%d" % h])
        yield
        for h in range(4):
            ysl = o.y32[:, h * 64:(h + 1) * 64]
            P.op("dve", lambda E, h=h, ysl=ysl: E.scalar_tensor_tensor(out=o.ybf[:, h * 64:(h + 1) * 64], in0=o.oall[:, 2, h * 66:h * 66 + 64], scalar=o.coef[:, 8 + h:9 + h], in1=ysl, op0=ALU.mult, op1=ALU.add),
                 reads=["oall%d_2" % g, ck, yk + "_%d" % h], writes=["ybf%d" % g])
        yield
        for c in range(2):
            P.op("pe", lambda E, c=c: E.transpose(out=pT2[:, c, :], in_=o.ybf[:, c * 128:(c + 1) * 128], identity=ident), reads=["ybf%d" % g, "K_ident"], writes=[sbk])
        cp("act", mixt[:, 4 + 2 * g:6 + 2 * g, :], pT2, [sbk], ["mixt%d" % b])
        yield

    def stage_a(t):
        b = t % 2
        sgall = sgalls[b]
        mixt = mixts[b]
        ts = slice(t * 128, (t + 1) * 128)
        P.dma("sp", hTt[b], hT_d[:, :, ts], reads=["hT_d%d" % t], writes=["hTt%d" % b])
        P.dma("sp", cs[b][:, 0, :], cd["cos"][:, t, :], writes=["cs%d" % b])
        P.dma("sp", cs[b][:, 1, :], cd["sin"][:, t, :], writes=["cs%d" % b])
        P.dma("sp", ccs[b][:, 0, :], cd["ccos"][:, t, :], writes=["ccs%d" % b])
        P.dma("sp", ccs[b][:, 1, :], cd["csin"][:, t, :], writes=["ccs%d" % b])
        chunks = [(0, 512, u[b], "u%d" % b), (512, 512, pj[:, 0:512], "pj"), (1024, 512, pj[:, 512:1024], "pj"), (1536, 280, pj[:, 1024:1304], "pj")]
        for ci, (c0, wd, dst, dk) in enumerate(chunks):
            bank = 6 + (ci % 2)
            for kc in range(8):
                P.op("pe", lambda E, kc=kc, c0=c0, wd=wd, bank=bank: E.matmul(ps[bank][:, 0:wd], lhsT=hTt[b][:, kc, :], rhs=win[:, kc, c0:c0 + wd],
                                                                               start=(kc == 0), stop=(kc == 7)),
                     reads=["hTt%d" % b, "win"], writes=["ps%d" % bank])
            cp("act" if ci % 2 == 0 else "dve", dst, ps[bank][:, 0:wd], ["ps%d" % bank], [dk])
            yield
        qk = pj[:, 0:768].rearrange("p (s d) -> p s d", d=64)
        tt("dve", sq, qk, qk, ALU.mult, ["pj"], ["sq"])
        P.op("dve", lambda E: E.tensor_reduce(out=st, in_=sq, axis=AX.X, op=ALU.add), reads=["sq"], writes=["st"])
        P.op("act", lambda E: E.activation(out=st, in_=st, func=ACTF.Ln, scale=1.0 / 64, bias=EPS), reads=["st"], writes=["st"])
        P.op("act", lambda E: E.activation(out=st, in_=st, func=ACTF.Exp, scale=-0.5), reads=["st"], writes=["st"])
        cp("pool", tball[:, :, 6:8, :], pj[:, 768:1024].rearrange("p (g s d) -> p g s d", g=2, d=64), ["pj"], ["tb"])
        cp("pool", Vall[:, :, t, :, 0:64], pj[:, 1024:1280].rearrange("p (g s d) -> p g s d", g=2, d=64), ["pj"], ["V"])
        P.op("act", lambda E: E.activation(out=sgall, in_=pj[:, 1280:1304], func=ACTF.Exp, scale=-1.0), reads=["pj"], writes=["sg%d" % b])
        yield
        tt("dve", xn, qk, st.unsqueeze(2).to_broadcast([128, 12, 64]), ALU.mult, ["pj", "st"], ["xn"])
        P.op("pool", lambda E: E.tensor_scalar_add(out=sgall, in0=sgall, scalar1=1.0), reads=["sg%d" % b], writes=["sg%d" % b])
        tt("dve", xn, xn, gn[:, 0:12, :], ALU.mult, ["xn", "gn"], ["xn"])
        P.op("dve", lambda E: E.reciprocal(out=sgall, in_=sgall), reads=["sg%d" % b], writes=["sg%d" % b])
        yield
        cosb = cs[b][:, 0, :].unsqueeze(1).to_broadcast([128, 12, 32])
        sinb = cs[b][:, 1, :].unsqueeze(1).to_broadcast([128, 12, 32])
        x1 = xn[:, :, 0:32]
        x2 = xn[:, :, 32:64]
        ck = "cs%d" % b
        tt("dve", ra, x1, cosb, ALU.mult, ["xn", ck], ["ra"])
        tt("pool", rb, x2, sinb, ALU.mult, ["xn", ck], ["rb"])
        tt("dve", rc, x2, cosb, ALU.mult, ["xn", ck], ["rc"])
        tt("pool", rd, x1, sinb, ALU.mult, ["xn", ck], ["rd"])
        yield
        g4 = lambda a: a.rearrange("p (g s) d -> p g s d", g=2)
        tt("dve", tball[:, :, 0:6, 0:32], g4(ra), g4(rb), ALU.subtract, ["ra", "rb"], ["tb"])
        tt("pool", tball[:, :, 0:6, 32:64], g4(rc), g4(rd), ALU.add, ["rc", "rd"], ["tb"])
        yield
        psP = ps[6]
        for gp in range(4):
            kind = 2 if t == 0 else 0
            P.op("pe", lambda E, gp=gp, kind=kind: E.matmul(psP[:, gp * 128:(gp + 1) * 128], lhsT=u[b][:, gp * 128:(gp + 1) * 128], rhs=bands[:, kind, gp, :],
                                                            start=True, stop=(t == 0), skip_group_check=True),
                 reads=["u%d" % b, "bands"], writes=["ps6"])
            if t > 0:
                P.op("pe", lambda E, gp=gp: E.matmul(psP[:, gp * 128:(gp + 1) * 128], lhsT=u[1 - b][:, gp * 128:(gp + 1) * 128], rhs=bands[:, 1, gp, :],
                                                     start=False, stop=True, skip_group_check=True),
                     reads=["u%d" % (1 - b), "bands"], writes=["ps6"])
        cp("act", pooledT, psP[:, :], ["ps6"], ["pooledT"])
        yield
        psY = ps[7]
        for gp in range(4):
            P.op("pe", lambda E, gp=gp: E.matmul(psY[:, gp * 128:(gp + 1) * 128], lhsT=pw[:, gp, :], rhs=pooledT[:, gp * 128:(gp + 1) * 128], start=True, stop=True),
                 reads=["pw", "pooledT"], writes=["ps7"])
        for gp in range(4):
            P.op("act", lambda E, gp=gp: E.activation(out=mixt[:, gp, :], in_=psY[:, gp * 128:(gp + 1) * 128], func=ACTF.Identity, scale=psc[:, gp:gp + 1]),
                 reads=["ps7", "psc"], writes=["mixt%d" % b])
    def run_streams(gens):
        while gens:
            for gen in list(gens):
                try:
                    next(gen)
                except StopIteration:
                    gens.remove(gen)

    run_streams([stage_a(0)])
    for t in range(nt):
        b = t % 2
        ts = slice(t * 128, (t + 1) * 128)
        gens = [group_stream(0, t, b), group_stream(1, t, b)]
        if t + 1 < nt:
            gens.append(stage_a(t + 1))
        run_streams(gens)
        P.dma("sp", mixT_d[:, :, ts], mixts[b], reads=["mixt%d" % b], writes=["mixT_d%d" % t])


def _perm_in_cols():
    o0, o1 = 512, 1024
    o2 = o1 + 768

    def kvc(g, kvi, br):
        st = o1 + ((kvi * 3 + br) * 2 + g) * 64
        return list(range(st, st + 64))
    cols = list(range(512))
    for g in range(2):
        for h in range(4):
            cols += list(range(o0 + (g * 4 + h) * 64, o0 + (g * 4 + h + 1) * 64))
        cols += kvc(g, 0, 1) + kvc(g, 0, 2)
    for g in range(2):
        cols += kvc(g, 0, 0) + kvc(g, 1, 0)
    for g in range(2):
        cols += kvc(g, 1, 1) + kvc(g, 1, 2)
    cols += list(range(o2, o2 + 24))
    assert len(cols) == INC
    return np.array(cols)


_PROG_CACHE = {}


def _get_prog(n_layers, debug=False):
    key = (n_layers, debug)
    if key not in _PROG_CACHE:
        _PROG_CACHE[key] = build_program(n_layers, debug=debug)
    return _PROG_CACHE[key]


def _layer_inputs(ls, x_b, c, w_mod, b_mod, norm1, norm2, w_in, pool_w, pool_scale, q_norm, k_norm,
                  cmp_pe, cmp_w1, cmp_w2, w_out, w_ff1, w_ff2, b, consts, perm):
    f = np.float32
    ls = list(ls)
    n = len(ls)

    def col(v):
        return np.ascontiguousarray(v.reshape(8, 128).T)
    d = {}
    d["x"] = np.ascontiguousarray(x_b, dtype=f)
    d["c_col"] = col(np.asarray(c[b], f))
    d["w_mod"] = np.ascontiguousarray(w_mod[ls], dtype=f)
    d["b_mod"] = np.ascontiguousarray(b_mod[ls], dtype=f).reshape(n, 1, 6 * D)
    d["n1c"] = np.stack([col(norm1[l]) for l in ls]).astype(f)
    d["n2c"] = np.stack([col(norm2[l]) for l in ls]).astype(f)
    d["w_in"] = np.ascontiguousarray(w_in[ls][:, :, perm], dtype=f)
    d["pool_w"] = np.ascontiguousarray(np.transpose(pool_w[ls], (0, 2, 1, 3)), dtype=f)
    d["pscale"] = np.ascontiguousarray(np.transpose(pool_scale[ls].reshape(n, 4, 128), (0, 2, 1)), dtype=f)
    gains = np.zeros((n, 128, 13, 64), f)
    for i, l in enumerate(ls):
        for g in range(2):
            gains[i, :, 6 * g:6 * g + 4, :] = q_norm[l][None, None, :]
            gains[i, :, 6 * g + 4, :] = k_norm[l, 1][None, :]
            gains[i, :, 6 * g + 5, :] = k_norm[l, 2][None, :]
        gains[i, :, 12, :] = k_norm[l, 0][None, :]
    d["gains"] = gains
    w1 = cmp_w1[ls].reshape(n, 2, 32, 64, 64)
    d["w1"] = np.ascontiguousarray(np.transpose(w1, (0, 3, 1, 2, 4)).reshape(n, 64, 2 * 32 * 64), dtype=f)
    d["w2"] = np.ascontiguousarray(np.transpose(cmp_w2[ls], (0, 2, 1, 3)).reshape(n, 64, 128), dtype=f)
    d["peT"] = np.ascontiguousarray(np.transpose(cmp_pe[ls], (0, 3, 1, 2)).reshape(n, 64, 64), dtype=f)
    d["w_out"] = np.ascontiguousarray(w_out[ls], dtype=f)
    d["w_ff1"] = np.ascontiguousarray(w_ff1[ls], dtype=f)
    d["w_ff2"] = np.ascontiguousarray(w_ff2[ls], dtype=f)
    for name, shape, _, _ in _CONST_SPECS:
        d["k_" + name] = np.ascontiguousarray(consts[name], dtype=f).reshape(shape)
    return d


LAYERS_PER_LAUNCH = 4


def kernel(x, c, w_mod, b_mod, norm1, norm2, w_in, pool_w, pool_scale, q_norm, k_norm,
           cmp_pe, cmp_w1, cmp_w2, w_out, w_ff1, w_ff2):
    args = [np.asarray(a) for a in (c, w_mod, b_mod, norm1, norm2, w_in, pool_w, pool_scale, q_norm, k_norm,
                                    cmp_pe, cmp_w1, cmp_w2, w_out, w_ff1, w_ff2)]
    x = np.asarray(x, np.float32)
    consts = _consts()
    perm = _perm_in_cols()
    nc = _get_prog(LAYERS_PER_LAUNCH)
    xs = [x[b] for b in range(B)]
    for l0 in range(0, L, LAYERS_PER_LAUNCH):
        ls = range(l0, l0 + LAYERS_PER_LAUNCH)
        maps = [_layer_inputs(ls, xs[b], *args, b, consts, perm) for b in range(B)]
        in_maps = [maps[i % B] for i in range(NCORES)]
        res = run_bass_kernel_spmd(nc, in_maps, core_ids=list(range(NCORES)))
        xs = [np.asarray(res.results[b]["y"], np.float32) for b in range(B)]
    return np.stack(xs, 0).astype(np.float32)
```

```python
import numpy as np
import ml_dtypes
import concourse.bass as bass
import concourse.mybir as mybir
from concourse.bass_utils import run_bass_kernel_spmd

F32 = mybir.dt.float32
BF16 = mybir.dt.bfloat16
ALU = mybir.AluOpType
ACTF = mybir.ActivationFunctionType
AX = mybir.AxisListType

NCORES = 8
B, S, D, L = 4, 4096, 1024, 4
NT = S // 128
DFF = 4096
EPS = 1e-6
GW = 652
INC = 512 + 2 * GW
POOL_WINDOWS = (2, 4, 8, 16)
ENGS = ("pe", "act", "dve", "pool", "sp")


class _Rec:
    def __getattr__(self, name):
        def f(*a, **k):
            self.call = (name, a, k)
            return self
        return f


class Prog:
    def __init__(self, nc, n_dma_sems=32):
        self.nc = nc
        self.q = {e: [] for e in ENGS}
        self.cnt = {e: 0 for e in ENGS}
        self.sems = {}
        self._ctx = []
        for e in ENGS:
            cm = nc.semaphore("s_" + e)
            self.sems[e] = cm.__enter__()
            self._ctx.append(cm)
        self.n_dma = n_dma_sems
        self.dma_uses = [0] * n_dma_sems
        self.dma_rr = 0
        for i in range(n_dma_sems):
            cm = nc.semaphore("s_dma%d" % i)
            self.sems["dma%d" % i] = cm.__enter__()
            self._ctx.append(cm)
        self.waited = {e: {} for e in ENGS}
        self.last_w = {}
        self.readers = {}
        self.ninst = 0

    def close(self):
        for cm in reversed(self._ctx):
            cm.__exit__(None, None, None)

    def _deps(self, eng, reads, writes):
        need = {}

        def add(tok):
            s, v = tok
            if eng == "pe" and s == "pe":
                return
            if need.get(s, 0) < v:
                need[s] = v
        for k in reads:
            if k in self.last_w:
                add(self.last_w[k])
        for k in writes:
            if k in self.last_w:
                add(self.last_w[k])
            for tok in self.readers.get(k, ()):
                add(tok)
        out = []
        w = self.waited[eng]
        for s, v in need.items():
            if w.get(s, 0) < v:
                w[s] = v
                out.append((s, v))
        return out

    def _commit(self, tok, reads, writes):
        for k in writes:
            self.last_w[k] = tok
            self.readers[k] = []
        for k in reads:
            if k in writes:
                continue
            self.readers.setdefault(k, []).append(tok)

    def op(self, eng, fn, reads=(), writes=()):
        waits = self._deps(eng, reads, writes)
        self.cnt[eng] += 1
        tok = (eng, self.cnt[eng])
        sems = self.sems
        rec = _Rec()
        fn(rec)
        name, a, k = rec.call

        def run(E, waits=waits, s=sems[eng], name=name, a=a, k=k):
            for (ws, wv) in waits:
                E.wait_ge(sems[ws], wv)
            getattr(E, name)(*a, **k).then_inc(s, 1)
        self.q[eng].append(run)
        self._commit(tok, reads, writes)
        self.ninst += 1 + len(waits)
        return tok

    def dma(self, eng, out, in_, reads=(), writes=(), **kw):
        i = self.dma_rr
        self.dma_rr = (self.dma_rr + 1) % self.n_dma
        sname = "dma%d" % i
        waits = self._deps(eng, reads, writes)
        prev = 16 * self.dma_uses[i]
        if prev and self.waited[eng].get(sname, 0) < prev:
            self.waited[eng][sname] = prev
            waits.append((sname, prev))
        self.dma_uses[i] += 1
        tok = (sname, 16 * self.dma_uses[i])
        sems = self.sems

        def run(E, waits=waits, s=sems[sname]):
            for (ws, wv) in waits:
                E.wait_ge(sems[ws], wv)
            E.dma_start(out=out, in_=in_, **kw).then_inc(s, 16)
        self.q[eng].append(run)
        self._commit(tok, reads, writes)
        self.ninst += 1 + len(waits)
        return tok

    def barrier(self):
        waits = []
        for e in ENGS:
            if self.cnt[e]:
                waits.append((e, self.cnt[e]))
        for i in range(self.n_dma):
            if self.dma_uses[i]:
                waits.append(("dma%d" % i, 16 * self.dma_uses[i]))
        sems = self.sems
        for e in ENGS:
            mine = [(s, v) for (s, v) in waits if self.waited[e].get(s, 0) < v and not (s == e and e == "pe")]
            for (s, v) in mine:
                self.waited[e][s] = v

            def run(E, mine=mine):
                for (ws, wv) in mine:
                    E.wait_ge(sems[ws], wv)
            self.q[e].append(run)
        self.last_w = {}
        self.readers = {}

    def emit(self):
        nc = self.nc
        q = self.q
        with nc.Block() as block:
            @block.tensor
            def _(E):
                for f in q["pe"]:
                    f(E)

            @block.scalar
            def _(E):
                for f in q["act"]:
                    f(E)

            @block.vector
            def _(E):
                for f in q["dve"]:
                    f(E)

            @block.gpsimd
            def _(E):
                for f in q["pool"]:
                    f(E)

            @block.sync
            def _(E):
                for f in q["sp"]:
                    f(E)


class Arena:
    def __init__(self, big, nbytes):
        self.big = big
        self.nbytes = nbytes
        self.off = 0
        self.marks = []

    def alloc(self, shape, dtype, parts=128):
        n = int(np.prod(shape))
        esz = 4 if dtype == F32 else 2
        nb = (n * esz + 31) // 32 * 32
        assert self.off + nb <= self.nbytes, ("SBUF arena overflow", self.off, nb, self.nbytes)
        w0 = self.off // 4
        v = self.big[0:parts, w0:w0 + nb // 4]
        if dtype != F32:
            v = v.bitcast(dtype)
        v = v[:, 0:n]
        self.off += nb
        if len(shape) == 2:
            return v.rearrange("p (a b) -> p a b", b=shape[1])
        if len(shape) == 3:
            return v.rearrange("p (a b c) -> p a b c", b=shape[1], c=shape[2])
        return v

    def mark(self):
        self.marks.append(self.off)

    def release(self):
        self.off = self.marks.pop()


def _consts():
    bf = ml_dtypes.bfloat16
    c = {}
    c["ident"] = np.eye(128, dtype=np.float32)
    inv = (10000.0 ** (-np.arange(32, dtype=np.float32) * 2.0 / 64.0)).astype(np.float32)
    pos = (np.arange(NT)[None, :] * 128 + np.arange(128)[:, None]).astype(np.float32)
    ang = pos[:, :, None] * inv[None, None, :]
    c["cos"] = np.cos(ang).astype(np.float32)
    c["sin"] = np.sin(ang).astype(np.float32)
    m = (np.arange(NT)[None, :] * 8 + np.arange(8)[:, None])
    cpos = (16 * m + 15).astype(np.float32)
    cang = cpos[:, :, None] * inv[None, None, :]
    c["ccos"] = np.cos(cang).astype(np.float32)
    c["csin"] = np.sin(cang).astype(np.float32)
    k = np.arange(128)[:, None]
    q = np.arange(128)[None, :]
    c["tri"] = (k <= q).astype(np.float32)
    c["triinv"] = (k > q).astype(np.float32)
    cm = np.zeros((128, 16, 128), np.float32)
    for tt in range(16):
        i = np.arange(128)[:, None] - 8 * tt
        r = np.arange(128)[None, :]
        vis = (i < 0) | ((i >= 0) & (i <= 7) & (r >= 16 * i + 15))
        cm[:, tt, :] = vis
    c["cmask"] = cm
    ex = np.zeros((64, S), np.float32)
    ex[np.arange(S) // 64, np.arange(S)] = 1.0
    c["expand"] = ex
    r = np.arange(128)[:, None]
    jj = np.arange(128)[None, :]
    cur = 64 + (r >= 64)
    keep = (jj < cur - 1).astype(np.float32)
    add = np.where((jj == cur) | (jj == cur - 1), 1e4, np.where(jj > cur, -1.0, 0.0)).astype(np.float32)
    c["keepB"] = keep
    c["addB"] = add
    n_cmp = (S - 32) // 16 + 1
    cs0 = np.arange(n_cmp) * 16
    ss0 = np.arange(64) * 64
    ov = np.minimum(cs0[:, None] + 32, ss0[None, :] + 64) - np.maximum(cs0[:, None], ss0[None, :])
    ov = np.clip(ov, 0, None).astype(np.float32) / 32.0
    ovs = np.zeros((256, 64), np.float32)
    ovs[1:1 + n_cmp] = ov
    c["ov"] = ovs.reshape(2, 128, 64).transpose(1, 0, 2).copy()
    bands = np.zeros((128, 3, 4, 128), np.float32)
    s = np.arange(128)[:, None]
    t = np.arange(128)[None, :]
    for gi, w in enumerate(POOL_WINDOWS):
        main = ((s <= t) & (s > t - w)).astype(np.float32) / w - (s == t)
        corner = (s >= 129 + t - w).astype(np.float32) / w
        cnt = np.minimum(t + 1, w).astype(np.float32)
        first = ((s <= t) & (s > t - w)).astype(np.float32) / cnt - (s == t)
        bands[:, 0, gi, :] = main
        bands[:, 1, gi, :] = corner
        bands[:, 2, gi, :] = first
    c["bands"] = bands
    c["ones"] = np.ones((128, 128), np.float32)
    return c


_CONST_SPECS = [
    ("ident", [128, 128], BF16, 128), ("cos", [128, NT, 32], F32, 128), ("sin", [128, NT, 32], F32, 128),
    ("ccos", [8, NT, 32], F32, 8), ("csin", [8, NT, 32], F32, 8),
    ("tri", [128, 128], BF16, 128), ("triinv", [128, 128], BF16, 128),
    ("cmask", [128, 16, 128], BF16, 128), ("expand", [64, S], BF16, 64),
    ("keepB", [128, 128], F32, 128), ("addB", [128, 128], F32, 128),
    ("ov", [128, 2, 64], BF16, 128), ("bands", [128, 3, 4, 128], F32, 128), ("ones", [128, 128], F32, 128),
]


def build_program(n_layers, first_layer_norm=True, debug=False, nt=NT, stop=None, mstage=99):
    nc = bass.Bass("TRN2", target_bir_lowering=False)
    LW = n_layers

    def din(name, shape, dt=F32):
        return nc.dram_tensor(name, shape, dt, kind="ExternalInput").ap()

    x_in = din("x", [S, D])
    c_col = din("c_col", [128, 8])
    w_mod = din("w_mod", [LW, D, 6 * D])
    b_mod = din("b_mod", [LW, 1, 6 * D])
    n1c = din("n1c", [LW, 128, 8])
    n2c = din("n2c", [LW, 128, 8])
    w_in = din("w_in", [LW, D, INC])
    pool_w = din("pool_w", [LW, 128, 4, 128])
    pscale = din("pscale", [LW, 128, 4])
    gains = din("gains", [LW, 128, 13, 64])
    w1 = din("w1", [LW, 64, 2 * 32 * 64])
    w2 = din("w2", [LW, 64, 2 * 64])
    peT = din("peT", [LW, 64, 2 * 32])
    w_out = din("w_out", [LW, D, D])
    w_ff1 = din("w_ff1", [LW, D, DFF])
    w_ff2 = din("w_ff2", [LW, DFF, D])
    cd = {name: din("k_" + name, shape) for (name, shape, _, _) in _CONST_SPECS}
    y_out = nc.dram_tensor("y", [S, D], F32, kind="ExternalOutput").ap()
    okind = dict(kind="ExternalOutput") if debug else {}
    hT_d = nc.dram_tensor("hT_d", [128, 8, S], BF16, **okind).ap()
    mixT_d = nc.dram_tensor("mixT_d", [128, 8, S], BF16, **okind).ap()
    xd = nc.dram_tensor("xd", [S, D], F32).ap()

    P = Prog(nc)
    ARENA_BYTES = 196 * 1024
    big_cm = nc.sbuf_tensor("arena", [128, ARENA_BYTES // 4], F32)
    big = big_cm.__enter__()
    A = Arena(big, ARENA_BYTES)
    ps_cms = [nc.psum_tensor("ps%d" % i, [128, 512], F32) for i in range(8)]
    ps = [cm.__enter__() for cm in ps_cms]

    def psv(i, shape, dtype=F32, parts=128):
        v = ps[i][0:parts, :]
        if dtype != F32:
            v = v.bitcast(dtype)
        n = int(np.prod(shape))
        v = v[:, 0:n]
        if len(shape) == 2:
            return v.rearrange("p (a b) -> p a b", b=shape[1])
        if len(shape) == 3:
            return v.rearrange("p (a b c) -> p a b c", b=shape[1], c=shape[2])
        return v

    K = {}
    for (name, shape, dt, parts) in _CONST_SPECS:
        if name in ("expand", "cmask", "cos", "sin", "ccos", "csin", "bands", "keepB", "addB", "ov", "tri", "triinv"):
            continue
        K[name] = A.alloc(shape[1:], dt, parts)
    ident = K["ident"]
    ones = K["ones"]
    cact = A.alloc([8], F32)
    modcols = A.alloc([4, 8], F32)
    s1c = A.alloc([8], F32)
    s2c = A.alloc([8], F32)
    n1t = A.alloc([8], F32)
    n2t = A.alloc([8], F32)
    gate1 = A.alloc([D], F32)
    gate2 = A.alloc([D], F32)
    small = A.alloc([64], F32)
    junk = A.alloc([D], BF16)

    def load_const(name, eng="pool"):
        for (nm, shape, dt, parts) in _CONST_SPECS:
            if nm == name:
                src = cd[name]
                dst = K[name]
                P.dma(eng if dt != F32 else "sp", dst, src, writes=["K_" + name])

    for nm in K:
        load_const(nm)
    P.dma("sp", cact, c_col, writes=["cact"])
    P.op("act", lambda E: E.activation(out=small[:, 0:8], in_=cact, func=ACTF.Exp, scale=-1.0), reads=["cact"], writes=["small"])
    P.op("dve", lambda E: E.tensor_scalar_add(out=small[:, 0:8], in0=small[:, 0:8], scalar1=1.0), reads=["small"], writes=["small"])
    P.op("dve", lambda E: E.reciprocal(out=small[:, 0:8], in_=small[:, 0:8]), reads=["small"], writes=["small"])
    P.op("dve", lambda E: E.tensor_tensor(out=cact, in0=cact, in1=small[:, 0:8], op=ALU.mult), reads=["small", "cact"], writes=["cact"])

    def rstd_from_ss(ss_ap, n, key, scale):
        P.op("act", lambda E: E.activation(out=ss_ap, in_=ss_ap, func=ACTF.Ln, scale=scale, bias=EPS), reads=[key], writes=[key])
        P.op("act", lambda E: E.activation(out=ss_ap, in_=ss_ap, func=ACTF.Exp, scale=-0.5), reads=[key], writes=[key])

    def norm_to_hT(xt, xkey, hT, hkey, scol, bcol, colkeys, xh, tag):
        ss = small[:, 32:33]
        P.op("act", lambda E: E.activation(out=junk, in_=xt, func=ACTF.Square, accum_out=ss), reads=[xkey], writes=["junk", "ss"])
        rstd_from_ss(ss, 1, "ss", 1.0 / D)
        P.op("dve", lambda E: E.tensor_scalar(out=xh, in0=xt, scalar1=ss, scalar2=None, op0=ALU.mult), reads=[xkey, "ss"], writes=["xh" + tag])
        pT = psv(2, [8, 128], BF16)
        for kc in range(8):
            P.op("pe", lambda E, kc=kc: E.transpose(out=pT[:, kc, :], in_=xh[:, kc * 128:(kc + 1) * 128], identity=ident),
                 reads=["xh" + tag, "K_ident"], writes=["ps2"])
        for kc in range(8):
            P.op("act", lambda E, kc=kc: E.activation(out=hT[:, kc, :], in_=pT[:, kc, :], func=ACTF.Identity,
                                                     scale=scol[:, kc:kc + 1], bias=bcol[:, kc:kc + 1]),
                 reads=["ps2"] + colkeys, writes=[hkey])

    for l in range(n_layers):
        x_src = x_in if l == 0 else xd
        x_dst = y_out if l == n_layers - 1 else xd

        P.barrier()
        A.mark()
        modrow = A.alloc([6 * D], F32, parts=1)
        bmrow = A.alloc([6 * D], F32, parts=1)
        wm = [A.alloc([8, 512], F32) for _ in range(2)]
        P.dma("sp", bmrow, b_mod[l], writes=["bmrow"])
        P.dma("sp", n1t, n1c[l], writes=["n1t"])
        P.dma("sp", n2t, n2c[l], writes=["n2t"])
        for ch in range(12):
            buf = wm[ch % 2]
            P.dma("sp", buf, w_mod[l][:, ch * 512:(ch + 1) * 512].rearrange("(k p) n -> p k n", p=128), writes=["wm%d" % (ch % 2)])
            pr = ps[ch % 2][0:1, :]
            for kc in range(8):
                P.op("pe", lambda E, kc=kc, buf=buf, pr=pr: E.matmul(pr, lhsT=cact[:, kc:kc + 1], rhs=buf[:, kc, :], start=(kc == 0), stop=(kc == 7)),
                     reads=["cact", "wm%d" % (ch % 2)], writes=["ps%d" % (ch % 2)])
            P.op("dve", lambda E, ch=ch, pr=pr: E.tensor_tensor(out=modrow[:, ch * 512:(ch + 1) * 512], in0=pr, in1=bmrow[:, ch * 512:(ch + 1) * 512], op=ALU.add),
                 reads=["ps%d" % (ch % 2), "bmrow"], writes=["modrow"])
        pc = ps[2][:, 0:32]
        for vi, off in enumerate((0, 1024, 3072, 4096)):
            for kc in range(8):
                j = vi * 8 + kc
                P.op("pe", lambda E, j=j, off=off, kc=kc: E.matmul(pc[:, j:j + 1], lhsT=modrow[0:1, off + kc * 128: off + (kc + 1) * 128],
                                                                  rhs=ones[0:1, 0:1], start=True, stop=True),
                     reads=["modrow", "K_ones"], writes=["ps2"])
        P.op("dve", lambda E: E.tensor_copy(out=modcols.rearrange("p a b -> p (a b)"), in_=pc), reads=["ps2"], writes=["modcols"])
        P.op("dve", lambda E: E.scalar_tensor_tensor(out=s1c, in0=modcols[:, 1, :], scalar=1.0, in1=n1t, op0=ALU.add, op1=ALU.mult),
             reads=["modcols", "n1t"], writes=["s1c"])
        P.op("dve", lambda E: E.scalar_tensor_tensor(out=s2c, in0=modcols[:, 3, :], scalar=1.0, in1=n2t, op0=ALU.add, op1=ALU.mult),
             reads=["modcols", "n2t"], writes=["s2c"])
        for gi, (gt, off) in enumerate(((gate1, 2048), (gate2, 5120))):
            for h in range(2):
                pb = ps[3 + h]
                P.op("pe", lambda E, off=off, h=h, pb=pb: E.matmul(pb[:, :], lhsT=ones[0:1, :], rhs=modrow[0:1, off + h * 512: off + (h + 1) * 512], start=True, stop=True),
                     reads=["modrow", "K_ones"], writes=["ps%d" % (3 + h)])
                P.op("dve", lambda E, gt=gt, h=h, pb=pb: E.tensor_copy(out=gt[:, h * 512:(h + 1) * 512], in_=pb[:, :]),
                     reads=["ps%d" % (3 + h)], writes=["gate%d" % gi])
        b1c = modcols[:, 0, :]
        b2c = modcols[:, 2, :]
        P.barrier()
        A.release()
        if stop == "mod":
            break

        if True:
            A.mark()
            xts = [A.alloc([D], F32) for _ in range(2)]
            xhs = [A.alloc([D], BF16) for _ in range(2)]
            hTs = [A.alloc([8, 128], BF16) for _ in range(2)]
            def n1(t):
                b = t % 2
                xt, xk = xts[b], "xt%d" % b
                P.dma("sp", xt, x_src[t * 128:(t + 1) * 128, :], reads=["x_d%d" % t], writes=[xk])
                ss = small[:, 36 + b:37 + b]
                sk = "ssn%d" % b
                P.op("act", lambda E: E.activation(out=junk, in_=xt, func=ACTF.Square, accum_out=ss), reads=[xk], writes=["junk", sk])
                rstd_from_ss(ss, 1, sk, 1.0 / D)
                P.op("dve", lambda E: E.tensor_scalar(out=xhs[b], in0=xt, scalar1=ss, scalar2=None, op0=ALU.mult), reads=[xk, sk], writes=["xhn%d" % b])

            def n2(t):
                b = t % 2
                bank = 2 + b
                pT = psv(bank, [8, 128], BF16)
                for kc in range(8):
                    P.op("pe", lambda E, kc=kc: E.transpose(out=pT[:, kc, :], in_=xhs[b][:, kc * 128:(kc + 1) * 128], identity=ident),
                         reads=["xhn%d" % b, "K_ident"], writes=["ps%d" % bank])
                for kc in range(8):
                    P.op("act", lambda E, kc=kc: E.activation(out=hTs[b][:, kc, :], in_=pT[:, kc, :], func=ACTF.Identity,
                                                             scale=s1c[:, kc:kc + 1], bias=b1c[:, kc:kc + 1]),
                         reads=["ps%d" % bank, "s1c", "modcols"], writes=["hT%d" % b])
                P.dma("sp", hT_d[:, :, t * 128:(t + 1) * 128], hTs[b], reads=["hT%d" % b], writes=["hT_d%d" % t])

            n1(0)
            for t in range(nt):
                if t + 1 < nt:
                    n1(t + 1)
                n2(t)
            P.barrier()
            A.release()
        if stop == "norm":
            break

        A.mark()
        mixer_phase(nc, P, A, ps, psv, l, nt, K, cd, dict(
            w_in=w_in, pool_w=pool_w, pscale=pscale, gains=gains, w1=w1, w2=w2, peT=peT,
            hT_d=hT_d, mixT_d=mixT_d, ident=ident, ones=ones, small=small, mstage=mstage))
        P.barrier()
        A.release()
        if stop == "mixer":
            break

        A.mark()
        wo = A.alloc([8, D], BF16)
        f1 = A.alloc([8, DFF], BF16)
        f2 = A.alloc([32, D], BF16)
        for kc in range(8):
            P.dma("pool", wo[:, kc, :], w_out[l][kc * 128:(kc + 1) * 128, :], writes=["wo"])
        for kc in range(8):
            for hh in range(2):
                P.dma("pool", f1[:, kc, hh * 2048:(hh + 1) * 2048], w_ff1[l][kc * 128:(kc + 1) * 128, hh * 2048:(hh + 1) * 2048], writes=["f1"])
        for c4 in range(8):
            P.dma("pool", f2[:, c4 * 4:(c4 + 1) * 4, :], w_ff2[l][c4 * 512:(c4 + 1) * 512, :].rearrange("(c p) n -> p c n", p=128), writes=["f2"])
        mts = [A.alloc([8, 128], BF16) for _ in range(2)]
        xts = [A.alloc([D], F32) for _ in range(2)]
        xhs = [A.alloc([D], BF16) for _ in range(2)]
        h2Ts = [A.alloc([8, 128], BF16) for _ in range(2)]
        aT = A.alloc([32, 128], BF16)
        rl = [A.alloc([512], F32) for _ in range(2)]
        tmp = A.alloc([512], F32)

        def ffn_a1(t):
            b = t % 2
            xt = xts[b]
            xk = "xt%d" % b
            P.dma("sp", mts[b], mixT_d[:, :, t * 128:(t + 1) * 128], reads=["mixT_d%d" % t], writes=["mt%d" % b])
            P.dma("sp", xt, x_src[t * 128:(t + 1) * 128, :], reads=["x_d%d" % t], writes=[xk])
            for h in range(2):
                for kc in range(8):
                    P.op("pe", lambda E, h=h, kc=kc, b=b: E.matmul(ps[h][:, :], lhsT=mts[b][:, kc, :], rhs=wo[:, kc, h * 512:(h + 1) * 512], start=(kc == 0), stop=(kc == 7)),
                         reads=["mt%d" % b, "wo"], writes=["ps%d" % h])
                P.op("dve", lambda E, h=h: E.tensor_tensor(out=tmp, in0=ps[h][:, :], in1=gate1[:, h * 512:(h + 1) * 512], op=ALU.mult),
                     reads=["ps%d" % h, "gate0"], writes=["tmp"])
                P.op("dve", lambda E, h=h, xt=xt: E.tensor_tensor(out=xt[:, h * 512:(h + 1) * 512], in0=xt[:, h * 512:(h + 1) * 512], in1=tmp, op=ALU.add),
                     reads=["tmp", xk], writes=[xk])
            ss = small[:, 34 + b:35 + b]
            sk = "ssf%d" % b
            P.op("act", lambda E: E.activation(out=junk, in_=xt, func=ACTF.Square, accum_out=ss), reads=[xk], writes=["junk", sk])
            rstd_from_ss(ss, 1, sk, 1.0 / D)
            P.op("dve", lambda E: E.tensor_scalar(out=xhs[b], in0=xt, scalar1=ss, scalar2=None, op0=ALU.mult), reads=[xk, sk], writes=["xhf%d" % b])

        def ffn_a2(t):
            b = t % 2
            pT = psv(2, [8, 128], BF16)
            for kc in range(8):
                P.op("pe", lambda E, kc=kc: E.transpose(out=pT[:, kc, :], in_=xhs[b][:, kc * 128:(kc + 1) * 128], identity=ident),
                     reads=["xhf%d" % b, "K_ident"], writes=["ps2"])
            for kc in range(8):
                P.op("act", lambda E, kc=kc: E.activation(out=h2Ts[b][:, kc, :], in_=pT[:, kc, :], func=ACTF.Identity,
                                                         scale=s2c[:, kc:kc + 1], bias=b2c[:, kc:kc + 1]),
                     reads=["ps2", "s2c", "modcols"], writes=["h2T%d" % b])

        def ffn_b1(t):
            b = t % 2
            h2T = h2Ts[b]
            for c4 in range(8):
                pf = ps[3 + (c4 % 2)]
                for cc in range(4):
                    c = c4 * 4 + cc
                    for kc in range(8):
                        P.op("pe", lambda E, c=c, cc=cc, kc=kc, pf=pf: E.matmul(pf[:, cc * 128:(cc + 1) * 128], lhsT=f1[:, kc, c * 128:(c + 1) * 128], rhs=h2T[:, kc, :],
                                                                               start=(kc == 0 and cc == 0), stop=(kc == 7 and cc == 3), skip_group_check=True),
                             reads=["h2T%d" % b, "f1"], writes=["ps%d" % (3 + c4 % 2)])
                r = rl[c4 % 2]
                P.op("act", lambda E, pf=pf, r=r: E.activation(out=r, in_=pf[:, :], func=ACTF.Relu), reads=["ps%d" % (3 + c4 % 2)], writes=["rl%d" % (c4 % 2)])
                P.op("pool", lambda E, r=r, c4=c4: E.tensor_tensor(out=aT[:, c4 * 4:(c4 + 1) * 4, :].rearrange("p a b -> p (a b)"), in0=r, in1=r, op=ALU.mult),
                     reads=["rl%d" % (c4 % 2)], writes=["aT%d" % c4])

        def ffn_b2(t):
            b = t % 2
            xt = xts[b]
            xk = "xt%d" % b
            for h in range(2):
                for c in range(32):
                    P.op("pe", lambda E, h=h, c=c: E.matmul(ps[h][:, :], lhsT=aT[:, c, :], rhs=f2[:, c, h * 512:(h + 1) * 512], start=(c == 0), stop=(c == 31)),
                         reads=["aT%d" % (c // 4), "f2"], writes=["ps%d" % h])
                P.op("dve", lambda E, h=h: E.tensor_tensor(out=tmp, in0=ps[h][:, :], in1=gate2[:, h * 512:(h + 1) * 512], op=ALU.mult),
                     reads=["ps%d" % h, "gate1"], writes=["tmp"])
                P.op("dve", lambda E, h=h, xt=xt: E.tensor_tensor(out=xt[:, h * 512:(h + 1) * 512], in0=xt[:, h * 512:(h + 1) * 512], in1=tmp, op=ALU.add),
                     reads=["tmp", xk], writes=[xk])
            P.dma("sp", x_dst[t * 128:(t + 1) * 128, :], xt, reads=[xk], writes=["x_d%d" % t])

        ffn_a1(0)
        ffn_a2(0)
        for t in range(nt):
            if t + 1 < nt:
                ffn_a1(t + 1)
            ffn_b1(t)
            if t + 1 < nt:
                ffn_a2(t + 1)
            ffn_b2(t)
        P.barrier()
        A.release()
        if l < n_layers - 1:
            pass

    P.barrier()
    P.emit()
    P.close()
    for cm in reversed(ps_cms):
        cm.__exit__(None, None, None)
    big_cm.__exit__(None, None, None)
    return nc


def mixer_phase(nc, P, A, ps, psv, l, nt, K, cd, W):
    ident = W["ident"]
    hT_d, mixT_d = W["hT_d"], W["mixT_d"]

    def cp(eng, out, in_, reads, writes):
        if eng == "act":
            P.op("act", lambda E: E.activation(out=out, in_=in_, func=ACTF.Identity), reads, writes)
        else:
            P.op(eng, lambda E: E.tensor_copy(out=out, in_=in_), reads, writes)

    def tt(eng, out, a, b, op, reads, writes):
        P.op(eng, lambda E: E.tensor_tensor(out=out, in0=a, in1=b, op=op), reads, writes)

    tri = A.alloc([128], BF16)
    triinv = A.alloc([128], BF16)
    cmask = A.alloc([16, 128], BF16)
    expand = A.alloc([S], BF16, parts=64)
    keepB = A.alloc([128], F32)
    addB = A.alloc([128], F32)
    ov = A.alloc([2, 64], BF16)
    bands = A.alloc([3, 4, 128], F32)
    P.dma("pool", tri, cd["tri"], writes=["tri"])
    P.dma("pool", triinv, cd["triinv"], writes=["triinv"])
    P.dma("pool", cmask, cd["cmask"], writes=["cmask"])
    for hh in range(2):
        P.dma("pool", expand[:, hh * 2048:(hh + 1) * 2048], cd["expand"][:, hh * 2048:(hh + 1) * 2048], writes=["expand"])
    P.dma("sp", keepB, cd["keepB"], writes=["keepB"])
    P.dma("sp", addB, cd["addB"], writes=["addB"])
    P.dma("pool", ov, cd["ov"], writes=["ov"])
    P.dma("sp", bands, cd["bands"], writes=["bands"])
    win = A.alloc([8, INC], BF16)
    for kc in range(8):
        P.dma("pool", win[:, kc, :], W["w_in"][l][kc * 128:(kc + 1) * 128, :], writes=["win"])
    pw = A.alloc([4, 128], BF16)
    P.dma("pool", pw, W["pool_w"][l], writes=["pw"])
    psc = A.alloc([4], F32)
    P.dma("sp", psc, W["pscale"][l], writes=["psc"])
    gn = A.alloc([13, 64], F32)
    P.dma("sp", gn, W["gains"][l], writes=["gn"])
    W1 = A.alloc([2, 32, 64], BF16, parts=64)
    for kv in range(2):
        P.dma("pool", W1[:, kv, :, :].rearrange("p a b -> p (a b)"), W["w1"][l][:, kv * 2048:(kv + 1) * 2048], writes=["W1"])
    W2 = A.alloc([2, 64], BF16, parts=64)
    P.dma("pool", W2.rearrange("p a b -> p (a b)"), W["w2"][l], writes=["W2"])
    peT = A.alloc([2, 32], BF16, parts=64)
    P.dma("pool", peT.rearrange("p a b -> p (a b)"), W["peT"][l], writes=["peT"])
    cbias = A.alloc([2], F32, parts=64)
    pb = ps[0][0:64, 0:2]
    for kv in range(2):
        for p in range(32):
            P.op("pe", lambda E, kv=kv, p=p: E.matmul(pb[:, kv:kv + 1], lhsT=W1[:, kv, p, :], rhs=peT[:, kv, p:p + 1], start=(p == 0), stop=(p == 31)),
                 reads=["W1", "peT"], writes=["ps0"])
    cp("dve", cbias, pb, ["ps0"], ["cbias"])
    kT = [A.alloc([2, S], BF16, parts=64) for _ in range(2)]
    craw = [A.alloc([2, 144], BF16, parts=64) for _ in range(2)]
    Vall = A.alloc([2 * nt * 2, 66], BF16).rearrange("p (g t k) c -> p g t k c", g=2, k=2)
    kTc = [A.alloc([256], BF16, parts=64) for _ in range(2)]
    Vc = [A.alloc([2, 66], BF16) for _ in range(2)]
    P.op("pool", lambda E: E.memset(Vall, 1.0), writes=["V"])
    for g in range(2):
        P.op("pool", lambda E, g=g: E.memset(Vc[g], 0.0), writes=["Vc%d" % g])
        P.op("pool", lambda E, g=g: E.memset(Vc[g][:, :, 64:65], 1.0), writes=["Vc%d" % g])
        P.op("pool", lambda E, g=g: E.memset(Vc[g][0:1, 0, 64:65], 0.0), writes=["Vc%d" % g])
        P.op("pool", lambda E, g=g: E.memset(kTc[g], 0.0), writes=["kTc%d" % g])
        P.op("pool", lambda E, g=g: E.memset(craw[g], 0.0), writes=["craw%d" % g])
    hTt = [A.alloc([8, 128], BF16) for _ in range(2)]
    cs = [A.alloc([2, 32], F32) for _ in range(2)]
    ccs = [A.alloc([2, 32], F32, parts=8) for _ in range(2)]
    u = [A.alloc([512], F32) for _ in range(2)]
    pj = A.alloc([2 * GW], F32)
    sq = A.alloc([12, 64], F32)
    st = A.alloc([12], F32)
    xn = A.alloc([12, 64], F32)
    ra = A.alloc([12, 32], F32)
    rb = A.alloc([12, 32], F32)
    rc = A.alloc([12, 32], F32)
    rd = A.alloc([12, 32], F32)
    tball = A.alloc([2, 9, 64], BF16)
    sgalls = [A.alloc([24], F32) for _ in range(2)]
    mixts = [A.alloc([8, 128], BF16) for _ in range(2)]
    pooledT = A.alloc([512], BF16)
    P.op("pool", lambda E: E.memset(tball, 0.0), writes=["tb"])

    class GB:
        pass
    G = []
    for g in range(2):
        o = GB()
        o.qT = A.alloc([512], BF16, parts=64)
        o.Eb = [A.alloc([512], BF16) for _ in range(2)]
        o.msk = A.alloc([128], BF16)
        o.oall = A.alloc([3, 264], F32)
        o.rden = A.alloc([12], F32)
        o.coef = A.alloc([12], F32)
        o.imp = A.alloc([64], F32)
        o.imp2 = A.alloc([64], F32)
        o.wk = A.alloc([64], F32)
        o.m8 = A.alloc([16], F32)
        o.thr = A.alloc([1], F32)
        o.selb = A.alloc([128], BF16)
        o.selT = A.alloc([128], BF16, parts=64)
        o.y32 = A.alloc([256], F32)
        o.ybf = A.alloc([256], BF16)
        o.zc = A.alloc([16], F32)
        o.ec = A.alloc([16], F32)
        o.sTc = A.alloc([16], BF16, parts=64)
        o.k8 = A.alloc([64], F32, parts=8)
        o.k8q = A.alloc([64], F32, parts=8)
        o.k8s = A.alloc([4], F32)
        o.k8r = [A.alloc([32], F32, parts=8) for _ in range(4)]
        o.k8b = A.alloc([128], BF16)
        o.v8 = A.alloc([64], BF16, parts=8)
        P.op("pool", lambda E, o=o: E.memset(o.zc, 0.0), writes=["zc%d" % g])
        P.op("pool", lambda E, o=o: E.memset(o.k8s, 1.0), writes=["k8s%d" % g])
        P.op("pool", lambda E, o=o: E.memset(o.selb, 0.0), writes=["selb%d" % g])
        P.op("pool", lambda E, o=o: E.memset(o.k8b, 0.0), writes=["k8b%d" % g])
        G.append(o)

    def bc_h(ap128):
        return ap128.unsqueeze(1).to_broadcast([128, 4, 128])

    def e4(buf):
        return buf.rearrange("p (h q) -> p h q", h=4)

    def attention_branch(g, t, br, kts, kTsrc, vsrc, vkey, masks, use_sel):
        o = G[g]
        B0 = 3 * g
        sbank, obank, ibank, mbank = B0, B0 + 1, B0 + 2, B0 + 2
        psO = ps[obank]
        psI = ps[ibank]
        nk = len(kts)

        def scores(i):
            kt = kts[i]
            P.op("pe", lambda E: E.matmul(ps[sbank][:, :], lhsT=kTsrc(kt), rhs=o.qT, start=True, stop=True),
                 reads=["kcache%d" % g, "kTc%d" % g, "qT%d" % g], writes=["ps%d" % sbank])
            if use_sel:
                slot = ps[mbank][:, 256:384]
                P.op("pe", lambda E: E.matmul(slot, lhsT=expand[:, kt * 128:(kt + 1) * 128], rhs=o.selT, start=True, stop=True),
                     reads=["expand", "selT%d" % g], writes=["ps%d" % mbank])
        scores(0)
        yield
        for i, kt in enumerate(kts):
            E_ = o.Eb[i % 2]
            ek = "Eb%d_%d" % (g, i % 2)
            P.op("act", lambda E, E_=E_: E.activation(out=E_, in_=ps[sbank][:, :], func=ACTF.Exp, scale=0.125),
                 reads=["ps%d" % sbank], writes=[ek])
            m = masks(kt)
            if use_sel:
                slot = ps[mbank][:, 256:384]
                sk = "ps%d" % mbank
                if m is not None:
                    tt("dve", o.msk, slot, m[0], ALU.mult, [sk, m[1]], ["msk%d" % g])
                    tt("pool", e4(E_), e4(E_), bc_h(o.msk), ALU.mult, [ek, "msk%d" % g], [ek])
                else:
                    tt("dve", e4(E_), e4(E_), bc_h(slot), ALU.mult, [ek, sk], [ek])
            elif m is not None:
                tt("pool", e4(E_), e4(E_), bc_h(m[0]), ALU.mult, [ek, m[1]], [ek])
            if i + 1 < nk:
                scores(i + 1)
            yield
            for h in range(4):
                first = (i == 0 and h == 0)
                last = (i == nk - 1 and h == 3)
                P.op("pe", lambda E, h=h, kt=kt, E_=E_, first=first, last=last: E.matmul(
                    psO[:, h * 66:(h + 1) * 66], lhsT=E_[:, h * 128:(h + 1) * 128], rhs=vsrc(kt), start=first, stop=last, skip_group_check=True),
                    reads=[ek, vkey], writes=["ps%d" % obank])
                if br == 0:
                    P.op("pe", lambda E, h=h, kt=kt, E_=E_, first=first, last=last: E.matmul(
                        psI[:, h * 64:(h + 1) * 64], lhsT=E_[:, h * 128:(h + 1) * 128], rhs=ov[:, kt, :], start=first, stop=last, skip_group_check=True),
                        reads=[ek, "ov"], writes=["ps%d" % ibank])
            yield
        cp("act", o.oall[:, br, :], psO[:, 0:264], ["ps%d" % obank], ["oall%d_%d" % (g, br)])
        P.op("dve", lambda E: E.tensor_scalar_max(out=o.rden[:, br * 4:(br + 1) * 4], in0=o.oall[:, br, :].rearrange("p (h c) -> p h c", c=66)[:, :, 64], scalar1=1e-30),
             reads=["oall%d_%d" % (g, br)], writes=["rden%d_%d" % (g, br)])
        P.op("dve", lambda E: E.reciprocal(out=o.rden[:, br * 4:(br + 1) * 4], in_=o.rden[:, br * 4:(br + 1) * 4]),
             reads=["rden%d_%d" % (g, br)], writes=["rden%d_%d" % (g, br)])
        yield

    def group_stream(g, t, b):
        o = G[g]
        B0 = 3 * g
        sbank, obank, ibank, mbank = B0, B0 + 1, B0 + 2, B0 + 2
        ts = slice(t * 128, (t + 1) * 128)
        tbg = tball[:, g, :, :]
        obk = "ps%d" % obank
        sgall = sgalls[b]
        mixt = mixts[b]
        pT128 = psv(sbank, [8, 128], BF16)
        pT = pT128[0:64]
        pT2 = psv(sbank, [2, 128], BF16)
        sbk = "ps%d" % sbank
        for s_ in range(8):
            P.op("pe", lambda E, s_=s_: E.transpose(out=pT128[:, s_, :], in_=tbg[:, s_:s_ + 2, :].rearrange("p a b -> p (a b)"), identity=ident),
                 reads=["tb", "K_ident"], writes=[sbk])
        cp("dve", o.qT.rearrange("p (a b) -> p a b", a=4), pT[:, 0:4, :], [sbk], ["qT%d" % g])
        cp("dve", kT[g][:, :, ts], pT[:, 4:6, :], [sbk], ["kcache%d" % g])
        if t > 0:
            cp("dve", craw[g][:, :, 0:16], craw[g][:, :, 128:144], ["craw%d" % g], ["craw%d" % g])
        cp("dve", craw[g][:, :, 16:144], pT[:, 6:8, :], [sbk], ["craw%d" % g])
        yield
        ibk = "ps%d" % ibank
        mbk = "ps%d" % mbank
        preT = ps[obank][0:64, 272:288]
        for kv in range(2):
            for p in range(32):
                P.op("pe", lambda E, kv=kv, p=p: E.matmul(preT[:, kv * 8:(kv + 1) * 8], lhsT=W1[:, kv, p, :], rhs=craw[g][:, kv, p:p + 113:16],
                                                          start=(p == 0), stop=(p == 31), skip_group_check=True),
                     reads=["W1", "craw%d" % g], writes=[obk])
        yield
        zk, ekk = "zc%d" % g, "ec%d" % g
        for kv in range(2):
            P.op("dve", lambda E, kv=kv: E.tensor_scalar(out=o.zc[0:64, kv * 8:(kv + 1) * 8], in0=preT[:, kv * 8:(kv + 1) * 8], scalar1=cbias[:, kv:kv + 1],
                                                         scalar2=None, op0=ALU.add), reads=[obk, "cbias"], writes=[zk])
        P.op("act", lambda E: E.activation(out=o.ec, in_=o.zc, func=ACTF.Exp, scale=-1.0), reads=[zk], writes=[ekk])
        yield
        P.op("dve", lambda E: E.tensor_scalar_add(out=o.ec[0:64], in0=o.ec[0:64], scalar1=1.0), reads=[ekk], writes=[ekk])
        P.op("dve", lambda E: E.reciprocal(out=o.ec[0:64], in_=o.ec[0:64]), reads=[ekk], writes=[ekk])
        tt("dve", o.sTc, o.zc[0:64], o.ec[0:64], ALU.mult, [zk, ekk], ["sTc%d" % g])
        yield
        k8p = ps[obank][0:8, 288:416]
        for kv in range(2):
            P.op("pe", lambda E, kv=kv: E.matmul(k8p[:, kv * 64:(kv + 1) * 64], lhsT=o.sTc[:, kv * 8:(kv + 1) * 8], rhs=W2[:, kv, :], start=True, stop=True),
                 reads=["sTc%d" % g, "W2"], writes=[obk])
        yield
        cp("dve", o.v8, k8p[:, 64:128], [obk], ["v8%d" % g])
        if t == 0:
            P.op("pool", lambda E: E.memset(o.v8[0:1, :], 0.0), reads=[], writes=["v8%d" % g])
        r0 = 8 * (t % 16)
        P.dma("sp", Vc[g][r0:r0 + 8, t // 16, 0:64], o.v8, reads=["v8%d" % g], writes=["Vc%d" % g])
        k8k = "k8%d" % g
        cp("dve", o.k8, k8p[:, 0:64], [obk], [k8k])
        tt("dve", o.k8q, o.k8, o.k8, ALU.mult, [k8k], ["k8q%d" % g])
        P.op("dve", lambda E: E.tensor_reduce(out=o.k8s[0:8, 0:1], in_=o.k8q, axis=AX.X, op=ALU.add), reads=["k8q%d" % g], writes=["k8s%d" % g])
        yield
        P.op("act", lambda E: E.activation(out=o.k8s[:, 0:1], in_=o.k8s[:, 0:1], func=ACTF.Ln, scale=1.0 / 64, bias=EPS), reads=["k8s%d" % g], writes=["k8s%d" % g])
        P.op("act", lambda E: E.activation(out=o.k8s[:, 0:1], in_=o.k8s[:, 0:1], func=ACTF.Exp, scale=-0.5), reads=["k8s%d" % g], writes=["k8s%d" % g])
        yield
        P.op("dve", lambda E: E.tensor_scalar(out=o.k8, in0=o.k8, scalar1=o.k8s[0:8, 0:1], scalar2=None, op0=ALU.mult), reads=[k8k, "k8s%d" % g], writes=[k8k])
        tt("dve", o.k8, o.k8, gn[0:8, 12, :], ALU.mult, [k8k, "gn"], [k8k])
        yield
        cck = "ccs%d" % b
        cc_, ss_ = ccs[b][:, 0, :], ccs[b][:, 1, :]
        kr = ["k8r%d_%d" % (g, i) for i in range(4)]
        tt("dve", o.k8r[0], o.k8[:, 0:32], cc_, ALU.mult, [k8k, cck], [kr[0]])
        tt("pool", o.k8r[1], o.k8[:, 32:64], ss_, ALU.mult, [k8k, cck], [kr[1]])
        tt("dve", o.k8r[2], o.k8[:, 32:64], cc_, ALU.mult, [k8k, cck], [kr[2]])
        tt("pool", o.k8r[3], o.k8[:, 0:32], ss_, ALU.mult, [k8k, cck], [kr[3]])
        yield
        tt("dve", o.k8b[0:8, 0:32], o.k8r[0], o.k8r[1], ALU.subtract, [kr[0], kr[1]], ["k8b%d" % g])
        tt("dve", o.k8b[0:8, 32:64], o.k8r[2], o.k8r[3], ALU.add, [kr[2], kr[3]], ["k8b%d" % g])
        yield
        pt8 = ps[obank][0:64, 416:424]
        P.op("pe", lambda E: E.matmul(pt8, lhsT=o.k8b[0:8, 0:64], rhs=ident[0:8, 0:8], start=True, stop=True), reads=["k8b%d" % g, "K_ident"], writes=[obk])
        cp("dve", kTc[g][:, 8 * t:8 * t + 8], pt8, [obk], ["kTc%d" % g])
        yield
        kts_c = [0] if t < 16 else [0, 1]
        yield from attention_branch(g, t, 0, kts_c, lambda kt: kTc[g][:, kt * 128:(kt + 1) * 128], lambda kt: Vc[g][:, kt, :], "Vc%d" % g,
                                    lambda kt: ((cmask[:, t % 16, :], "cmask") if kt == t // 16 else None), False)
        psI = ps[ibank]
        rk = "rden%d_0" % g
        P.op("dve", lambda E: E.tensor_scalar(out=o.imp, in0=psI[:, 0:64], scalar1=o.rden[:, 0:1], scalar2=None, op0=ALU.mult), reads=[ibk, rk], writes=["imp%d" % g])
        for h in range(1, 4):
            P.op("dve", lambda E, h=h: E.scalar_tensor_tensor(out=o.imp, in0=psI[:, h * 64:(h + 1) * 64], scalar=o.rden[:, h:h + 1], in1=o.imp, op0=ALU.mult, op1=ALU.add),
                 reads=[ibk, rk, "imp%d" % g], writes=["imp%d" % g])
            if h == 2:
                yield
        j0 = 64 - 2 * t
        i2 = "imp2%d" % g
        tt("dve", o.imp2, o.imp, keepB[:, j0:j0 + 64], ALU.mult, ["imp%d" % g, "keepB"], [i2])
        tt("dve", o.imp2, o.imp2, addB[:, j0:j0 + 64], ALU.add, [i2, "addB"], [i2])
        yield
        P.op("dve", lambda E: E.memset(o.imp2[:, 0:1], 1e4), reads=[], writes=[i2])
        P.op("dve", lambda E: E.max(out=o.m8[:, 0:8], in_=o.imp2), reads=[i2], writes=["m8%d" % g])
        yield
        P.op("dve", lambda E: E.match_replace(out=o.wk, in_to_replace=o.m8[:, 0:8], in_values=o.imp2, imm_value=-2.0), reads=[i2, "m8%d" % g], writes=["wk%d" % g])
        P.op("dve", lambda E: E.max(out=o.m8[:, 8:16], in_=o.wk), reads=["wk%d" % g], writes=["m8%d" % g])
        yield
        P.op("dve", lambda E: E.tensor_scalar_max(out=o.thr, in0=o.m8[:, 15:16], scalar1=0.0), reads=["m8%d" % g], writes=["thr%d" % g])
        P.op("dve", lambda E: E.tensor_scalar(out=o.selb[:, 0:64], in0=o.imp2, scalar1=o.thr[:, 0:1], scalar2=None, op0=ALU.is_ge), reads=[i2, "thr%d" % g], writes=["selb%d" % g])
        yield
        psel = ps[mbank][0:64, 384:512]
        P.op("pe", lambda E: E.matmul(psel, lhsT=o.selb[:, 0:64], rhs=ident, start=True, stop=True), reads=["selb%d" % g, "K_ident"], writes=[mbk])
        cp("dve", o.selT, psel, [mbk], ["selT%d" % g])
        yield
        yield from attention_branch(g, t, 1, list(range(t + 1)), lambda kt: kT[g][:, 0, kt * 128:(kt + 1) * 128], lambda kt: Vall[:, g, kt, 0, :], "V",
                                    lambda kt: ((tri, "tri") if kt == t else None), True)
        yield from attention_branch(g, t, 2, list(range(max(0, t - 4), t + 1)), lambda kt: kT[g][:, 1, kt * 128:(kt + 1) * 128], lambda kt: Vall[:, g, kt, 1, :], "V",
                                    lambda kt: ((tri, "tri") if kt == t else ((triinv, "triinv") if kt == t - 4 else None)), False)
        ck = "coef%d" % g
        tt("dve", o.coef.rearrange("p (b h) -> p b h", h=4), sgall[:, g * 12:(g + 1) * 12].rearrange("p (h b) -> p b h", b=3), o.rden.rearrange("p (b h) -> p b h", h=4), ALU.mult,
           ["sg%d" % b, "rden%d_0" % g, "rden%d_1" % g, "rden%d_2" % g], [ck])
        yield
        yk = "y32%d" % g
        for h in range(4):
            ysl = o.y32[:, h * 64:(h + 1) * 64]
            P.op("dve", lambda E, h=h, ysl=ysl: E.tensor_scalar(out=ysl, in0=o.oall[:, 0, h * 66:h * 66 + 64], scalar1=o.coef[:, h:h + 1], scalar2=None, op0=ALU.mult),
                 reads=["oall%d_0" % g, ck], writes=[yk + "_%d" % h])
        yield
        for h in range(4):
            ysl = o.y32[:, h * 64:(h + 1) * 64]
            P.op("dve", lambda E, h=h, ysl=ysl: E.scalar_tensor_tensor(out=ysl, in0=o.oall[:, 1, h * 66:h * 66 + 64], scalar=o.coef[:, 4 + h:5 + h], in1=ysl, op0=ALU.mult, op1=ALU.add),
                 reads=["oall%d_1" % g, ck, yk + "_%d" % h], writes=[yk + "_%d" % h])
        yield
        for h in range(4):
            ysl = o.y32[:, h * 64:(h + 1) * 64]
            P.op("dve", lambda E, h=h, ysl=ysl: E.scalar_tensor_tensor(out=o.ybf[:, h * 64:(h + 1) * 64], in0=o.oall[:, 2, h * 66:h * 66 + 64], scalar=o.coef[:, 8 + h:9 + h], in1=ysl, op0=ALU.mult, op1=ALU.add),
                 reads=["oall%d_2" % g, ck, yk + "_%d" % h], writes=["ybf%d" % g])
        yield
        for c in range(2):
            P.op("pe", lambda E, c=c: E.transpose(out=pT2[:, c, :], in_=o.ybf[:, c * 128:(c + 1) * 128], identity=ident), reads=["ybf%d" % g, "K_ident"], writes=[sbk])
        cp("act", mixt[:, 4 + 2 * g:6 + 2 * g, :], pT2, [sbk], ["mixt%d" % b])
        yield

    def stage_a(t):
        b = t % 2
        sgall = sgalls[b]
        mixt = mixts[b]
        ts = slice(t * 128, (t + 1) * 128)
        P.dma("sp", hTt[b], hT_d[:, :, ts], reads=["hT_d%d" % t], writes=["hTt%d" % b])
        P.dma("sp", cs[b][:, 0, :], cd["cos"][:, t, :], writes=["cs%d" % b])
        P.dma("sp", cs[b][:, 1, :], cd["sin"][:, t, :], writes=["cs%d" % b])
        P.dma("sp", ccs[b][:, 0, :], cd["ccos"][:, t, :], writes=["ccs%d" % b])
        P.dma("sp", ccs[b][:, 1, :], cd["csin"][:, t, :], writes=["ccs%d" % b])
        chunks = [(0, 512, u[b], "u%d" % b), (512, 512, pj[:, 0:512], "pj"), (1024, 512, pj[:, 512:1024], "pj"), (1536, 280, pj[:, 1024:1304], "pj")]
        for ci, (c0, wd, dst, dk) in enumerate(chunks):
            bank = 6 + (ci % 2)
            for kc in range(8):
                P.op("pe", lambda E, kc=kc, c0=c0, wd=wd, bank=bank: E.matmul(ps[bank][:, 0:wd], lhsT=hTt[b][:, kc, :], rhs=win[:, kc, c0:c0 + wd],
                                                                               start=(kc == 0), stop=(kc == 7)),
                     reads=["hTt%d" % b, "win"], writes=["ps%d" % bank])
            cp("act" if ci % 2 == 0 else "dve", dst, ps[bank][:, 0:wd], ["ps%d" % bank], [dk])
            yield
        qk = pj[:, 0:768].rearrange("p (s d) -> p s d", d=64)
        tt("dve", sq, qk, qk, ALU.mult, ["pj"], ["sq"])
        P.op("dve", lambda E: E.tensor_reduce(out=st, in_=sq, axis=AX.X, op=ALU.add), reads=["sq"], writes=["st"])
        P.op("act", lambda E: E.activation(out=st, in_=st, func=ACTF.Ln, scale=1.0 / 64, bias=EPS), reads=["st"], writes=["st"])
        P.op("act", lambda E: E.activation(out=st, in_=st, func=ACTF.Exp, scale=-0.5), reads=["st"], writes=["st"])
        cp("pool", tball[:, :, 6:8, :], pj[:, 768:1024].rearrange("p (g s d) -> p g s d", g=2, d=64), ["pj"], ["tb"])
        cp("pool", Vall[:, :, t, :, 0:64], pj[:, 1024:1280].rearrange("p (g s d) -> p g s d", g=2, d=64), ["pj"], ["V"])
        P.op("act", lambda E: E.activation(out=sgall, in_=pj[:, 1280:1304], func=ACTF.Exp, scale=-1.0), reads=["pj"], writes=["sg%d" % b])
        yield
        tt("dve", xn, qk, st.unsqueeze(2).to_broadcast([128, 12, 64]), ALU.mult, ["pj", "st"], ["xn"])
        P.op("pool", lambda E: E.tensor_scalar_add(out=sgall, in0=sgall, scalar1=1.0), reads=["sg%d" % b], writes=["sg%d" % b])
        tt("dve", xn, xn, gn[:, 0:12, :], ALU.mult, ["xn", "gn"], ["xn"])
        P.op("dve", lambda E: E.reciprocal(out=sgall, in_=sgall), reads=["sg%d" % b], writes=["sg%d" % b])
        yield
        cosb = cs[b][:, 0, :].unsqueeze(1).to_broadcast([128, 12, 32])
        sinb = cs[b][:, 1, :].unsqueeze(1).to_broadcast([128, 12, 32])
        x1 = xn[:, :, 0:32]
        x2 = xn[:, :, 32:64]
        ck = "cs%d" % b
        tt("dve", ra, x1, cosb, ALU.mult, ["xn", ck], ["ra"])
        tt("pool", rb, x2, sinb, ALU.mult, ["xn", ck], ["rb"])
        tt("dve", rc, x2, cosb, ALU.mult, ["xn", ck], ["rc"])
        tt("pool", rd, x1, sinb, ALU.mult, ["xn", ck], ["rd"])
        yield
        g4 = lambda a: a.rearrange("p (g s) d -> p g s d", g=2)
        tt("dve", tball[:, :, 0:6, 0:32], g4(ra), g4(rb), ALU.subtract, ["ra", "rb"], ["tb"])
        tt("pool", tball[:, :, 0:6, 32:64], g4(rc), g4(rd), ALU.add, ["rc", "rd"], ["tb"])
        yield
        psP = ps[6]
        for gp in range(4):
            kind = 2 if t == 0 else 0
            P.op("pe", lambda E, gp=gp, kind=kind: E.matmul(psP[:, gp * 128:(gp + 1) * 128], lhsT=u[b][:, gp * 128:(gp + 1) * 128], rhs=bands[:, kind, gp, :],
                                                            start=True, stop=(t == 0), skip_group_check=True),
                 reads=["u%d" % b, "bands"], writes=["ps6"])
            if t > 0:
                P.op("pe", lambda E, gp=gp: E.matmul(psP[:, gp * 128:(gp + 1) * 128], lhsT=u[1 - b][:, gp * 128:(gp + 1) * 128], rhs=bands[:, 1, gp, :],
                                                     start=False, stop=True, skip_group_check=True),
                     reads=["u%d" % (1 - b), "bands"], writes=["ps6"])
        cp("act", pooledT, psP[:, :], ["ps6"], ["pooledT"])
        yield
        psY = ps[7]
        for gp in range(4):
            P.op("pe", lambda E, gp=gp: E.matmul(psY[:, gp * 128:(gp + 1) * 128], lhsT=pw[:, gp, :], rhs=pooledT[:, gp * 128:(gp + 1) * 128], start=True, stop=True),
                 reads=["pw", "pooledT"], writes=["ps7"])
        for gp in range(4):
            P.op("act", lambda E, gp=gp: E.activation(out=mixt[:, gp, :], in_=psY[:, gp * 128:(gp + 1) * 128], func=ACTF.Identity, scale=psc[:, gp:gp + 1]),
                 reads=["ps7", "psc"], writes=["mixt%d" % b])
    def run_streams(gens):
        while gens:
            for gen in list(gens):
                try:
                    next(gen)
                except StopIteration:
                    gens.remove(gen)

    run_streams([stage_a(0)])
    for t in range(nt):
        b = t % 2
        ts = slice(t * 128, (t + 1) * 128)
        gens = [group_stream(0, t, b), group_stream(1, t, b)]
        if t + 1 < nt:
            gens.append(stage_a(t + 1))
        run_streams(gens)
        P.dma("sp", mixT_d[:, :, ts], mixts[b], reads=["mixt%d" % b], writes=["mixT_d%d" % t])


def _perm_in_cols():
    o0, o1 = 512, 1024
    o2 = o1 + 768

    def kvc(g, kvi, br):
        st = o1 + ((kvi * 3 + br) * 2 + g) * 64
        return list(range(st, st + 64))
    cols = list(range(512))
    for g in range(2):
        for h in range(4):
            cols += list(range(o0 + (g * 4 + h) * 64, o0 + (g * 4 + h + 1) * 64))
        cols += kvc(g, 0, 1) + kvc(g, 0, 2)
    for g in range(2):
        cols += kvc(g, 0, 0) + kvc(g, 1, 0)
    for g in range(2):
        cols += kvc(g, 1, 1) + kvc(g, 1, 2)
    cols += list(range(o2, o2 + 24))
    assert len(cols) == INC
    return np.array(cols)


_PROG_CACHE = {}


def _get_prog(n_layers, debug=False):
    key = (n_layers, debug)
    if key not in _PROG_CACHE:
        _PROG_CACHE[key] = build_program(n_layers, debug=debug)
    return _PROG_CACHE[key]


def _layer_inputs(ls, x_b, c, w_mod, b_mod, norm1, norm2, w_in, pool_w, pool_scale, q_norm, k_norm,
                  cmp_pe, cmp_w1, cmp_w2, w_out, w_ff1, w_ff2, b, consts, perm):
    f = np.float32
    ls = list(ls)
    n = len(ls)

    def col(v):
        return np.ascontiguousarray(v.reshape(8, 128).T)
    d = {}
    d["x"] = np.ascontiguousarray(x_b, dtype=f)
    d["c_col"] = col(np.asarray(c[b], f))
    d["w_mod"] = np.ascontiguousarray(w_mod[ls], dtype=f)
    d["b_mod"] = np.ascontiguousarray(b_mod[ls], dtype=f).reshape(n, 1, 6 * D)
    d["n1c"] = np.stack([col(norm1[l]) for l in ls]).astype(f)
    d["n2c"] = np.stack([col(norm2[l]) for l in ls]).astype(f)
    d["w_in"] = np.ascontiguousarray(w_in[ls][:, :, perm], dtype=f)
    d["pool_w"] = np.ascontiguousarray(np.transpose(pool_w[ls], (0, 2, 1, 3)), dtype=f)
    d["pscale"] = np.ascontiguousarray(np.transpose(pool_scale[ls].reshape(n, 4, 128), (0, 2, 1)), dtype=f)
    gains = np.zeros((n, 128, 13, 64), f)
    for i, l in enumerate(ls):
        for g in range(2):
            gains[i, :, 6 * g:6 * g + 4, :] = q_norm[l][None, None, :]
            gains[i, :, 6 * g + 4, :] = k_norm[l, 1][None, :]
            gains[i, :, 6 * g + 5, :] = k_norm[l, 2][None, :]
        gains[i, :, 12, :] = k_norm[l, 0][None, :]
    d["gains"] = gains
    w1 = cmp_w1[ls].reshape(n, 2, 32, 64, 64)
    d["w1"] = np.ascontiguousarray(np.transpose(w1, (0, 3, 1, 2, 4)).reshape(n, 64, 2 * 32 * 64), dtype=f)
    d["w2"] = np.ascontiguousarray(np.transpose(cmp_w2[ls], (0, 2, 1, 3)).reshape(n, 64, 128), dtype=f)
    d["peT"] = np.ascontiguousarray(np.transpose(cmp_pe[ls], (0, 3, 1, 2)).reshape(n, 64, 64), dtype=f)
    d["w_out"] = np.ascontiguousarray(w_out[ls], dtype=f)
    d["w_ff1"] = np.ascontiguousarray(w_ff1[ls], dtype=f)
    d["w_ff2"] = np.ascontiguousarray(w_ff2[ls], dtype=f)
    for name, shape, _, _ in _CONST_SPECS:
        d["k_" + name] = np.ascontiguousarray(consts[name], dtype=f).reshape(shape)
    return d


LAYERS_PER_LAUNCH = 4


def kernel(x, c, w_mod, b_mod, norm1, norm2, w_in, pool_w, pool_scale, q_norm, k_norm,
           cmp_pe, cmp_w1, cmp_w2, w_out, w_ff1, w_ff2):
    args = [np.asarray(a) for a in (c, w_mod, b_mod, norm1, norm2, w_in, pool_w, pool_scale, q_norm, k_norm,
                                    cmp_pe, cmp_w1, cmp_w2, w_out, w_ff1, w_ff2)]
    x = np.asarray(x, np.float32)
    consts = _consts()
    perm = _perm_in_cols()
    nc = _get_prog(LAYERS_PER_LAUNCH)
    xs = [x[b] for b in range(B)]
    for l0 in range(0, L, LAYERS_PER_LAUNCH):
        ls = range(l0, l0 + LAYERS_PER_LAUNCH)
        maps = [_layer_inputs(ls, xs[b], *args, b, consts, perm) for b in range(B)]
        in_maps = [maps[i % B] for i in range(NCORES)]
        res = run_bass_kernel_spmd(nc, in_maps, core_ids=list(range(NCORES)))
        xs = [np.asarray(res.results[b]["y"], np.float32) for b in range(B)]
    return np.stack(xs, 0).astype(np.float32)
```
